# Optimizing a Trainium2 kernel written in Bass

```python
import jax, jax.numpy as jnp
from jax import lax
import numpy as np

D_MODEL = 1024
BATCH = 2
SEQ = 8192
DEPTH = 1

MLA_HEADS = 8
Q_LORA_RANK = 256
KV_LORA_RANK = 128
QK_NOPE_DIM = 64
QK_ROPE_DIM = 32
V_HEAD_DIM = 64
MLA_SCALE = (QK_NOPE_DIM + QK_ROPE_DIM) ** -0.5
Q_BLOCK = 128

RET_HEADS = 8
RET_HEAD_DIM = 64
RET_CHUNK = 128

D_MIX = MLA_HEADS * V_HEAD_DIM + RET_HEADS * RET_HEAD_DIM
D_FF = 2816
ROPE_THETA = 10000.0
EPS = 1e-6

IN_SIZES = (Q_LORA_RANK, KV_LORA_RANK, QK_ROPE_DIM,
            RET_HEADS * RET_HEAD_DIM, RET_HEADS * RET_HEAD_DIM,
            RET_HEADS * RET_HEAD_DIM, RET_HEADS * RET_HEAD_DIM)
D_IN = sum(IN_SIZES)
SPLIT_POINTS = tuple(int(s) for s in np.cumsum(IN_SIZES)[:-1])

kernel_name = "hybrid_mla_retention_macaron_encoder"


def rms_norm(x, g):
    xf = x.astype(jnp.float32)
    y = xf * lax.rsqrt(jnp.mean(xf * xf, axis=-1, keepdims=True) + EPS)
    return (y * g.astype(jnp.float32)).astype(x.dtype)


def swiglu(x, w_gate, w_up, w_down):
    return (jax.nn.silu(x @ w_gate) * (x @ w_up)) @ w_down


def rope(x, pos_f):
    half = x.shape[-1] // 2
    inv = ROPE_THETA ** (-jnp.arange(half, dtype=jnp.float32) / half)
    ang = pos_f[..., None] * inv
    cos = jnp.cos(ang)[:, :, None, :]
    sin = jnp.sin(ang)[:, :, None, :]
    x1 = x[..., :half].astype(jnp.float32)
    x2 = x[..., half:].astype(jnp.float32)
    return jnp.concatenate([x1 * cos - x2 * sin, x1 * sin + x2 * cos], axis=-1).astype(x.dtype)


def mla_group(c_q, c_kv, k_pe_raw, pos_f, q_norm, w_uq, kv_norm, w_ukv):
    B, S, _ = c_q.shape
    H = MLA_HEADS
    q = (rms_norm(c_q, q_norm) @ w_uq).reshape(B, S, H, QK_NOPE_DIM + QK_ROPE_DIM)
    q_nope = q[..., :QK_NOPE_DIM]
    q_pe = rope(q[..., QK_NOPE_DIM:], pos_f)
    kv = (rms_norm(c_kv, kv_norm) @ w_ukv).reshape(B, S, H, QK_NOPE_DIM + V_HEAD_DIM)
    k_nope = kv[..., :QK_NOPE_DIM]
    v = kv[..., QK_NOPE_DIM:]
    k_pe = rope(k_pe_raw[:, :, None, :], pos_f)[:, :, 0]
    nb = S // Q_BLOCK

    def block(args):
        qn, qp = args
        s = (jnp.einsum('bqhd,bkhd->bhqk', qn, k_nope).astype(jnp.float32)
             + jnp.einsum('bqhr,bkr->bhqk', qp, k_pe).astype(jnp.float32))
        p = jax.nn.softmax(s * MLA_SCALE, axis=-1)
        return jnp.einsum('bhqk,bkhd->bqhd', p.astype(v.dtype), v)

    qn_b = q_nope.reshape(B, nb, Q_BLOCK, H, QK_NOPE_DIM).swapaxes(0, 1)
    qp_b = q_pe.reshape(B, nb, Q_BLOCK, H, QK_ROPE_DIM).swapaxes(0, 1)
    out = lax.map(block, (qn_b, qp_b))
    return out.swapaxes(0, 1).reshape(B, S, H * V_HEAD_DIM)


def retention_scan(q, k, v, log_gamma, inclusive):
    B, S, H, d = q.shape
    C = RET_CHUNK
    nc = S // C
    idx = jnp.arange(C, dtype=jnp.float32)
    rel = idx[:, None] - idx[None, :]
    mask = (rel >= 0) if inclusive else (rel > 0)
    d_inner = jnp.where(mask[None], jnp.exp(log_gamma[:, None, None] * jnp.maximum(rel, 0.0)[None]), 0.0)
    q_dec = jnp.exp(log_gamma[None, :] * (idx[:, None] + 1.0))
    k_dec = jnp.exp(log_gamma[None, :] * (C - 1.0 - idx)[:, None])
    c_dec = jnp.exp(log_gamma * C)

    def to_chunks(t):
        return t.reshape(B, nc, C, H, d).swapaxes(0, 1)

    def step(state, inp):
        qc, kc, vc = inp
        s = jnp.einsum('bihd,bjhd->bhij', qc, kc) * d_inner
        inner = jnp.einsum('bhij,bjhe->bihe', s, vc)
        cross = jnp.einsum('bihd,bhde->bihe', qc * q_dec[None, :, :, None], state)
        state = (state * c_dec[None, :, None, None]
                 + jnp.einsum('bjhd,bjhe->bhde', kc * k_dec[None, :, :, None], vc))
        return state, inner + cross

    state0 = jnp.zeros((B, H, d, d), jnp.float32)
    _, out = lax.scan(step, state0, (to_chunks(q), to_chunks(k), to_chunks(v)))
    return out.swapaxes(0, 1).reshape(B, S, H, d)


def retention_group(r_q, r_k, r_v, r_g, pos_f, decay_fwd, decay_bwd):
    B, S, _ = r_q.shape
    shp = (B, S, RET_HEADS, RET_HEAD_DIM)
    q = rope(r_q.reshape(shp), pos_f).astype(jnp.float32)
    k = rope(r_k.reshape(shp), pos_f).astype(jnp.float32) * (RET_HEAD_DIM ** -0.5)
    v = r_v.reshape(shp).astype(jnp.float32)
    lg_f = jax.nn.log_sigmoid(decay_fwd.astype(jnp.float32))
    lg_b = jax.nn.log_sigmoid(decay_bwd.astype(jnp.float32))
    fwd = retention_scan(q, k, v, lg_f, True)
    bwd = jnp.flip(retention_scan(jnp.flip(q, 1), jnp.flip(k, 1), jnp.flip(v, 1), lg_b, False), 1)
    o = fwd + bwd
    mu = jnp.mean(o, axis=-1, keepdims=True)
    var = jnp.mean(jnp.square(o - mu), axis=-1, keepdims=True)
    o = ((o - mu) * lax.rsqrt(var + EPS)).reshape(B, S, RET_HEADS * RET_HEAD_DIM)
    return (jax.nn.silu(r_g.astype(jnp.float32)) * o).astype(r_q.dtype)


def setup_inputs(seed: int = 0) -> dict:
    key = jax.random.key(seed)
    ks = jax.random.split(key, 24)
    f32 = jnp.float32

    def w(k, shape, fan_in):
        return jax.random.normal(k, shape, f32) * (fan_in ** -0.5)

    def gain(k, shape):
        return 1.0 + 0.02 * jax.random.normal(k, shape, f32)

    L = DEPTH
    offset = jax.random.randint(ks[1], (BATCH, 1), 0, 4096, dtype=jnp.int32)
    positions = offset + jnp.arange(SEQ, dtype=jnp.int32)[None, :]
    decay_init = jnp.asarray(np.log(2.0 ** (5.0 + np.arange(RET_HEADS)) - 1.0), f32)
    return {
        "x": jax.random.normal(ks[0], (BATCH, SEQ, D_MODEL), f32),
        "positions": positions,
        "ffn1_norm": gain(ks[2], (L, D_MODEL)),
        "ffn1_w_gate": w(ks[3], (L, D_MODEL, D_FF), D_MODEL),
        "ffn1_w_up": w(ks[4], (L, D_MODEL, D_FF), D_MODEL),
        "ffn1_w_down": w(ks[5], (L, D_FF, D_MODEL), D_FF),
        "mix_norm": gain(ks[6], (L, D_MODEL)),
        "w_in": w(ks[7], (L, D_MODEL, D_IN), D_MODEL),
        "q_norm": gain(ks[8], (L, Q_LORA_RANK)),
        "w_uq": w(ks[9], (L, Q_LORA_RANK, MLA_HEADS * (QK_NOPE_DIM + QK_ROPE_DIM)), Q_LORA_RANK),
        "kv_norm": gain(ks[10], (L, KV_LORA_RANK)),
        "w_ukv": w(ks[11], (L, KV_LORA_RANK, MLA_HEADS * (QK_NOPE_DIM + V_HEAD_DIM)), KV_LORA_RANK),
        "ret_decay_fwd": decay_init[None, :] + 0.05 * jax.random.normal(ks[12], (L, RET_HEADS), f32),
        "ret_decay_bwd": decay_init[None, :] + 0.05 * jax.random.normal(ks[13], (L, RET_HEADS), f32),
        "w_o": w(ks[14], (L, D_MIX, D_MODEL), D_MIX),
        "ffn2_norm": gain(ks[15], (L, D_MODEL)),
        "ffn2_w_gate": w(ks[16], (L, D_MODEL, D_FF), D_MODEL),
        "ffn2_w_up": w(ks[17], (L, D_MODEL, D_FF), D_MODEL),
        "ffn2_w_down": w(ks[18], (L, D_FF, D_MODEL), D_FF),
        "final_norm": gain(ks[19], (D_MODEL,)),
    }


def reference(x, positions, ffn1_norm, ffn1_w_gate, ffn1_w_up, ffn1_w_down, mix_norm, w_in,
              q_norm, w_uq, kv_norm, w_ukv, ret_decay_fwd, ret_decay_bwd, w_o,
              ffn2_norm, ffn2_w_gate, ffn2_w_up, ffn2_w_down, final_norm):
    pos_f = positions.astype(jnp.float32)
    h = x
    for l in range(DEPTH):
        h = h + 0.5 * swiglu(rms_norm(h, ffn1_norm[l]), ffn1_w_gate[l], ffn1_w_up[l], ffn1_w_down[l])
        n = rms_norm(h, mix_norm[l])
        proj = n @ w_in[l]
        c_q, c_kv, k_pe, r_q, r_k, r_v, r_g = jnp.split(proj, SPLIT_POINTS, axis=-1)
        a = mla_group(c_q, c_kv, k_pe, pos_f, q_norm[l], w_uq[l], kv_norm[l], w_ukv[l])
        r = retention_group(r_q, r_k, r_v, r_g, pos_f, ret_decay_fwd[l], ret_decay_bwd[l])
        h = h + jnp.concatenate([a, r], axis=-1) @ w_o[l]
        h = h + 0.5 * swiglu(rms_norm(h, ffn2_norm[l]), ffn2_w_gate[l], ffn2_w_up[l], ffn2_w_down[l])
    return rms_norm(h, final_norm)
```

```python
import math
from contextlib import ExitStack
import numpy as np
import concourse.bass as bass
import concourse.mybir as mybir
from concourse.bass_utils import run_bass_kernel_spmd

F32 = mybir.dt.float32
BF16 = mybir.dt.bfloat16
I32 = mybir.dt.int32
ALU = mybir.AluOpType
AF = mybir.ActivationFunctionType
AX = mybir.AxisListType

NCORES = 8
NT = 2048
NTT = 16
D = 1024
DFF = 2816
NFC = 22
EPS = 1e-6
FFN_PARTS = [(0, 6), (6, 6), (12, 6), (18, 4)]


class Sched:
    def __init__(self, nc, n_dma=20):
        self.nc = nc
        self.eng = {'pe': nc.tensor, 'act': nc.scalar, 'dve': nc.vector,
                    'pool': nc.gpsimd, 'sp': nc.sync}
        self.semh = {}
        self.cnt = {}
        for k in ('pe', 'act', 'dve', 'pool'):
            self.semh[k] = nc.alloc_semaphore("s_" + k)
            self.cnt[k] = 0
        self.waited = {k: {} for k in self.eng}
        self.res = {}
        self.ring = {}
        self.ring_i = {}
        self.uses = {}
        for q in ('sp', 'pool'):
            keys = []
            for i in range(n_dma):
                key = "d_%s%d" % (q, i)
                self.semh[key] = nc.alloc_semaphore(key)
                self.uses[key] = 0
                keys.append(key)
            self.ring[q] = keys
            self.ring_i[q] = 0
        self.nwaits = 0
        self.ninstr = 0
        self.cc_dummy = nc.alloc_sbuf_tensor("cc_dummy", [128, 8], F32)

    def _deps(self, reads, writes):
        deps = []
        for r in reads:
            e = self.res.get(r)
            if e is not None and e[0] is not None:
                deps.append(e[0])
        for w in writes:
            e = self.res.get(w)
            if e is not None:
                if e[0] is not None:
                    deps.append(e[0])
                deps.extend(e[1].values())
        return deps

    def _record(self, tok, reads, writes, rkey):
        for r in reads:
            e = self.res.setdefault(r, [None, {}])
            e[1][rkey] = tok
        for w in writes:
            self.res[w] = [tok, {}]

    def _wait(self, q, tok, skip_self_pe=True):
        key, val = tok
        if q == 'pe' and key == 'pe' and skip_self_pe:
            return
        if self.waited[q].get(key, 0) >= val:
            return
        self.eng[q].wait_ge(self.semh[key], val)
        self.waited[q][key] = val
        self.nwaits += 1

    def op(self, q, fn, reads=(), writes=(), inc=True):
        for d in self._deps(reads, writes):
            self._wait(q, d)
        ins = fn(self.eng[q])
        self.ninstr += 1
        if inc:
            self.cnt[q] += 1
            ins.then_inc(self.semh[q], 1)
            tok = (q, self.cnt[q])
        else:
            tok = (q, self.cnt[q] + 1)
        self._record(tok, reads, writes, q)
        return tok

    def dma(self, q, out, in_, reads=(), writes=(), **kw):
        ring = self.ring[q]
        key = ring[self.ring_i[q] % len(ring)]
        self.ring_i[q] += 1
        use = self.uses[key]
        if use > 0:
            self._wait(q, (key, 16 * use))
        for d in self._deps(reads, writes):
            self._wait(q, d)
        self.eng[q].dma_start(out=out, in_=in_, **kw).then_inc(self.semh[key], 16)
        self.ninstr += 1
        self.uses[key] = use + 1
        tok = (key, 16 * (use + 1))
        self._record(tok, reads, writes, key)
        return tok

    def collective(self, kind, ins_ap, outs_ap, groups, reads, writes, name):
        key = "cc_" + name
        self.semh[key] = self.nc.alloc_semaphore(key)
        for d in self._deps(reads, writes):
            self._wait('pool', d)
        self.nc.gpsimd.collective_compute(kind, ALU.bypass, replica_groups=groups,
                                          ins=[ins_ap], outs=[outs_ap]).then_inc(self.semh[key])
        self.nc.gpsimd.wait_ge(self.semh[key], 1)
        self.ninstr += 1
        return self.op('pool', lambda e: e.memset(self.cc_dummy[:], 0.0), reads=reads, writes=writes)

    def barrier(self):
        toks = [(k, self.cnt[k]) for k in ('pe', 'act', 'dve', 'pool') if self.cnt[k] > 0]
        toks += [(key, 16 * u) for key, u in self.uses.items() if u > 0]
        for q in self.eng:
            for t in toks:
                self._wait(q, t, skip_self_pe=False)

    def wait_all(self, q, keys):
        for k in keys:
            e = self.res.get(k)
            if e is not None:
                if e[0] is not None:
                    self._wait(q, e[0], skip_self_pe=False)
                for t in e[1].values():
                    self._wait(q, t, skip_self_pe=False)


NCT = 464
SCALE = 96 ** -0.5
MAGIC = 12582912.0
C1 = 6.28125
C2 = 2.0 * math.pi - 6.28125
PI_S = 3.141592
LN8 = math.log(0.125)


class _Stop(Exception):
    pass


def build(stop=None, skip_mid=False, skip_ffn=False, tsub=99):
    import os as _os
    nc = bass.Bass("TRN2", target_bir_lowering=False)

    def din(name, shape, dtype=F32):
        return nc.dram_tensor(name, list(shape), dtype, kind="ExternalInput").ap()

    x_d = din("x", [NT, D])
    g_ffn1 = din("g_ffn1", [1, D]); wg1 = din("wg1", [D, DFF]); wu1 = din("wu1", [D, DFF]); wd1 = din("wd1", [DFF, D])
    g_ffn2 = din("g_ffn2", [1, D]); wg2 = din("wg2", [D, DFF]); wu2 = din("wu2", [D, DFF]); wd2 = din("wd2", [DFF, D])
    g_fin = din("g_fin", [1, D])
    g_mix = din("g_mix", [1, D]); w_in = din("w_in", [D, 2464])
    g_q = din("g_q", [1, 256]); w_uq = din("w_uq", [256, 768])
    g_kv = din("g_kv", [1, 128]); w_ukv = din("w_ukv", [128, 1024])
    dec_d = din("dec", [1, 16]); w_o = din("w_o", [D, D])
    pos_d = din("pos_tok", [128, NTT], I32)
    ctab_d = din("ctab", [128, NCT]); geo_d = din("geo", [128, 8])
    out_d = nc.dram_tensor("out", [NT, D], F32, kind="ExternalOutput").ap()
    if stop is not None:
        dbg32 = nc.dram_tensor("dbg32", [128, 2048], F32, kind="ExternalOutput").ap()
        dbgbf = nc.dram_tensor("dbgbf", [128, 8192], BF16, kind="ExternalOutput").ap()
        dbgbf2 = nc.dram_tensor("dbgbf2", [128, 2048], BF16, kind="ExternalOutput").ap()
    xin = nc.dram_tensor("xin", [160, NT], BF16)
    xout = nc.dram_tensor("xout", [640, NT], BF16)
    xin2 = nc.dram_tensor("xin2", [128, 512], F32)
    xout2 = nc.dram_tensor("xout2", [512, 512], F32)
    GROUPS = [[0, 1, 2, 3], [4, 5, 6, 7]]

    S = Sched(nc)
    ps = nc.alloc_psum_tensor

    h = nc.alloc_sbuf_tensor("h", [128, NTT, D], F32)
    XT = nc.alloc_sbuf_tensor("XT", [128, 8, NT], BF16)
    ident = nc.alloc_sbuf_tensor("ident", [128, 128], BF16)
    ss = nc.alloc_sbuf_tensor("ss", [128, NTT], F32)
    var = nc.alloc_sbuf_tensor("var", [128, NTT], F32)
    sd = nc.alloc_sbuf_tensor("sd", [128, NTT], F32)
    rstd = nc.alloc_sbuf_tensor("rstd", [128, NTT], F32)

    ctab = nc.alloc_sbuf_tensor("ctab_sb", [128, NCT], F32)
    cr = nc.alloc_sbuf_tensor("cr_sb", [128, 512], F32); sr = nc.alloc_sbuf_tensor("sr_sb", [128, 512], F32)
    cm = nc.alloc_sbuf_tensor("cm_sb", [128, 256], F32); sm = nc.alloc_sbuf_tensor("sm_sb", [128, 256], F32)
    INVR = ctab[:, 416:448]; INVM = ctab[:, 448:464]

    bank = [ps("bank%d" % i, [128, 512], F32) for i in range(8)]
    bankbf = [b.bitcast(BF16) for b in bank]

    S.op('dve', lambda e: e.memset(ident[:], 1.0), writes=['ident'])
    S.op('pool', lambda e: e.affine_select(out=ident[:], in_=ident[:], pattern=[[-1, 128]],
                                           compare_op=ALU.is_equal, fill=0.0, base=0, channel_multiplier=1),
         reads=['ident'], writes=['ident'])

    out_keys = []
    ov = out_d.rearrange("(t p) d -> p t d", p=128)

    def load_x():
        xv = x_d.rearrange("(t p) d -> p t d", p=128)
        for tt in range(NTT):
            S.dma('sp', h[:, tt, :], xv[:, tt, :], writes=[('h', tt)])

    def rstd_batch(ss_t, n, inv_n, var_t, sd_t, rstd_t, key):
        S.op('dve', lambda e: e.tensor_scalar(out=var_t[:], in0=ss_t[:], scalar1=inv_n, scalar2=EPS,
                                              op0=ALU.mult, op1=ALU.add),
             reads=[(key, t) for t in range(n)], writes=[key + '_var'])
        S.op('act', lambda e: e.activation(out=sd_t[:], in_=var_t[:], func=AF.Sqrt),
             reads=[key + '_var'], writes=[key + '_sd'])
        S.op('dve', lambda e: e.reciprocal(out=rstd_t[:], in_=sd_t[:]), reads=[key + '_sd'], writes=[key + '_rstd'])

    def norm_stats(gain_d, gbc, junk):
        S.dma('sp', gbc[:], gain_d[0, :].partition_broadcast(128), writes=['gbc'])
        for tt in range(NTT):
            S.op('dve', lambda e, tt=tt: e.scalar_tensor_tensor(
                out=junk[:], in0=h[:, tt, :], scalar=1.0, in1=h[:, tt, :],
                op0=ALU.mult, op1=ALU.mult, accum_out=ss[:, tt:tt + 1]),
                reads=[('h', tt)], writes=[('ss', tt)])
        rstd_batch(ss, NTT, 1.0 / D, var, sd, rstd, 'ss')

    def norm_to_XT(gain_d, gbc, junk, xn):
        norm_stats(gain_d, gbc, junk)
        for tt in range(NTT):
            b = tt % 2
            S.op('dve', lambda e, tt=tt, b=b: e.scalar_tensor_tensor(
                out=xn[b][:], in0=h[:, tt, :], scalar=rstd[:, tt:tt + 1], in1=gbc[:],
                op0=ALU.mult, op1=ALU.mult),
                reads=[('h', tt), 'ss_rstd', 'gbc'], writes=[('xn', b)])
            pb = 6 + b
            pv = bankbf[pb]
            for k in range(8):
                S.op('pe', lambda e, k=k, b=b, pv=pv: e.transpose(
                    out=pv[:, k * 128:(k + 1) * 128], in_=xn[b][:, k * 128:(k + 1) * 128], identity=ident[:]),
                    reads=[('xn', b), 'ident'], writes=[('bank', pb)], inc=(k == 7))
            S.op('act', lambda e, tt=tt, pv=pv: e.activation(
                out=XT[:, :, tt * 128:(tt + 1) * 128],
                in_=pv[:, 0:1024].rearrange("p (k t) -> p k t", k=8), func=AF.Copy),
                reads=[('bank', pb)], writes=[('XT', tt)])

    def rope_tables():
        S.dma('sp', ctab[:], ctab_d, writes=['ctab'])
        with ExitStack() as es:
            sb = lambda n, s, d: es.enter_context(nc.sbuf_tensor("t_" + n, s, d))
            pos_i = sb("pos_i", [128, NTT], I32); pos_f = sb("pos_f", [128, NTT], F32)
            ang = sb("ang", [128, 512], F32)
            ta = sb("ta", [128, 512], F32); tu = sb("tu", [128, 512], F32); tr = sb("tr", [128, 512], F32)
            S.dma('sp', pos_i[:], pos_d, writes=['pos_i'])
            S.op('dve', lambda e: e.tensor_copy(out=pos_f[:], in_=pos_i[:]), reads=['pos_i'], writes=['pos_f'])

            def sincos(n, nf, inv_ap, o_sin, o_cos, key):
                a3 = ang[:, 0:n].rearrange("p (a b) -> p a b", a=NTT)
                S.op('dve', lambda e: e.tensor_tensor(
                    out=a3, in0=pos_f[:, :].unsqueeze(2).to_broadcast([128, NTT, nf]),
                    in1=inv_ap.unsqueeze(1).to_broadcast([128, NTT, nf]), op=ALU.mult),
                    reads=['pos_f', 'ctab', 'tu'], writes=['ang'])
                for dst, off, dk in ((o_sin, 0.0, key + 's'), (o_cos, math.pi / 2, key + 'c')):
                    if off != 0.0:
                        S.op('dve', lambda e, off=off: e.tensor_scalar(out=ta[:, 0:n], in0=ang[:, 0:n], scalar1=off,
                                                                      scalar2=None, op0=ALU.add),
                             reads=['ang'], writes=['ta'])
                        src, sk = ta, 'ta'
                    else:
                        src, sk = ang, 'ang'
                    S.op('dve', lambda e, src=src: e.tensor_scalar(out=tu[:, 0:n], in0=src[:, 0:n],
                                                                  scalar1=1.0 / (2 * math.pi), scalar2=MAGIC,
                                                                  op0=ALU.mult, op1=ALU.add),
                         reads=[sk], writes=['tu'])
                    S.op('dve', lambda e: e.tensor_scalar(out=tu[:, 0:n], in0=tu[:, 0:n], scalar1=MAGIC, scalar2=None,
                                                          op0=ALU.subtract), reads=['tu'], writes=['tu'])
                    S.op('dve', lambda e, src=src: e.scalar_tensor_tensor(out=tr[:, 0:n], in0=tu[:, 0:n], scalar=-C1,
                                                                         in1=src[:, 0:n], op0=ALU.mult, op1=ALU.add),
                         reads=['tu', sk], writes=['tr'])
                    S.op('dve', lambda e: e.scalar_tensor_tensor(out=tr[:, 0:n], in0=tu[:, 0:n], scalar=-C2,
                                                                in1=tr[:, 0:n], op0=ALU.mult, op1=ALU.add),
                         reads=['tu', 'tr'], writes=['tr'])
                    S.op('dve', lambda e: e.tensor_scalar(out=tr[:, 0:n], in0=tr[:, 0:n], scalar1=-PI_S, scalar2=PI_S,
                                                          op0=ALU.max, op1=ALU.min), reads=['tr'], writes=['tr'])
                    S.op('act', lambda e, dst=dst: e.activation(out=dst[:, 0:n], in_=tr[:, 0:n], func=getattr(AF, _os.environ.get('SINFN', 'Sin'))),
                         reads=['tr'], writes=[dk])

            if tsub >= 1:
                sincos(512, 32, INVR, sr, cr, 'ropeR')
            if tsub >= 2:
                sincos(256, 16, INVM, sm, cm, 'ropeM')

            S.barrier()

    def ffn(gain_d, wg_d, wu_d, wd_d, tag):
        with ExitStack() as es:
            sb = lambda n, s, d: es.enter_context(nc.sbuf_tensor(tag + n, s, d))
            gbc = sb("gbc", [128, D], F32)
            junk = sb("junk", [128, D], BF16)
            xn = [sb("xn%d" % i, [128, D], BF16) for i in range(2)]
            wgb = [sb("wgb%d" % i, [128, 8, 256], BF16) for i in range(3)]
            wub = [sb("wub%d" % i, [128, 8, 256], BF16) for i in range(3)]
            wdb = [sb("wdb%d" % i, [128, 6, D], BF16) for i in range(2)]
            actT = sb("actT", [128, 6, NT], BF16)
            sil = [sb("sil%d" % i, [128, 512], F32) for i in range(2)]
            wgv = wg_d.rearrange("(kc p) f -> p kc f", p=128)
            wuv = wu_d.rearrange("(kc p) f -> p kc f", p=128)
            wdv = wd_d.rearrange("(fc p) d -> p fc d", p=128)

            def load_gran(g):
                s = g % 3
                S.dma('pool', wgb[s][:], wgv[:, :, g * 256:(g + 1) * 256], writes=[('wg', s)])
                S.dma('pool', wub[s][:], wuv[:, :, g * 256:(g + 1) * 256], writes=[('wu', s)])

            def load_wd(p):
                a, n = FFN_PARTS[p]
                s = p % 2
                S.dma('pool', wdb[s][:, 0:n, :], wdv[:, a:a + n, :], writes=[('wd', s)])

            load_gran(0); load_gran(1); load_gran(2)
            load_wd(0); load_wd(1)
            norm_to_XT(gain_d, gbc, junk, xn)
            step = 0
            dstep = 0
            for p, (a, n) in enumerate(FFN_PARTS):
                for g in range(a // 2, (a + n) // 2):
                    s = g % 3
                    for fl in range(2):
                        fcp = (g * 2 + fl) - a
                        for tg in range(4):
                            bg = (step % 2) * 2
                            bu = bg + 1
                            xr = [('XT', tg * 4 + i) for i in range(4)]
                            for k in range(8):
                                S.op('pe', lambda e, k=k, s=s, fl=fl, tg=tg, bg=bg: e.matmul(
                                    bank[bg][:], lhsT=wgb[s][:, k, fl * 128:(fl + 1) * 128],
                                    rhs=XT[:, k, tg * 512:(tg + 1) * 512], start=(k == 0), stop=(k == 7)),
                                    reads=[('wg', s)] + xr, writes=[('bank', bg)], inc=(k == 7))
                            for k in range(8):
                                S.op('pe', lambda e, k=k, s=s, fl=fl, tg=tg, bu=bu: e.matmul(
                                    bank[bu][:], lhsT=wub[s][:, k, fl * 128:(fl + 1) * 128],
                                    rhs=XT[:, k, tg * 512:(tg + 1) * 512], start=(k == 0), stop=(k == 7)),
                                    reads=[('wu', s)] + xr, writes=[('bank', bu)], inc=(k == 7))
                            sl = step % 2
                            S.op('act', lambda e, sl=sl, bg=bg: e.activation(out=sil[sl][:], in_=bank[bg][:], func=AF.Silu),
                                 reads=[('bank', bg)], writes=[('sil', sl)])
                            S.op('dve', lambda e, sl=sl, bu=bu, fcp=fcp, tg=tg: e.tensor_tensor(
                                out=actT[:, fcp, tg * 512:(tg + 1) * 512], in0=sil[sl][:], in1=bank[bu][:], op=ALU.mult),
                                reads=[('sil', sl), ('bank', bu)], writes=[('actT', fcp, tg)])
                            step += 1
                    if g + 3 < NFC // 2:
                        load_gran(g + 3)
                s = p % 2
                for tt in range(NTT):
                    for half in range(2):
                        bd = 4 + (dstep % 2)
                        for f in range(n):
                            S.op('pe', lambda e, f=f, tt=tt, half=half, bd=bd, s=s: e.matmul(
                                bank[bd][:], lhsT=actT[:, f, tt * 128:(tt + 1) * 128],
                                rhs=wdb[s][:, f, half * 512:(half + 1) * 512], start=(f == 0), stop=(f == n - 1)),
                                reads=[('actT', f, tt // 4), ('wd', s)], writes=[('bank', bd)], inc=(f == n - 1))
                        S.op('dve', lambda e, tt=tt, half=half, bd=bd: e.scalar_tensor_tensor(
                            out=h[:, tt, half * 512:(half + 1) * 512], in0=bank[bd][:], scalar=0.5,
                            in1=h[:, tt, half * 512:(half + 1) * 512], op0=ALU.mult, op1=ALU.add),
                            reads=[('bank', bd), ('h', tt)], writes=[('h', tt)])
                        dstep += 1
                if p + 2 < len(FFN_PARTS):
                    load_wd(p + 2)
            S.barrier()

    def store_h_raw():
        for tt in range(NTT):
            S.dma('sp', ov[:, tt, :], h[:, tt, :], reads=[('h', tt)], writes=[('out', tt)])
            out_keys.append(('out', tt))

    def final_norm_store():
        with ExitStack() as es:
            sb = lambda n, s, d: es.enter_context(nc.sbuf_tensor("fin_" + n, s, d))
            gbc = sb("gbc", [128, D], F32)
            junk = sb("junk", [128, D], BF16)
            obuf = [sb("obuf%d" % i, [128, D], F32) for i in range(2)]
            norm_stats(g_fin, gbc, junk)
            for tt in range(NTT):
                b = tt % 2
                S.op('dve', lambda e, tt=tt, b=b: e.scalar_tensor_tensor(
                    out=obuf[b][:], in0=h[:, tt, :], scalar=rstd[:, tt:tt + 1], in1=gbc[:],
                    op0=ALU.mult, op1=ALU.mult),
                    reads=[('h', tt), 'ss_rstd', 'gbc'], writes=[('obuf', b)])
                S.dma('sp', ov[:, tt, :], obuf[b][:], reads=[('obuf', b)], writes=[('out', tt)])
                out_keys.append(('out', tt))
            S.wait_all('sp', out_keys)
            S.barrier()

    def middle():
        w_inv = w_in.rearrange("(kc p) n -> p kc n", p=128)
        with ExitStack() as esm:
            sbm = lambda n, s, d: esm.enter_context(nc.sbuf_tensor("m_" + n, s, d))
            geo = sbm("geo", [128, 8], F32)
            LGrow = sbm("LGrow", [128, 16], F32); LG = sbm("LG", [128, 8], F32)
            KD = sbm("KD", [128, 16], F32); gC = sbm("gC", [128, 8], F32)
            DCAR = sbm("DCAR", [128, 128], F32); coef = sbm("coef", [128, 32], F32)
            ones32 = sbm("ones32", [128, 128], F32)
            RF = ctab[:, 0:128]; RB = ctab[:, 128:256]; EI = ctab[:, 256:384]
            E2 = ctab[:, 384:400]; M2 = ctab[:, 400:416]
            cr3 = cr[:, :].rearrange("p (a b) -> p a b", a=NTT); sr3 = sr[:, :].rearrange("p (a b) -> p a b", a=NTT)
            cm3 = cm[:, :].rearrange("p (a b) -> p a b", a=NTT); sm3 = sm[:, :].rearrange("p (a b) -> p a b", a=NTT)

            S.dma('sp', geo[:], geo_d, writes=['geo'])
            S.op('dve', lambda e: e.memset(ones32[:], 1.0), writes=['ones32'])

            with ExitStack() as es:
                sb = lambda n, s, d: es.enter_context(nc.sbuf_tensor("t_" + n, s, d))
                raw = sb("raw", [128, 16], F32); e1 = sb("e1", [128, 16], F32)
                t16 = sb("t16", [128, 16], F32);
                t8 = sb("t8", [128, 8], F32)
                if tsub >= 3:
                    S.dma('sp', raw[:], dec_d[0, :].partition_broadcast(128), writes=['raw'])
                    S.op('act', lambda e: e.activation(out=e1[:], in_=raw[:], func=AF.Exp, scale=-1.0), reads=['raw'], writes=['e1'])
                    S.op('act', lambda e: e.activation(out=e1[:], in_=e1[:], func=AF.Ln, bias=1.0), reads=['e1'], writes=['e1'])
                    S.op('dve', lambda e: e.tensor_scalar(out=LGrow[:], in0=e1[:], scalar1=-1.0, scalar2=None, op0=ALU.mult),
                         reads=['e1'], writes=['LGrow'])
                    S.op('dve', lambda e: e.tensor_copy(out=LG[0:64, :], in_=LGrow[0:64, 0:8]), reads=['LGrow'], writes=['LGa'])
                    S.op('dve', lambda e: e.tensor_copy(out=LG[64:128, :], in_=LGrow[64:128, 8:16]), reads=['LGrow'], writes=['LGb'])
                    LGk = ['LGa', 'LGb']
                if tsub >= 4:
                    S.op('dve', lambda e: e.tensor_tensor(out=t16[:], in0=LGrow[:], in1=E2, op=ALU.mult),
                         reads=['LGrow', 'ctab'], writes=['t16'])
                    S.op('act', lambda e: e.activation(out=KD[:], in_=t16[:], func=AF.Exp, bias=LN8), reads=['t16'], writes=['KD'])
                if tsub >= 5:
                    S.op('act', lambda e: e.activation(out=gC[:], in_=LG[:], func=AF.Exp, scale=128.0), reads=LGk, writes=['gC'])
                    d3 = DCAR[:, :].rearrange("p (c hh) -> p c hh", c=NTT)
                    S.op('dve', lambda e: e.tensor_tensor(out=d3, in0=M2.unsqueeze(2).to_broadcast([128, NTT, 8]),
                                                          in1=LG[:, :].unsqueeze(1).to_broadcast([128, NTT, 8]), op=ALU.mult),
                         reads=['ctab'] + LGk, writes=['DCAR'])
                    S.op('act', lambda e: e.activation(out=DCAR[:], in_=DCAR[:], func=AF.Exp), reads=['DCAR'], writes=['DCAR'])
                if tsub >= 6:
                    for r_ in range(4):
                        S.op('dve', lambda e, r_=r_: e.tensor_scalar(out=t8[:], in0=LG[:], scalar1=geo[:, r_:r_ + 1], scalar2=None,
                                                                     op0=ALU.mult), reads=LGk + ['geo'], writes=['t8'])
                        S.op('act', lambda e, r_=r_: e.activation(out=coef[:, r_ * 8:(r_ + 1) * 8], in_=t8[:], func=AF.Exp, scale=2048.0),
                             reads=['t8'], writes=[('coef', r_)])
                        S.op('dve', lambda e, r_=r_: e.tensor_scalar(out=coef[:, r_ * 8:(r_ + 1) * 8], in0=coef[:, r_ * 8:(r_ + 1) * 8],
                                                                     scalar1=geo[:, 4 + r_:5 + r_], scalar2=None, op0=ALU.mult),
                             reads=[('coef', r_), 'geo'], writes=[('coef', r_)])

                S.barrier()
                if stop == 'tables':
                    S.dma('sp', dbg32[:, 0:512], cr[:], reads=['ropeRc'], writes=['dbg_a'])
                    S.dma('sp', dbg32[:, 512:1024], sr[:], reads=['ropeRs'], writes=['dbg_b'])
                    S.dma('sp', dbg32[:, 1024:1280], cm[:], reads=['ropeMc'], writes=['dbg_c'])
                    S.dma('sp', dbg32[:, 1280:1536], sm[:], reads=['ropeMs'], writes=['dbg_d'])
                    S.dma('sp', dbg32[:, 1536:1552], KD[:], reads=['KD'], writes=['dbg_e'])
                    S.dma('sp', dbg32[:, 1552:1560], LG[:], reads=['LGa', 'LGb'], writes=['dbg_f'])
                    S.dma('sp', dbg32[:, 1560:1688], DCAR[:], reads=['DCAR'], writes=['dbg_g'])
                    S.dma('sp', dbg32[:, 1688:1720], coef[:], reads=[('coef', i) for i in range(4)], writes=['dbg_h'])
                    S.dma('sp', dbg32[:, 1720:1728], gC[:], reads=['gC'], writes=['dbg_i'])
                    out_keys.extend(['dbg_' + c for c in 'abcdefghi'])
                    raise _Stop()

            TAB = ['ropeRs', 'ropeRc', 'ropeMs', 'ropeMc']

            def rope_tok(x4, o4, c3t, s3t, nh, hf, tmps, rk):
                cb = c3t.unsqueeze(1).to_broadcast([128, nh, hf])
                sb_ = s3t.unsqueeze(1).to_broadcast([128, nh, hf])
                n = nh * hf
                tv = [t[:, 0:n].rearrange("p (a b) -> p a b", a=nh) for t in tmps]
                x1 = x4[:, :, 0, :]; x2 = x4[:, :, 1, :]
                rd, wr = rk
                S.op('dve', lambda e: e.tensor_tensor(out=tv[0], in0=x1, in1=cb, op=ALU.mult), reads=rd + TAB, writes=['rt0'])
                S.op('dve', lambda e: e.tensor_tensor(out=tv[1], in0=x2, in1=sb_, op=ALU.mult), reads=rd + TAB, writes=['rt1'])
                S.op('dve', lambda e: e.tensor_tensor(out=o4[:, :, 0, :], in0=tv[0], in1=tv[1], op=ALU.subtract),
                     reads=['rt0', 'rt1'], writes=wr)
                S.op('dve', lambda e: e.tensor_tensor(out=tv[2], in0=x1, in1=sb_, op=ALU.mult), reads=rd + TAB, writes=['rt2'])
                S.op('dve', lambda e: e.tensor_tensor(out=tv[3], in0=x2, in1=cb, op=ALU.mult), reads=rd + TAB, writes=['rt3'])
                S.op('dve', lambda e: e.tensor_tensor(out=o4[:, :, 1, :], in0=tv[2], in1=tv[3], op=ALU.add),
                     reads=['rt2', 'rt3'], writes=wr)

            cqnT = esm.enter_context(nc.sbuf_tensor("m_cqnT", [128, 2, NT], BF16))
            with ExitStack() as es:
                sb = lambda n, s, d: es.enter_context(nc.sbuf_tensor("p1_" + n, s, d))
                gbc = sb("gbc", [128, D], F32); junk = sb("junk", [128, D], BF16)
                xn = [sb("xn%d" % i, [128, D], BF16) for i in range(2)]
                W1 = sb("W1", [128, 8, 416], BF16)
                cqkv = sb("cqkv", [128, NTT, 416], F32)
                nrm = [sb("nrm%d" % i, [128, 512], BF16) for i in range(2)]
                ckvnT_loc = sb("ckvnT_loc", [128, NT], BF16)
                kpeT_loc = sb("kpeT_loc", [32, NT], BF16)
                gq = sb("gq", [128, 256], F32); gkv = sb("gkv", [128, 128], F32)
                kpe_r = sb("kpe_r", [128, NTT, 32], BF16)
                ssq = sb("ssq", [128, NTT], F32); sskv = sb("sskv", [128, NTT], F32)
                v1 = sb("v1", [128, NTT], F32); s1 = sb("s1", [128, NTT], F32)
                rq = sb("rq", [128, NTT], F32); rkv = sb("rkv", [128, NTT], F32)
                tmps = [sb("rt%d" % i, [128, 256], F32) for i in range(4)]
                for i_ in range(2):
                    S.op('dve', lambda e, i_=i_: e.memset(nrm[i_][:], 0.0), writes=[('nrmq', i_), ('nrmk', i_), ('nrmp', i_)])
                S.dma('pool', W1[:], w_inv[:, :, 0:416], writes=['W1'])
                S.dma('sp', gq[:], g_q[0, :].partition_broadcast(128), writes=['gq'])
                S.dma('sp', gkv[:], g_kv[0, :].partition_broadcast(128), writes=['gkv'])
                norm_to_XT(g_mix, gbc, junk, xn)
                for tt in range(NTT):
                    b = tt % 2
                    for k in range(8):
                        S.op('pe', lambda e, k=k, tt=tt, b=b: e.matmul(
                            bank[b][:, 0:416], lhsT=XT[:, k, tt * 128:(tt + 1) * 128], rhs=W1[:, k, :],
                            start=(k == 0), stop=(k == 7)),
                            reads=[('XT', tt), 'W1'], writes=[('bank', b)], inc=(k == 7))
                    S.op('act', lambda e, tt=tt, b=b: e.activation(out=cqkv[:, tt, :], in_=bank[b][:, 0:416], func=AF.Copy),
                         reads=[('bank', b)], writes=[('cqkv', tt)])
                    S.op('dve', lambda e, tt=tt: e.scalar_tensor_tensor(
                        out=junk[:, 0:256], in0=cqkv[:, tt, 0:256], scalar=1.0, in1=cqkv[:, tt, 0:256],
                        op0=ALU.mult, op1=ALU.mult, accum_out=ssq[:, tt:tt + 1]), reads=[('cqkv', tt)], writes=[('ssq', tt)])
                    S.op('dve', lambda e, tt=tt: e.scalar_tensor_tensor(
                        out=junk[:, 256:384], in0=cqkv[:, tt, 256:384], scalar=1.0, in1=cqkv[:, tt, 256:384],
                        op0=ALU.mult, op1=ALU.mult, accum_out=sskv[:, tt:tt + 1]), reads=[('cqkv', tt)], writes=[('sskv', tt)])
                if tsub == 10:
                    S.barrier(); raise _Stop()
                rstd_batch(ssq, NTT, 1.0 / 256, v1, s1, rq, 'ssq')
                rstd_batch(sskv, NTT, 1.0 / 128, v1, s1, rkv, 'sskv')
                x4 = cqkv[:, :, 384:416].rearrange("p t (two j) -> p t two j", two=2)
                o4 = kpe_r[:, :, :].rearrange("p t (two j) -> p t two j", two=2)
                tv = [t[:, 0:256].rearrange("p (a b) -> p a b", a=NTT) for t in tmps]
                allq = [('cqkv', t) for t in range(NTT)]
                S.op('dve', lambda e: e.tensor_tensor(out=tv[0], in0=x4[:, :, 0, :], in1=cm3, op=ALU.mult), reads=allq + TAB, writes=['rt0'])
                S.op('dve', lambda e: e.tensor_tensor(out=tv[1], in0=x4[:, :, 1, :], in1=sm3, op=ALU.mult), reads=allq + TAB, writes=['rt1'])
                S.op('dve', lambda e: e.tensor_tensor(out=o4[:, :, 0, :], in0=tv[0], in1=tv[1], op=ALU.subtract), reads=['rt0', 'rt1'], writes=['kpe_r'])
                S.op('dve', lambda e: e.tensor_tensor(out=tv[2], in0=x4[:, :, 0, :], in1=sm3, op=ALU.mult), reads=allq + TAB, writes=['rt2'])
                S.op('dve', lambda e: e.tensor_tensor(out=tv[3], in0=x4[:, :, 1, :], in1=cm3, op=ALU.mult), reads=allq + TAB, writes=['rt3'])
                S.op('dve', lambda e: e.tensor_tensor(out=o4[:, :, 1, :], in0=tv[2], in1=tv[3], op=ALU.add), reads=['rt2', 'rt3'], writes=['kpe_r'])
                if tsub == 11:
                    S.barrier(); raise _Stop()
                for tt in range(NTT):
                    b = tt % 2
                    S.op('dve', lambda e, tt=tt, b=b: e.scalar_tensor_tensor(
                        out=nrm[b][:, 0:256], in0=cqkv[:, tt, 0:256], scalar=rq[:, tt:tt + 1], in1=gq[:],
                        op0=ALU.mult, op1=ALU.mult), reads=[('cqkv', tt), 'ssq_rstd', 'gq'], writes=[('nrmq', b)])
                    S.op('dve', lambda e, tt=tt, b=b: e.scalar_tensor_tensor(
                        out=nrm[b][:, 256:384], in0=cqkv[:, tt, 256:384], scalar=rkv[:, tt:tt + 1], in1=gkv[:],
                        op0=ALU.mult, op1=ALU.mult), reads=[('cqkv', tt), 'sskv_rstd', 'gkv'], writes=[('nrmk', b)])
                    pb = 6 + b
                    pv = bankbf[pb]
                    S.op('act', lambda e, tt=tt, b=b: e.activation(out=nrm[b][:, 384:416], in_=kpe_r[:, tt, :], func=AF.Copy),
                         reads=['kpe_r'], writes=[('nrmp', b)])
                    for j in range(4):
                        S.op('pe', lambda e, j=j, b=b, pv=pv: e.transpose(
                            out=pv[:, j * 128:(j + 1) * 128], in_=nrm[b][:, j * 128:(j + 1) * 128], identity=ident[:]),
                            reads=[('nrmq', b), ('nrmk', b), ('nrmp', b), 'ident'], writes=[('bank', pb)], inc=(j == 3))
                    S.op('act', lambda e, tt=tt, pv=pv: e.activation(
                        out=cqnT[:, :, tt * 128:(tt + 1) * 128],
                        in_=pv[:, 0:256].rearrange("p (k t) -> p k t", k=2), func=AF.Copy),
                        reads=[('bank', pb)], writes=[('cqnT', tt)])
                    S.op('act', lambda e, tt=tt, pv=pv: e.activation(out=ckvnT_loc[:, tt * 128:(tt + 1) * 128], in_=pv[:, 256:384], func=AF.Copy),
                         reads=[('bank', pb)], writes=['ckvnT_loc'])
                    S.op('act', lambda e, tt=tt, pv=pv: e.activation(out=kpeT_loc[0:32, tt * 128:(tt + 1) * 128], in_=pv[0:32, 384:512], func=AF.Copy),
                         reads=[('bank', pb)], writes=['kpeT_loc'])
                if tsub == 12:
                    S.barrier(); raise _Stop()
                S.dma('sp', xin[0:128, :], ckvnT_loc[:], reads=['ckvnT_loc'], writes=['xin_a'])
                S.dma('sp', xin[128:160, :], kpeT_loc[0:32, :], reads=['kpeT_loc'], writes=['xin_b'])
                if tsub == 13:
                    S.barrier(); raise _Stop()
                S.collective("AllGather", xin.ap().opt(), xout.ap().opt(), GROUPS, reads=['xin_a', 'xin_b'], writes=['xout'], name="ag1")
                if tsub == 14:
                    S.barrier(); raise _Stop()
                if stop == 'p1':
                    S.dma('sp', dbgbf[:, 0:2048], ckvnT_loc[:], reads=['ckvnT_loc'], writes=['dbg_a'])
                    S.dma('sp', dbgbf[0:32, 2048:4096], kpeT_loc[0:32, :], reads=['kpeT_loc'], writes=['dbg_b'])
                    S.dma('sp', dbgbf[:, 4096:8192], cqnT[:, :, :].rearrange("p a t -> p (a t)"), reads=[('cqnT', t) for t in range(NTT)], writes=['dbg_c'])
                    S.dma('sp', dbgbf2[:, :], xout[0:128, :], reads=['xout'], writes=['dbg_d'])
                    out_keys.extend(['dbg_' + c for c in 'abcd'])
                S.barrier()
                if stop == 'p1':
                    raise _Stop()

            esT = ExitStack()
            T_all = esT.enter_context(nc.sbuf_tensor("m_T_all", [128, NTT, 512], BF16))
            DT = esT.enter_context(nc.sbuf_tensor("m_DT", [128, 8, 128], BF16))
            QD = esT.enter_context(nc.sbuf_tensor("m_QD", [128, 8, 128], BF16))
            with ExitStack() as es:
                sb = lambda n, s, d: es.enter_context(nc.sbuf_tensor("dt_" + n, s, d))
                t128 = sb("t128", [128, 128], F32); t128b = sb("t128b", [128, 128], F32)
                LGk = ['LGa', 'LGb']
                for hh in range(8):
                    S.op('dve', lambda e, hh=hh: e.tensor_scalar(out=t128[:], in0=RF, scalar1=LGrow[:, hh:hh + 1], scalar2=None,
                                                                 op0=ALU.mult), reads=['ctab', 'LGrow'], writes=['t128'])
                    S.op('dve', lambda e, hh=hh: e.scalar_tensor_tensor(out=t128b[:], in0=RB, scalar=LGrow[:, 8 + hh:9 + hh],
                                                                        in1=t128[:], op0=ALU.mult, op1=ALU.add),
                         reads=['ctab', 'LGrow', 't128'], writes=['t128b'])
                    S.op('act', lambda e, hh=hh: e.activation(out=DT[:, hh, :], in_=t128b[:], func=AF.Exp, bias=LN8),
                         reads=['t128b'], writes=[('DT', hh)])
                    S.op('act', lambda e, hh=hh: e.activation(out=QD[:, hh, :], in_=EI, func=AF.Exp, scale=LG[:, hh:hh + 1]),
                         reads=['ctab'] + LGk, writes=[('QD', hh)])
                S.barrier()
                if tsub == 20:
                    raise _Stop()

            with ExitStack() as es:
                sb = lambda n, s, d: es.enter_context(nc.sbuf_tensor("ra_" + n, s, d))
                wkr = sb("wkr", [128, 8, 512], BF16); wvr = sb("wvr", [128, 8, 512], BF16)
                rk_c = [sb("rk_c%d" % i, [128, 512], BF16) for i in range(2)]
                rv_c = [sb("rv_c%d" % i, [128, 512], BF16) for i in range(2)]
                kdd = [sb("kdd%d" % i, [128, 8, 128], BF16) for i in range(2)]
                tmps = [sb("rt%d" % i, [128, 256], F32) for i in range(4)]
                R = [sb("R%d" % i, [128, 512], F32) for i in range(2)]
                tR = sb("tR", [128, 512], F32)
                S.dma('pool', wkr[:], w_inv[:, :, 928:1440], writes=['wkr'])
                S.dma('pool', wvr[:], w_inv[:, :, 1440:1952], writes=['wvr'])
                S.op('dve', lambda e: e.memset(R[0][:], 0.0), writes=[('R', 0, 'f'), ('R', 0, 'b')])
                gCb = gC[:, :].unsqueeze(2).to_broadcast([128, 8, 64])
                KDf = KD[:, 0:8].unsqueeze(2).to_broadcast([128, 8, 64])
                KDb = KD[:, 8:16].unsqueeze(2).to_broadcast([128, 8, 64])

                def scan_step(c, i, dirn):
                    lo, hi = (0, 64) if dirn == 'f' else (64, 128)
                    cur, nxt = R[i % 2], R[(i + 1) % 2]
                    S.op('dve', lambda e: e.tensor_tensor(
                        out=tR[lo:hi, :].rearrange("p (a b) -> p a b", a=8), in0=cur[lo:hi, :].rearrange("p (a b) -> p a b", a=8),
                        in1=gCb[lo:hi], op=ALU.mult), reads=[('R', i % 2, dirn), 'gC'], writes=[('tR', dirn)])
                    S.op('dve', lambda e: e.tensor_tensor(out=nxt[lo:hi, :], in0=tR[lo:hi, :], in1=T_all[lo:hi, c, :], op=ALU.add),
                         reads=[('tR', dirn), ('T', c, dirn)], writes=[('R', (i + 1) % 2, dirn)])
                    S.op('act', lambda e: e.activation(out=T_all[lo:hi, c, :], in_=cur[lo:hi, :], func=AF.Copy),
                         reads=[('R', i % 2, dirn)], writes=[('T', c, dirn)])

                def retA_proj(c):
                    b = c % 2
                    bk, bv, bu = b, 2 + b, 4 + b
                    for k in range(8):
                        S.op('pe', lambda e, k=k, c=c, bk=bk: e.matmul(
                            bank[bk][:], lhsT=XT[:, k, c * 128:(c + 1) * 128], rhs=wkr[:, k, :], start=(k == 0), stop=(k == 7)),
                            reads=[('XT', c), 'wkr'], writes=[('bank', bk)], inc=(k == 7))
                    for k in range(8):
                        S.op('pe', lambda e, k=k, c=c, bv=bv: e.matmul(
                            bank[bv][:], lhsT=XT[:, k, c * 128:(c + 1) * 128], rhs=wvr[:, k, :], start=(k == 0), stop=(k == 7)),
                            reads=[('XT', c), 'wvr'], writes=[('bank', bv)], inc=(k == 7))

                def retA_dve(c):
                    b = c % 2
                    bk, bv, bu = b, 2 + b, 4 + b
                    x4 = bank[bk][:, :].rearrange("p (hh two j) -> p hh two j", hh=8, two=2)
                    o4 = rk_c[b][:, :].rearrange("p (hh two j) -> p hh two j", hh=8, two=2)
                    rope_tok(x4, o4, cr3[:, c, :], sr3[:, c, :], 8, 32, tmps, ([('bank', bk)], [('rk_c', b)]))
                    S.op('act', lambda e, b=b, bv=bv: e.activation(out=rv_c[b][:], in_=bank[bv][:], func=AF.Copy),
                         reads=[('bank', bv)], writes=[('rv_c', b)])
                    rk3 = rk_c[b][:, :].rearrange("p (a d) -> p a d", a=8)
                    S.op('dve', lambda e, b=b, rk3=rk3: e.tensor_tensor(out=kdd[b][:, :, 0:64], in0=rk3, in1=KDf, op=ALU.mult),
                         reads=[('rk_c', b), 'KD'], writes=[('kdd', b)])
                    S.op('dve', lambda e, b=b, rk3=rk3: e.tensor_tensor(out=kdd[b][:, :, 64:128], in0=rk3, in1=KDb, op=ALU.mult),
                         reads=[('rk_c', b), 'KD'], writes=[('kdd', b)])

                def retA_state(c):
                    b = c % 2
                    bk, bv, bu = b, 2 + b, 4 + b
                    for hh in range(8):
                        S.op('pe', lambda e, hh=hh, b=b, bu=bu: e.matmul(
                            bank[bu][:, hh * 64:(hh + 1) * 64], lhsT=kdd[b][:, hh, :], rhs=rv_c[b][:, hh * 64:(hh + 1) * 64],
                            start=True, stop=True), reads=[('kdd', b), ('rv_c', b)], writes=[('bank', bu)], inc=(hh == 7))
                    S.op('act', lambda e, c=c, bu=bu: e.activation(out=T_all[:, c, :], in_=bank[bu][:], func=AF.Copy),
                         reads=[('bank', bu)], writes=[('T', c, 'f'), ('T', c, 'b')])
                    scan_step(c, c, 'f')

                retA_proj(0)
                for c in range(NTT):
                    retA_dve(c)
                    if c + 1 < NTT:
                        retA_proj(c + 1)
                    retA_state(c)
                if tsub == 25:
                    S.barrier(); raise _Stop()
                for i in range(NTT):
                    scan_step(NTT - 1 - i, i, 'b')
                    if tsub == 26 and i == 0:
                        S.barrier(); raise _Stop()
                if tsub == 27:
                    S.barrier(); raise _Stop()
                S.dma('sp', xin2[:, :], R[0][:], reads=[('R', 0, 'f'), ('R', 0, 'b')], writes=['xin2'])
                if tsub == 28:
                    S.barrier(); raise _Stop()
                S.collective("AllGather", xin2.ap().opt(), xout2.ap().opt(), GROUPS, reads=['xin2'], writes=['xout2'], name="ag2")
                if tsub == 29:
                    S.barrier(); raise _Stop()
                if stop == 'reta':
                    S.dma('sp', dbgbf[:, 0:8192], T_all[:, :, :].rearrange("p a t -> p (a t)"),
                          reads=[('T', c, d_) for c in range(NTT) for d_ in 'fb'], writes=['dbg_a'])
                    S.dma('sp', dbg32[:, 0:512], R[0][:], reads=[('R', 0, 'f'), ('R', 0, 'b')], writes=['dbg_b'])
                    out_keys.extend(['dbg_a', 'dbg_b'])
                S.barrier()
                if stop == 'reta':
                    raise _Stop()

            with ExitStack() as es:
                sb = lambda n, s, d: es.enter_context(nc.sbuf_tensor("rb_" + n, s, d))
                wqr = sb("wqr", [128, 8, 512], BF16); wkr = sb("wkr", [128, 8, 512], BF16)
                wvr = sb("wvr", [128, 8, 512], BF16); wgr = sb("wgr", [128, 8, 512], BF16)
                carry = sb("carry", [128, 512], F32)
                tC = sb("tC", [128, 512], F32)
                rq_dup = [sb("rq_dup%d" % i, [128, 8, 128], BF16) for i in range(2)]
                rk_c = [sb("rk_c%d" % i, [128, 512], BF16) for i in range(2)]
                rv_c = [sb("rv_c%d" % i, [128, 512], BF16) for i in range(2)]
                qTp = [sb("qTp%d" % i, [128, 8, 128], BF16) for i in range(2)]
                qdT = [sb("qdT%d" % i, [128, 8, 128], BF16) for i in range(2)]
                kT = [sb("kT%d" % i, [128, 4, 128], BF16) for i in range(2)]
                sTm = sb("sTm", [128, 8, 128], BF16)
                Tbf = sb("Tbf", [128, 512], BF16)
                tmps = [sb("rt%d" % i, [128, 256], F32) for i in range(4)]
                osq = sb("osq", [128, 512], F32); ocen = sb("ocen", [128, 512], F32)
                sg = sb("sg", [128, 512], F32); rout = sb("rout", [128, 512], BF16)
                st_s = sb("st_s", [128, 8], F32); st_q = sb("st_q", [128, 8], F32)
                st_m = sb("st_m", [128, 8], F32); st_v = sb("st_v", [128, 8], F32); st_r = sb("st_r", [128, 8], F32)
                mhalf = sb("mhalf", [128, 8], F32)
                S.dma('pool', wqr[:], w_inv[:, :, 416:928], writes=['wqr'])
                S.dma('pool', wkr[:], w_inv[:, :, 928:1440], writes=['wkr'])
                S.dma('pool', wvr[:], w_inv[:, :, 1440:1952], writes=['wvr'])
                S.dma('pool', wgr[:], w_inv[:, :, 1952:2464], writes=['wgr'])
                S.op('dve', lambda e: e.memset(mhalf[:], -0.5), writes=['mhalf'])
                with ExitStack() as es2:
                    Epc = [es2.enter_context(nc.sbuf_tensor("rb_Epc%d" % i, [128, 512], F32)) for i in range(2)]
                    for r_ in range(4):
                        S.dma('sp', Epc[r_ % 2][:], xout2[r_ * 128:(r_ + 1) * 128, :], reads=['xout2'], writes=[('Epc', r_ % 2)])
                        cf = coef[:, r_ * 8:(r_ + 1) * 8].unsqueeze(2).to_broadcast([128, 8, 64])
                        dst = (carry if r_ == 0 else tC)
                        S.op('dve', lambda e, r_=r_, cf=cf, dst=dst: e.tensor_tensor(
                            out=dst[:, :].rearrange("p (a b) -> p a b", a=8), in0=Epc[r_ % 2][:, :].rearrange("p (a b) -> p a b", a=8),
                            in1=cf, op=ALU.mult), reads=[('Epc', r_ % 2), ('coef', r_)], writes=['carry' if r_ == 0 else 'tC'])
                        if r_ > 0:
                            S.op('dve', lambda e: e.tensor_tensor(out=carry[:], in0=carry[:], in1=tC[:], op=ALU.add),
                                 reads=['carry', 'tC'], writes=['carry'])
                    S.barrier()
                d3 = DCAR[:, :].rearrange("p (c hh) -> p c hh", c=NTT)
                sT4 = sTm[:, :, :].rearrange("p (a two) t -> p a two t", two=2)
                DT4 = DT[:, :, :].rearrange("p (a two) t -> p a two t", two=2)

                def retB_front(c):
                    p = c % 2
                    for (bk, wt, wk_) in ((0, wqr, 'wqr'), (1, wkr, 'wkr'), (2, wvr, 'wvr')):
                        for k in range(8):
                            S.op('pe', lambda e, k=k, bk=bk, wt=wt: e.matmul(
                                bank[bk][:], lhsT=XT[:, k, c * 128:(c + 1) * 128], rhs=wt[:, k, :], start=(k == 0), stop=(k == 7)),
                                reads=[('XT', c), wk_], writes=[('bank', bk)], inc=(k == 7))
                    x4 = bank[0][:, :].rearrange("p (hh two j) -> p hh two j", hh=8, two=2)
                    o4 = rq_dup[p][:, :, 0:64].rearrange("p hh (two j) -> p hh two j", two=2)
                    rope_tok(x4, o4, cr3[:, c, :], sr3[:, c, :], 8, 32, tmps, ([('bank', 0)], [('rq_a', p)]))
                    S.op('act', lambda e: e.activation(out=rq_dup[p][:, :, 64:128], in_=rq_dup[p][:, :, 0:64], func=AF.Copy),
                         reads=[('rq_a', p)], writes=[('rq_b', p)])
                    x4 = bank[1][:, :].rearrange("p (hh two j) -> p hh two j", hh=8, two=2)
                    o4 = rk_c[p][:, :].rearrange("p (hh two j) -> p hh two j", hh=8, two=2)
                    rope_tok(x4, o4, cr3[:, c, :], sr3[:, c, :], 8, 32, tmps, ([('bank', 1)], [('rk_c', p)]))
                    S.op('act', lambda e: e.activation(out=rv_c[p][:], in_=bank[2][:], func=AF.Copy),
                         reads=[('bank', 2)], writes=[('rv_c', p)])
                    pq = bankbf[3]
                    for hh in range(8):
                        S.op('pe', lambda e, hh=hh: e.transpose(out=pq[:, hh * 128:(hh + 1) * 128], in_=rq_dup[p][:, hh, :], identity=ident[:]),
                             reads=[('rq_a', p), ('rq_b', p), 'ident'], writes=[('bank', 3)], inc=(hh == 7))
                    pk = bankbf[4]
                    for pr in range(4):
                        S.op('pe', lambda e, pr=pr: e.transpose(out=pk[:, pr * 128:(pr + 1) * 128], in_=rk_c[p][:, pr * 128:(pr + 1) * 128],
                                                              identity=ident[:]),
                             reads=[('rk_c', p), 'ident'], writes=[('bank', 4)], inc=(pr == 3))
                    S.op('act', lambda e: e.activation(out=qTp[p][:, :, :], in_=pq[:, 0:1024].rearrange("p (a t) -> p a t", a=8), func=AF.Copy),
                         reads=[('bank', 3)], writes=[('qTp', p)])
                    S.op('dve', lambda e: e.tensor_tensor(out=qdT[p][:, :, :], in0=qTp[p][:, :, :], in1=QD[:, :, :], op=ALU.mult),
                         reads=[('qTp', p)] + [('QD', i) for i in range(8)], writes=[('qdT', p)])
                    S.op('act', lambda e: e.activation(out=kT[p][:, :, :], in_=pk[:, 0:512].rearrange("p (a t) -> p a t", a=4), func=AF.Copy),
                         reads=[('bank', 4)], writes=[('kT', p)])

                def retB_back(c):
                    p = c % 2
                    for hh in range(8):
                        lo = (hh % 2) * 64
                        sbk = 5 + (hh % 2)
                        S.op('pe', lambda e, hh=hh, lo=lo, sbk=sbk: e.matmul(
                            bank[sbk][:, (hh // 2) * 128:(hh // 2 + 1) * 128], lhsT=kT[p][lo:lo + 64, hh // 2, :],
                            rhs=qTp[p][lo:lo + 64, hh, :], start=True, stop=True),
                            reads=[('kT', p), ('qTp', p)], writes=[('bank', sbk)], inc=(hh >= 6))
                    for g4 in range(2):
                        S.op('dve', lambda e, g4=g4: e.tensor_tensor(
                            out=sT4[:, :, g4, :], in0=bank[5 + g4][:, :].rearrange("p (a t) -> p a t", a=4),
                            in1=DT4[:, :, g4, :], op=ALU.mult),
                            reads=[('bank', 5 + g4)] + [('DT', i) for i in range(8)], writes=[('sTm', g4)])
                    S.op('dve', lambda e: e.tensor_tensor(
                        out=tC[:, :].rearrange("p (a b) -> p a b", a=8), in0=carry[:, :].rearrange("p (a b) -> p a b", a=8),
                        in1=d3[:, c, :].unsqueeze(2).to_broadcast([128, 8, 64]), op=ALU.mult),
                        reads=['carry', 'DCAR'], writes=['tC'])
                    S.op('dve', lambda e: e.tensor_tensor(out=Tbf[:], in0=tC[:], in1=T_all[:, c, :], op=ALU.add),
                         reads=['tC', ('T', c, 'f'), ('T', c, 'b')], writes=['Tbf'])
                    for hh in range(8):
                        S.op('pe', lambda e, hh=hh: e.matmul(bank[7][:, hh * 64:(hh + 1) * 64], lhsT=sTm[:, hh, :],
                                                             rhs=rv_c[p][:, hh * 64:(hh + 1) * 64], start=True, stop=False),
                             reads=[('sTm', hh % 2), ('rv_c', p)], writes=[('bank', 7)], inc=False)
                        S.op('pe', lambda e, hh=hh: e.matmul(bank[7][:, hh * 64:(hh + 1) * 64], lhsT=qdT[p][:, hh, :],
                                                             rhs=Tbf[:, hh * 64:(hh + 1) * 64], start=False, stop=True),
                             reads=[('qdT', p), 'Tbf'], writes=[('bank', 7)], inc=(hh == 7))
                    for k in range(8):
                        S.op('pe', lambda e, k=k: e.matmul(
                            bank[5][:], lhsT=XT[:, k, c * 128:(c + 1) * 128], rhs=wgr[:, k, :], start=(k == 0), stop=(k == 7)),
                            reads=[('XT', c), 'wgr'], writes=[('bank', 5)], inc=(k == 7))
                    S.op('act', lambda e: e.activation(out=sg[:], in_=bank[5][:], func=AF.Silu), reads=[('bank', 5)], writes=['sg'])
                    o3 = bank[7][:, :].rearrange("p (a b) -> p a b", a=8)
                    S.op('dve', lambda e: e.tensor_reduce(out=st_s[:], in_=o3, axis=AX.X, op=ALU.add), reads=[('bank', 7)], writes=['st_s'])
                    S.op('act', lambda e: e.activation(out=osq[:], in_=bank[7][:], func=AF.Square), reads=[('bank', 7)], writes=['osq'])
                    S.op('dve', lambda e: e.tensor_reduce(out=st_q[:], in_=osq[:, :].rearrange("p (a b) -> p a b", a=8), axis=AX.X, op=ALU.add),
                         reads=['osq'], writes=['st_q'])
                    S.op('dve', lambda e: e.tensor_scalar(out=st_m[:], in0=st_s[:], scalar1=1.0 / 64, scalar2=None, op0=ALU.mult),
                         reads=['st_s'], writes=['st_m'])
                    S.op('dve', lambda e: e.tensor_tensor(out=st_v[:], in0=st_m[:], in1=st_m[:], op=ALU.mult), reads=['st_m'], writes=['st_v'])
                    S.op('dve', lambda e: e.scalar_tensor_tensor(out=st_v[:], in0=st_q[:], scalar=1.0 / 64, in1=st_v[:],
                                                                op0=ALU.mult, op1=ALU.subtract), reads=['st_q', 'st_v'], writes=['st_v'])
                    S.op('dve', lambda e: e.tensor_scalar(out=st_v[:], in0=st_v[:], scalar1=EPS, scalar2=None, op0=ALU.add),
                         reads=['st_v'], writes=['st_v'])
                    S.op('pool', lambda e: e.tensor_tensor(out=st_r[:], in0=st_v[:], in1=mhalf[:], op=ALU.pow),
                         reads=['st_v', 'mhalf'], writes=['st_r'])
                    S.op('dve', lambda e: e.tensor_tensor(out=ocen[:, :].rearrange("p (a b) -> p a b", a=8), in0=o3,
                                                          in1=st_m[:, :].unsqueeze(2).to_broadcast([128, 8, 64]), op=ALU.subtract),
                         reads=[('bank', 7), 'st_m'], writes=['ocen'])
                    S.op('dve', lambda e: e.tensor_tensor(out=ocen[:, :].rearrange("p (a b) -> p a b", a=8),
                                                          in0=ocen[:, :].rearrange("p (a b) -> p a b", a=8),
                                                          in1=st_r[:, :].unsqueeze(2).to_broadcast([128, 8, 64]), op=ALU.mult),
                         reads=['ocen', 'st_r'], writes=['ocen'])
                    S.op('dve', lambda e: e.tensor_tensor(out=rout[:], in0=ocen[:], in1=sg[:], op=ALU.mult),
                         reads=['ocen', 'sg'], writes=['rout'])
                    pr_ = bankbf[6]
                    for j in range(4):
                        S.op('pe', lambda e, j=j: e.transpose(out=pr_[:, j * 128:(j + 1) * 128], in_=rout[:, j * 128:(j + 1) * 128], identity=ident[:]),
                             reads=['rout', 'ident'], writes=[('bank', 6)], inc=(j == 3))
                    S.op('act', lambda e: e.activation(out=XT[:, 4:8, c * 128:(c + 1) * 128],
                                                       in_=pr_[:, 0:512].rearrange("p (a t) -> p a t", a=4), func=AF.Copy),
                         reads=[('bank', 6)], writes=[('XT', c)])

                retB_front(0)
                for c in range(NTT):
                    if c + 1 < NTT:
                        retB_front(c + 1)
                    retB_back(c)
                if stop == 'retb':
                    S.dma('sp', dbgbf[:, 0:8192], XT[:, 4:8, :].rearrange("p a t -> p (a t)"), reads=[('XT', c) for c in range(NTT)], writes=['dbg_a'])
                    out_keys.extend(['dbg_a'])
                S.barrier()
                if stop == 'retb':
                    raise _Stop()

            esT.close()

            with ExitStack() as es:
                sb = lambda n, s, d: es.enter_context(nc.sbuf_tensor("at_" + n, s, d))
                qaugT = sb("qaugT", [128, 8, NT], BF16)
                ckvnT_all = sb("ckvnT_all", [128, 4, NT], BF16)
                for r_ in range(4):
                    S.dma('sp', ckvnT_all[:, r_, :], xout[r_ * 160:r_ * 160 + 128, :], reads=['xout'], writes=[('ckvnT_all', r_)])
                wuq = sb("wuq", [128, 2, 768], BF16)
                wukv = sb("wukv", [128, 1024], BF16)
                qa_tok = [sb("qa_tok%d" % i, [128, 8, 128], BF16) for i in range(2)]
                qs = sb("qs", [128, 768], F32)
                tmps = [sb("rt%d" % i, [128, 256], F32) for i in range(4)]
                kaug = [sb("kaug%d" % i, [128, 512], BF16) for i in range(3)]
                vext = [[sb("vext%d_%d" % (par, i), [128, 4, 128], BF16) for i in range(3)] for par in range(2)]
                PT = [sb("PT%d" % i, [128, 512], BF16) for i in range(4)]
                accs = [sb("accs%d" % i, [128, 512], F32) for i in range(4)]
                lnd = sb("lnd", [1, 512], F32); rden = sb("rden", [1, 512], F32)
                bcs = sb("bcs", [128, 512], F32)
                S.dma('pool', wuq[:], w_uq.rearrange("(kc p) n -> p kc n", p=128), writes=['wuq'])
                S.dma('pool', wukv[:], w_ukv, writes=['wukv'])
                for par in range(2):
                    for i in range(3):
                        S.op('dve', lambda e, par=par, i=i: e.memset(vext[par][i][:], 0.0), writes=[('vext', par, i)])
                        col = 64 if par == 0 else 0
                        S.op('dve', lambda e, par=par, i=i, col=col: e.memset(vext[par][i][:, :, col:col + 1], 1.0),
                             reads=[('vext', par, i)], writes=[('vext', par, i)])
                for i_ in range(2):
                    S.op('dve', lambda e, i_=i_: e.memset(qa_tok[i_][:], 0.0),
                         writes=[('qa_n', i_, 0), ('qa_n', i_, 1), ('qa_p', i_, 0), ('qa_p', i_, 1)])
                if tsub == 39:
                    S.barrier(); raise _Stop()
                for tt in range(NTT):
                    b = tt % 2
                    for (bk, c0, c1) in ((0, 0, 480), (1, 480, 768)):
                        for kc in range(2):
                            S.op('pe', lambda e, kc=kc, tt=tt, bk=bk, c0=c0, c1=c1: e.matmul(
                                bank[bk][:, 0:c1 - c0], lhsT=cqnT[:, kc, tt * 128:(tt + 1) * 128], rhs=wuq[:, kc, c0:c1],
                                start=(kc == 0), stop=(kc == 1)), reads=[('cqnT', tt), 'wuq'], writes=[('bank', bk)], inc=(kc == 1))
                    if tsub == 401 and tt == 0:
                        S.barrier(); raise _Stop()
                    S.op('act', lambda e: e.activation(out=qs[:, 0:480], in_=bank[0][:, 0:480], func=AF.Copy),
                         reads=[('bank', 0)], writes=['qs0'])
                    S.op('act', lambda e: e.activation(out=qs[:, 480:768], in_=bank[1][:, 0:288], func=AF.Copy),
                         reads=[('bank', 1)], writes=['qs1'])
                    qs3 = qs[:, :].rearrange("p (a d) -> p a d", a=8)
                    S.op('act', lambda e, b=b, qs3=qs3: e.activation(out=qa_tok[b][:, :, 0:64], in_=qs3[:, :, 0:64], func=AF.Copy),
                         reads=['qs0', 'qs1'], writes=[('qa_n', b, 0), ('qa_n', b, 1)])
                    x4 = qs3[:, :, 64:96].rearrange("p a (two j) -> p a two j", two=2)
                    o4 = qa_tok[b][:, :, 64:96].rearrange("p a (two j) -> p a two j", two=2)
                    rope_tok(x4, o4, cm3[:, tt, :], sm3[:, tt, :], 8, 16, tmps, (['qs0', 'qs1'], [('qa_p', b, 0), ('qa_p', b, 1)]))
                    if tsub == 402 and tt == 0:
                        S.barrier(); raise _Stop()
                    pb = 6 + b
                    pv = bankbf[pb]
                    for hh in range(8):
                        S.op('pe', lambda e, hh=hh, b=b, pv=pv: e.transpose(
                            out=pv[:, hh * 128:(hh + 1) * 128], in_=qa_tok[b][:, hh, :], identity=ident[:]),
                            reads=[('qa_n', b, 0), ('qa_n', b, 1), ('qa_p', b, 0), ('qa_p', b, 1), 'ident'],
                            writes=[('bank', pb)], inc=(hh == 7))
                    S.op('act', lambda e, tt=tt, pv=pv: e.activation(
                        out=qaugT[:, :, tt * 128:(tt + 1) * 128],
                        in_=pv[:, 0:1024].rearrange("p (a t) -> p a t", a=8), func=AF.Copy),
                        reads=[('bank', pb)], writes=[('qaug', tt)])

                if tsub == 40:
                    S.barrier(); raise _Stop()
                xo = xout.ap()

                def prodK(gp):
                    hh, p = gp // 16, gp % 16
                    r_, t0 = p // 4, (p % 4) * 512
                    sl = gp % 3
                    S.op('pe', lambda e: e.matmul(bank[7][0:64, :], lhsT=wukv[:, hh * 128:hh * 128 + 64],
                                                  rhs=ckvnT_all[:, r_, t0:t0 + 512], start=True, stop=True),
                         reads=['wukv', ('ckvnT_all', r_)], writes=[('bank', 7)])
                    S.op('dve', lambda e: e.tensor_copy(out=kaug[sl][0:64, :], in_=bank[7][0:64, :]),
                         reads=[('bank', 7)], writes=[('kaug_n', sl)])
                    S.dma('sp', kaug[sl][64:96, :], xo[r_ * 160 + 128:r_ * 160 + 160, t0:t0 + 512],
                          reads=['xout'], writes=[('kaug_p', sl)])

                def prodV(gp):
                    hh, p = gp // 16, gp % 16
                    r_, t0 = p // 4, (p % 4) * 512
                    sl = gp % 3
                    par = hh % 2
                    for j in range(4):
                        S.op('pe', lambda e, j=j: e.matmul(bank[7][:, j * 64:(j + 1) * 64],
                                                           lhsT=ckvnT_all[:, r_, t0 + j * 128:t0 + (j + 1) * 128],
                                                           rhs=wukv[:, hh * 128 + 64:hh * 128 + 128], start=True, stop=True),
                             reads=['wukv', ('ckvnT_all', r_)], writes=[('bank', 7)], inc=(j == 3))
                    off = 0 if par == 0 else 64
                    S.op('dve', lambda e: e.tensor_copy(out=vext[par][sl][:, :, off:off + 64],
                                                        in_=bank[7][:, 0:256].rearrange("p (a d) -> p a d", a=4)),
                         reads=[('bank', 7)], writes=[('vext', par, sl)])

                steps = [(gp, kt, qg) for gp in range(8 * 16) for kt in range(4) for qg in range(4)]
                NS = len(steps)

                def emit_S(i):
                    gp, kt, qg = steps[i]
                    hh = gp // 16
                    sl = gp % 3
                    sbk = 4 + (i % 3)
                    S.op('pe', lambda e: e.matmul(bank[sbk][:], lhsT=kaug[sl][0:96, kt * 128:(kt + 1) * 128],
                                                  rhs=qaugT[0:96, hh, qg * 512:(qg + 1) * 512], start=True, stop=True),
                         reads=[('kaug_n', sl), ('kaug_p', sl)] + [('qaug', qg * 4 + j) for j in range(4)], writes=[('bank', sbk)])
                    S.op('act', lambda e: e.activation(out=PT[i % 4][:], in_=bank[sbk][:], func=AF.Exp, scale=SCALE),
                         reads=[('bank', sbk)], writes=[('PT', i % 4)])

                def emit_PV(i):
                    gp, kt, qg = steps[i]
                    hh, p = gp // 16, gp % 16
                    sl = gp % 3
                    par = hh % 2
                    first = (p == 0 and kt == 0)
                    last = (p == 15 and kt == 3)
                    S.op('pe', lambda e: e.matmul(bank[qg][:], lhsT=vext[par][sl][:, kt, :], rhs=PT[i % 4][:],
                                                  start=first, stop=last),
                         reads=[('vext', par, sl), ('PT', i % 4)], writes=[('bank', qg)], inc=last)
                    if last:
                        S.op('dve', lambda e: e.tensor_copy(out=accs[qg][:], in_=bank[qg][:]),
                             reads=[('bank', qg)], writes=[('accs', qg)])

                def finish_head(hh):
                    par = hh % 2
                    drow = 64 if par == 0 else 0
                    o0 = 0 if par == 0 else 64
                    for qg in range(4):
                        S.op('act', lambda e, qg=qg: e.activation(out=lnd[0:1, :], in_=accs[qg][drow:drow + 1, :], func=AF.Ln),
                             reads=[('accs', qg)], writes=['lnd'])
                        S.op('act', lambda e: e.activation(out=rden[0:1, :], in_=lnd[0:1, :], func=AF.Exp, scale=-1.0),
                             reads=['lnd'], writes=['rden'])
                        S.op('pe', lambda e: e.matmul(bank[7][:], lhsT=ones32[0:1, :], rhs=rden[0:1, :], start=True, stop=True),
                             reads=['ones32', 'rden'], writes=[('bank', 7)])
                        S.op('dve', lambda e: e.tensor_copy(out=bcs[:], in_=bank[7][:]), reads=[('bank', 7)], writes=['bcs'])
                        S.op('dve', lambda e, qg=qg: e.tensor_tensor(
                            out=XT[o0:o0 + 64, hh // 2, qg * 512:(qg + 1) * 512], in0=accs[qg][o0:o0 + 64, :],
                            in1=bcs[o0:o0 + 64, :], op=ALU.mult),
                            reads=[('accs', qg), 'bcs'], writes=[('XTa', hh // 2, qg)])

                prodK(0)
                prodV(0)
                emit_S(0)
                emit_S(1)
                for i in range(NS):
                    gp, kt, qg = steps[i]
                    if qg == 0 and gp + 1 < 128:
                        if kt == 0:
                            prodK(gp + 1)
                        elif kt == 2:
                            prodV(gp + 1)
                    if i + 2 < NS:
                        emit_S(i + 2)
                    emit_PV(i)
                    if gp % 16 == 15 and kt == 3 and qg == 3:
                        finish_head(gp // 16)
                if stop == 'attn':
                    S.dma('sp', dbgbf[:, 0:8192], XT[:, 0:4, :].rearrange("p a t -> p (a t)"),
                          reads=[('XTa', a, q_) for a in range(4) for q_ in range(4)], writes=['dbg_a'])
                    out_keys.extend(['dbg_a'])
                S.barrier()
                if stop == 'attn':
                    raise _Stop()

            with ExitStack() as es:
                sb = lambda n, s, d: es.enter_context(nc.sbuf_tensor("wo_" + n, s, d))
                wo = sb("wo", [128, 8, D], BF16)
                S.dma('pool', wo[:], w_o.rearrange("(kc p) n -> p kc n", p=128), writes=['wo'])
                st = 0
                for tt in range(NTT):
                    for half in range(2):
                        bd = st % 2
                        for k in range(8):
                            lhs = XT[:, k, tt * 128:(tt + 1) * 128]
                            rd = [('XTa', k, tt // 4)] if k < 4 else [('XT', tt)]
                            S.op('pe', lambda e, k=k, lhs=lhs, half=half, bd=bd: e.matmul(
                                bank[bd][:], lhsT=lhs, rhs=wo[:, k, half * 512:(half + 1) * 512], start=(k == 0), stop=(k == 7)),
                                reads=rd + ['wo'], writes=[('bank', bd)], inc=(k == 7))
                        S.op('dve', lambda e, tt=tt, half=half, bd=bd: e.tensor_tensor(
                            out=h[:, tt, half * 512:(half + 1) * 512], in0=bank[bd][:], in1=h[:, tt, half * 512:(half + 1) * 512],
                            op=ALU.add), reads=[('bank', bd), ('h', tt)], writes=[('h', tt)])
                        st += 1
                S.barrier()

    rope_tables()
    load_x()
    if skip_ffn and _os.environ.get("PRE_ACT"):
        fn = getattr(AF, _os.environ["PRE_ACT"])
        S.op('act', lambda e: e.activation(out=ss[:], in_=ss[:], func=fn), writes=['pre'])
        S.barrier()
    if not skip_ffn:
        ffn(g_ffn1, wg1, wu1, wd1, "f1_")
    if stop == 'ffn1':
        store_h_raw()
        S.wait_all('sp', out_keys)
    else:
        if not skip_mid:
            try:
                middle()
            except _Stop:
                pass
        if stop is not None:
            store_h_raw()
            S.wait_all('sp', out_keys)
        else:
            ffn(g_ffn2, wg2, wu2, wd2, "f2_")
            final_norm_store()
    print("instr", S.ninstr, "waits", S.nwaits, "cnt", S.cnt)
    return nc


_NC_CACHE = {}


def _const_tables():
    ct = np.zeros((128, NCT), np.float32)
    j = np.arange(128, dtype=np.float64)[:, None]
    i = np.arange(128, dtype=np.float64)[None, :]
    ct[:, 0:128] = np.maximum(i - j, 0.0)
    ct[:, 128:256] = np.maximum(j - i, 0.0)
    ct[0:64, 256:384] = i + 1.0
    ct[64:128, 256:384] = 128.0 - i
    ct[:, 384:392] = 127.0 - j
    ct[:, 392:400] = j
    c = np.arange(16, dtype=np.float64)[None, :]
    ct[0:64, 400:416] = 128.0 * c
    ct[64:128, 400:416] = 128.0 * (15.0 - c)
    ct[:, 416:448] = (np.float32(10000.0) ** (-np.arange(32, dtype=np.float32) / np.float32(32)))[None, :]
    ct[:, 448:464] = (np.float32(10000.0) ** (-np.arange(16, dtype=np.float32) / np.float32(16)))[None, :]
    return ct


def _geo(r):
    g = np.zeros((128, 8), np.float32)
    for rp in range(4):
        if rp < r:
            g[0:64, rp] = r - 1 - rp
            g[0:64, 4 + rp] = 1.0
        if rp > r:
            g[64:128, rp] = rp - r - 1
            g[64:128, 4 + rp] = 1.0
    return g


def _core_inputs(inputs):
    f32 = lambda a: np.ascontiguousarray(np.asarray(a, dtype=np.float32))
    x = f32(inputs["x"])
    pos = np.asarray(inputs["positions"]).astype(np.int32)
    common = {
        "g_mix": f32(inputs["mix_norm"]).reshape(1, D), "w_in": f32(inputs["w_in"])[0],
        "g_q": f32(inputs["q_norm"]).reshape(1, 256), "w_uq": f32(inputs["w_uq"])[0],
        "g_kv": f32(inputs["kv_norm"]).reshape(1, 128), "w_ukv": f32(inputs["w_ukv"])[0],
        "dec": np.ascontiguousarray(np.concatenate([f32(inputs["ret_decay_fwd"]).reshape(1, 8),
                                                    f32(inputs["ret_decay_bwd"]).reshape(1, 8)], axis=1)),
        "w_o": f32(inputs["w_o"])[0],
        "ctab": _const_tables(),
        "g_ffn1": f32(inputs["ffn1_norm"]).reshape(1, D),
        "wg1": f32(inputs["ffn1_w_gate"])[0], "wu1": f32(inputs["ffn1_w_up"])[0], "wd1": f32(inputs["ffn1_w_down"])[0],
        "g_ffn2": f32(inputs["ffn2_norm"]).reshape(1, D),
        "wg2": f32(inputs["ffn2_w_gate"])[0], "wu2": f32(inputs["ffn2_w_up"])[0], "wd2": f32(inputs["ffn2_w_down"])[0],
        "g_fin": f32(inputs["final_norm"]).reshape(1, D),
    }
    in_maps = []
    for c in range(NCORES):
        b, r = c // 4, c % 4
        m = dict(common)
        m["x"] = np.ascontiguousarray(x[b, r * NT:(r + 1) * NT, :])
        m["pos_tok"] = np.ascontiguousarray(pos[b, r * NT:(r + 1) * NT].reshape(NTT, 128).T)
        m["geo"] = _geo(r)
        in_maps.append(m)
    return in_maps


def run(inputs, stop=None, skip_mid=False, trace=False, skip_ffn=False, tsub=99):
    key = (stop, skip_mid, skip_ffn, tsub)
    if key not in _NC_CACHE:
        _NC_CACHE[key] = build(stop=stop, skip_mid=skip_mid, skip_ffn=skip_ffn, tsub=tsub)
    nc = _NC_CACHE[key]
    in_maps = _core_inputs(inputs)
    res = run_bass_kernel_spmd(nc, in_maps, core_ids=list(range(NCORES)), **({"trace": True} if trace else {}))
    out = np.empty((2, 8192, D), np.float32)
    for c in range(NCORES):
        b, r = c // 4, c % 4
        out[b, r * NT:(r + 1) * NT, :] = res.results[c]["out"]
    return out, res


def kernel(**inputs):
    out, _ = run(inputs)
    return out
```

```python
import math
from contextlib import ExitStack
import numpy as np
import concourse.bass as bass
import concourse.mybir as mybir
from concourse.bass_utils import run_bass_kernel_spmd

F32 = mybir.dt.float32
BF16 = mybir.dt.bfloat16
I32 = mybir.dt.int32
ALU = mybir.AluOpType
AF = mybir.ActivationFunctionType
AX = mybir.AxisListType

NCORES = 8
NT = 2048
NTT = 16
D = 1024
DFF = 2816
NFC = 22
EPS = 1e-6
FFN_PARTS = [(0, 6), (6, 6), (12, 6), (18, 4)]


class Sched:
    def __init__(self, nc, n_dma=20):
        self.nc = nc
        self.eng = {'pe': nc.tensor, 'act': nc.scalar, 'dve': nc.vector,
                    'pool': nc.gpsimd, 'sp': nc.sync}
        self.semh = {}
        self.cnt = {}
        for k in ('pe', 'act', 'dve', 'pool'):
            self.semh[k] = nc.alloc_semaphore("s_" + k)
            self.cnt[k] = 0
        self.waited = {k: {} for k in self.eng}
        self.res = {}
        self.ring = {}
        self.ring_i = {}
        self.uses = {}
        for q in ('sp', 'pool'):
            keys = []
            for i in range(n_dma):
                key = "d_%s%d" % (q, i)
                self.semh[key] = nc.alloc_semaphore(key)
                self.uses[key] = 0
                keys.append(key)
            self.ring[q] = keys
            self.ring_i[q] = 0
        self.nwaits = 0
        self.ninstr = 0
        self.cc_dummy = nc.alloc_sbuf_tensor("cc_dummy", [128, 8], F32)

    def _deps(self, reads, writes):
        deps = []
        for r in reads:
            e = self.res.get(r)
            if e is not None and e[0] is not None:
                deps.append(e[0])
        for w in writes:
            e = self.res.get(w)
            if e is not None:
                if e[0] is not None:
                    deps.append(e[0])
                deps.extend(e[1].values())
        return deps

    def _record(self, tok, reads, writes, rkey):
        for r in reads:
            e = self.res.setdefault(r, [None, {}])
            e[1][rkey] = tok
        for w in writes:
            self.res[w] = [tok, {}]

    def _wait(self, q, tok, skip_self_pe=True):
        key, val = tok
        if q == 'pe' and key == 'pe' and skip_self_pe:
            return
        if self.waited[q].get(key, 0) >= val:
            return
        self.eng[q].wait_ge(self.semh[key], val)
        self.waited[q][key] = val
        self.nwaits += 1

    def op(self, q, fn, reads=(), writes=(), inc=True):
        for d in self._deps(reads, writes):
            self._wait(q, d)
        ins = fn(self.eng[q])
        self.ninstr += 1
        if inc:
            self.cnt[q] += 1
            ins.then_inc(self.semh[q], 1)
            tok = (q, self.cnt[q])
        else:
            tok = (q, self.cnt[q] + 1)
        self._record(tok, reads, writes, q)
        return tok

    def dma(self, q, out, in_, reads=(), writes=(), **kw):
        ring = self.ring[q]
        key = ring[self.ring_i[q] % len(ring)]
        self.ring_i[q] += 1
        use = self.uses[key]
        if use > 0:
            self._wait(q, (key, 16 * use))
        for d in self._deps(reads, writes):
            self._wait(q, d)
        self.eng[q].dma_start(out=out, in_=in_, **kw).then_inc(self.semh[key], 16)
        self.ninstr += 1
        self.uses[key] = use + 1
        tok = (key, 16 * (use + 1))
        self._record(tok, reads, writes, key)
        return tok

    def collective(self, kind, ins_ap, outs_ap, groups, reads, writes, name):
        key = "cc_" + name
        self.semh[key] = self.nc.alloc_semaphore(key)
        for d in self._deps(reads, writes):
            self._wait('pool', d)
        self.nc.gpsimd.collective_compute(kind, ALU.bypass, replica_groups=groups,
                                          ins=[ins_ap], outs=[outs_ap]).then_inc(self.semh[key])
        self.nc.gpsimd.wait_ge(self.semh[key], 1)
        self.ninstr += 1
        return self.op('pool', lambda e: e.memset(self.cc_dummy[:], 0.0), reads=reads, writes=writes)

    def barrier(self):
        toks = [(k, self.cnt[k]) for k in ('pe', 'act', 'dve', 'pool') if self.cnt[k] > 0]
        toks += [(key, 16 * u) for key, u in self.uses.items() if u > 0]
        for q in self.eng:
            for t in toks:
                self._wait(q, t, skip_self_pe=False)

    def wait_all(self, q, keys):
        for k in keys:
            e = self.res.get(k)
            if e is not None:
                if e[0] is not None:
                    self._wait(q, e[0], skip_self_pe=False)
                for t in e[1].values():
                    self._wait(q, t, skip_self_pe=False)


NCT = 464
SCALE = 96 ** -0.5
MAGIC = 12582912.0
C1 = 6.28125
C2 = 2.0 * math.pi - 6.28125
PI_S = 3.141592
LN8 = math.log(0.125)


class _Stop(Exception):
    pass


def build(stop=None, skip_mid=False, skip_ffn=False, tsub=99):
    import os as _os
    nc = bass.Bass("TRN2", target_bir_lowering=False)

    def din(name, shape, dtype=F32):
        return nc.dram_tensor(name, list(shape), dtype, kind="ExternalInput").ap()

    x_d = din("x", [NT, D])
    g_ffn1 = din("g_ffn1", [1, D]); wg1 = din("wg1", [D, DFF]); wu1 = din("wu1", [D, DFF]); wd1 = din("wd1", [DFF, D])
    g_ffn2 = din("g_ffn2", [1, D]); wg2 = din("wg2", [D, DFF]); wu2 = din("wu2", [D, DFF]); wd2 = din("wd2", [DFF, D])
    g_fin = din("g_fin", [1, D])
    g_mix = din("g_mix", [1, D]); w_in = din("w_in", [D, 2464])
    g_q = din("g_q", [1, 256]); w_uq = din("w_uq", [256, 768])
    g_kv = din("g_kv", [1, 128]); w_ukv = din("w_ukv", [128, 1024])
    dec_d = din("dec", [1, 16]); w_o = din("w_o", [D, D])
    pos_d = din("pos_tok", [128, NTT], I32)
    ctab_d = din("ctab", [128, NCT]); geo_d = din("geo", [128, 8])
    out_d = nc.dram_tensor("out", [NT, D], F32, kind="ExternalOutput").ap()
    if stop is not None:
        dbg32 = nc.dram_tensor("dbg32", [128, 2048], F32, kind="ExternalOutput").ap()
        dbgbf = nc.dram_tensor("dbgbf", [128, 8192], BF16, kind="ExternalOutput").ap()
        dbgbf2 = nc.dram_tensor("dbgbf2", [128, 2048], BF16, kind="ExternalOutput").ap()
    xin = nc.dram_tensor("xin", [160, NT], BF16)
    xout = nc.dram_tensor("xout", [640, NT], BF16)
    xin2 = nc.dram_tensor("xin2", [128, 512], F32)
    xout2 = nc.dram_tensor("xout2", [512, 512], F32)
    GROUPS = [[0, 1, 2, 3], [4, 5, 6, 7]]

    S = Sched(nc)
    ps = nc.alloc_psum_tensor

    h = nc.alloc_sbuf_tensor("h", [128, NTT, D], F32)
    XT = nc.alloc_sbuf_tensor("XT", [128, 8, NT], BF16)
    ident = nc.alloc_sbuf_tensor("ident", [128, 128], BF16)
    ss = nc.alloc_sbuf_tensor("ss", [128, NTT], F32)
    var = nc.alloc_sbuf_tensor("var", [128, NTT], F32)
    sd = nc.alloc_sbuf_tensor("sd", [128, NTT], F32)
    rstd = nc.alloc_sbuf_tensor("rstd", [128, NTT], F32)

    ctab = nc.alloc_sbuf_tensor("ctab_sb", [128, NCT], F32)
    cr = nc.alloc_sbuf_tensor("cr_sb", [128, 512], F32); sr = nc.alloc_sbuf_tensor("sr_sb", [128, 512], F32)
    cm = nc.alloc_sbuf_tensor("cm_sb", [128, 256], F32); sm = nc.alloc_sbuf_tensor("sm_sb", [128, 256], F32)
    INVR = ctab[:, 416:448]; INVM = ctab[:, 448:464]

    bank = [ps("bank%d" % i, [128, 512], F32) for i in range(8)]
    bankbf = [b.bitcast(BF16) for b in bank]

    S.op('dve', lambda e: e.memset(ident[:], 1.0), writes=['ident'])
    S.op('pool', lambda e: e.affine_select(out=ident[:], in_=ident[:], pattern=[[-1, 128]],
                                           compare_op=ALU.is_equal, fill=0.0, base=0, channel_multiplier=1),
         reads=['ident'], writes=['ident'])

    out_keys = []
    ov = out_d.rearrange("(t p) d -> p t d", p=128)

    def load_x():
        xv = x_d.rearrange("(t p) d -> p t d", p=128)
        for tt in range(NTT):
            S.dma('sp', h[:, tt, :], xv[:, tt, :], writes=[('h', tt)])

    def rstd_batch(ss_t, n, inv_n, var_t, sd_t, rstd_t, key):
        S.op('dve', lambda e: e.tensor_scalar(out=var_t[:], in0=ss_t[:], scalar1=inv_n, scalar2=EPS,
                                              op0=ALU.mult, op1=ALU.add),
             reads=[(key, t) for t in range(n)], writes=[key + '_var'])
        S.op('act', lambda e: e.activation(out=sd_t[:], in_=var_t[:], func=AF.Sqrt),
             reads=[key + '_var'], writes=[key + '_sd'])
        S.op('dve', lambda e: e.reciprocal(out=rstd_t[:], in_=sd_t[:]), reads=[key + '_sd'], writes=[key + '_rstd'])

    def norm_stats(gain_d, gbc, junk):
        S.dma('sp', gbc[:], gain_d[0, :].partition_broadcast(128), writes=['gbc'])
        for tt in range(NTT):
            S.op('dve', lambda e, tt=tt: e.scalar_tensor_tensor(
                out=junk[:], in0=h[:, tt, :], scalar=1.0, in1=h[:, tt, :],
                op0=ALU.mult, op1=ALU.mult, accum_out=ss[:, tt:tt + 1]),
                reads=[('h', tt)], writes=[('ss', tt)])
        rstd_batch(ss, NTT, 1.0 / D, var, sd, rstd, 'ss')

    def norm_to_XT(gain_d, gbc, junk, xn):
        norm_stats(gain_d, gbc, junk)
        for tt in range(NTT):
            b = tt % 2
            S.op('dve', lambda e, tt=tt, b=b: e.scalar_tensor_tensor(
                out=xn[b][:], in0=h[:, tt, :], scalar=rstd[:, tt:tt + 1], in1=gbc[:],
                op0=ALU.mult, op1=ALU.mult),
                reads=[('h', tt), 'ss_rstd', 'gbc'], writes=[('xn', b)])
            pb = 6 + b
            pv = bankbf[pb]
            for k in range(8):
                S.op('pe', lambda e, k=k, b=b, pv=pv: e.transpose(
                    out=pv[:, k * 128:(k + 1) * 128], in_=xn[b][:, k * 128:(k + 1) * 128], identity=ident[:]),
                    reads=[('xn', b), 'ident'], writes=[('bank', pb)], inc=(k == 7))
            S.op('act', lambda e, tt=tt, pv=pv: e.activation(
                out=XT[:, :, tt * 128:(tt + 1) * 128],
                in_=pv[:, 0:1024].rearrange("p (k t) -> p k t", k=8), func=AF.Copy),
                reads=[('bank', pb)], writes=[('XT', tt)])

    def rope_tables():
        S.dma('sp', ctab[:], ctab_d, writes=['ctab'])
        with ExitStack() as es:
            sb = lambda n, s, d: es.enter_context(nc.sbuf_tensor("t_" + n, s, d))
            pos_i = sb("pos_i", [128, NTT], I32); pos_f = sb("pos_f", [128, NTT], F32)
            ang = sb("ang", [128, 512], F32)
            ta = sb("ta", [128, 512], F32); tu = sb("tu", [128, 512], F32); tr = sb("tr", [128, 512], F32)
            S.dma('sp', pos_i[:], pos_d, writes=['pos_i'])
            S.op('dve', lambda e: e.tensor_copy(out=pos_f[:], in_=pos_i[:]), reads=['pos_i'], writes=['pos_f'])

            def sincos(n, nf, inv_ap, o_sin, o_cos, key):
                a3 = ang[:, 0:n].rearrange("p (a b) -> p a b", a=NTT)
                S.op('dve', lambda e: e.tensor_tensor(
                    out=a3, in0=pos_f[:, :].unsqueeze(2).to_broadcast([128, NTT, nf]),
                    in1=inv_ap.unsqueeze(1).to_broadcast([128, NTT, nf]), op=ALU.mult),
                    reads=['pos_f', 'ctab', 'tu'], writes=['ang'])
                for dst, off, dk in ((o_sin, 0.0, key + 's'), (o_cos, math.pi / 2, key + 'c')):
                    if off != 0.0:
                        S.op('dve', lambda e, off=off: e.tensor_scalar(out=ta[:, 0:n], in0=ang[:, 0:n], scalar1=off,
                                                                      scalar2=None, op0=ALU.add),
                             reads=['ang'], writes=['ta'])
                        src, sk = ta, 'ta'
                    else:
                        src, sk = ang, 'ang'
                    S.op('dve', lambda e, src=src: e.tensor_scalar(out=tu[:, 0:n], in0=src[:, 0:n],
                                                                  scalar1=1.0 / (2 * math.pi), scalar2=MAGIC,
                                                                  op0=ALU.mult, op1=ALU.add),
                         reads=[sk], writes=['tu'])
                    S.op('dve', lambda e: e.tensor_scalar(out=tu[:, 0:n], in0=tu[:, 0:n], scalar1=MAGIC, scalar2=None,
                                                          op0=ALU.subtract), reads=['tu'], writes=['tu'])
                    S.op('dve', lambda e, src=src: e.scalar_tensor_tensor(out=tr[:, 0:n], in0=tu[:, 0:n], scalar=-C1,
                                                                         in1=src[:, 0:n], op0=ALU.mult, op1=ALU.add),
                         reads=['tu', sk], writes=['tr'])
                    S.op('dve', lambda e: e.scalar_tensor_tensor(out=tr[:, 0:n], in0=tu[:, 0:n], scalar=-C2,
                                                                in1=tr[:, 0:n], op0=ALU.mult, op1=ALU.add),
                         reads=['tu', 'tr'], writes=['tr'])
                    S.op('dve', lambda e: e.tensor_scalar(out=tr[:, 0:n], in0=tr[:, 0:n], scalar1=-PI_S, scalar2=PI_S,
                                                          op0=ALU.max, op1=ALU.min), reads=['tr'], writes=['tr'])
                    S.op('act', lambda e, dst=dst: e.activation(out=dst[:, 0:n], in_=tr[:, 0:n], func=getattr(AF, _os.environ.get('SINFN', 'Sin'))),
                         reads=['tr'], writes=[dk])

            if tsub >= 1:
                sincos(512, 32, INVR, sr, cr, 'ropeR')
            if tsub >= 2:
                sincos(256, 16, INVM, sm, cm, 'ropeM')

            S.barrier()

    def ffn(gain_d, wg_d, wu_d, wd_d, tag):
        with ExitStack() as es:
            sb = lambda n, s, d: es.enter_context(nc.sbuf_tensor(tag + n, s, d))
            gbc = sb("gbc", [128, D], F32)
            junk = sb("junk", [128, D], BF16)
            xn = [sb("xn%d" % i, [128, D], BF16) for i in range(2)]
            wgb = [sb("wgb%d" % i, [128, 8, 256], BF16) for i in range(3)]
            wub = [sb("wub%d" % i, [128, 8, 256], BF16) for i in range(3)]
            wdb = [sb("wdb%d" % i, [128, 6, D], BF16) for i in range(2)]
            actT = sb("actT", [128, 6, NT], BF16)
            sil = [sb("sil%d" % i, [128, 512], F32) for i in range(2)]
            wgv = wg_d.rearrange("(kc p) f -> p kc f", p=128)
            wuv = wu_d.rearrange("(kc p) f -> p kc f", p=128)
            wdv = wd_d.rearrange("(fc p) d -> p fc d", p=128)

            def load_gran(g):
                s = g % 3
                S.dma('pool', wgb[s][:], wgv[:, :, g * 256:(g + 1) * 256], writes=[('wg', s)])
                S.dma('pool', wub[s][:], wuv[:, :, g * 256:(g + 1) * 256], writes=[('wu', s)])

            def load_wd(p):
                a, n = FFN_PARTS[p]
                s = p % 2
                S.dma('pool', wdb[s][:, 0:n, :], wdv[:, a:a + n, :], writes=[('wd', s)])

            load_gran(0); load_gran(1); load_gran(2)
            load_wd(0); load_wd(1)
            norm_to_XT(gain_d, gbc, junk, xn)
            step = 0
            dstep = 0
            for p, (a, n) in enumerate(FFN_PARTS):
                for g in range(a // 2, (a + n) // 2):
                    s = g % 3
                    for fl in range(2):
                        fcp = (g * 2 + fl) - a
                        for tg in range(4):
                            bg = (step % 2) * 2
                            bu = bg + 1
                            xr = [('XT', tg * 4 + i) for i in range(4)]
                            for k in range(8):
                                S.op('pe', lambda e, k=k, s=s, fl=fl, tg=tg, bg=bg: e.matmul(
                                    bank[bg][:], lhsT=wgb[s][:, k, fl * 128:(fl + 1) * 128],
                                    rhs=XT[:, k, tg * 512:(tg + 1) * 512], start=(k == 0), stop=(k == 7)),
                                    reads=[('wg', s)] + xr, writes=[('bank', bg)], inc=(k == 7))
                            for k in range(8):
                                S.op('pe', lambda e, k=k, s=s, fl=fl, tg=tg, bu=bu: e.matmul(
                                    bank[bu][:], lhsT=wub[s][:, k, fl * 128:(fl + 1) * 128],
                                    rhs=XT[:, k, tg * 512:(tg + 1) * 512], start=(k == 0), stop=(k == 7)),
                                    reads=[('wu', s)] + xr, writes=[('bank', bu)], inc=(k == 7))
                            sl = step % 2
                            S.op('act', lambda e, sl=sl, bg=bg: e.activation(out=sil[sl][:], in_=bank[bg][:], func=AF.Silu),
                                 reads=[('bank', bg)], writes=[('sil', sl)])
                            S.op('dve', lambda e, sl=sl, bu=bu, fcp=fcp, tg=tg: e.tensor_tensor(
                                out=actT[:, fcp, tg * 512:(tg + 1) * 512], in0=sil[sl][:], in1=bank[bu][:], op=ALU.mult),
                                reads=[('sil', sl), ('bank', bu)], writes=[('actT', fcp, tg)])
                            step += 1
                    if g + 3 < NFC // 2:
                        load_gran(g + 3)
                s = p % 2
                for tt in range(NTT):
                    for half in range(2):
                        bd = 4 + (dstep % 2)
                        for f in range(n):
                            S.op('pe', lambda e, f=f, tt=tt, half=half, bd=bd, s=s: e.matmul(
                                bank[bd][:], lhsT=actT[:, f, tt * 128:(tt + 1) * 128],
                                rhs=wdb[s][:, f, half * 512:(half + 1) * 512], start=(f == 0), stop=(f == n - 1)),
                                reads=[('actT', f, tt // 4), ('wd', s)], writes=[('bank', bd)], inc=(f == n - 1))
                        S.op('dve', lambda e, tt=tt, half=half, bd=bd: e.scalar_tensor_tensor(
                            out=h[:, tt, half * 512:(half + 1) * 512], in0=bank[bd][:], scalar=0.5,
                            in1=h[:, tt, half * 512:(half + 1) * 512], op0=ALU.mult, op1=ALU.add),
                            reads=[('bank', bd), ('h', tt)], writes=[('h', tt)])
                        dstep += 1
                if p + 2 < len(FFN_PARTS):
                    load_wd(p + 2)
            S.barrier()

    def store_h_raw():
        for tt in range(NTT):
            S.dma('sp', ov[:, tt, :], h[:, tt, :], reads=[('h', tt)], writes=[('out', tt)])
            out_keys.append(('out', tt))

    def final_norm_store():
        with ExitStack() as es:
            sb = lambda n, s, d: es.enter_context(nc.sbuf_tensor("fin_" + n, s, d))
            gbc = sb("gbc", [128, D], F32)
            junk = sb("junk", [128, D], BF16)
            obuf = [sb("obuf%d" % i, [128, D], F32) for i in range(2)]
            norm_stats(g_fin, gbc, junk)
            for tt in range(NTT):
                b = tt % 2
                S.op('dve', lambda e, tt=tt, b=b: e.scalar_tensor_tensor(
                    out=obuf[b][:], in0=h[:, tt, :], scalar=rstd[:, tt:tt + 1], in1=gbc[:],
                    op0=ALU.mult, op1=ALU.mult),
                    reads=[('h', tt), 'ss_rstd', 'gbc'], writes=[('obuf', b)])
                S.dma('sp', ov[:, tt, :], obuf[b][:], reads=[('obuf', b)], writes=[('out', tt)])
                out_keys.append(('out', tt))
            S.wait_all('sp', out_keys)
            S.barrier()

    def middle():
        w_inv = w_in.rearrange("(kc p) n -> p kc n", p=128)
        with ExitStack() as esm:
            sbm = lambda n, s, d: esm.enter_context(nc.sbuf_tensor("m_" + n, s, d))
            geo = sbm("geo", [128, 8], F32)
            LGrow = sbm("LGrow", [128, 16], F32); LG = sbm("LG", [128, 8], F32)
            KD = sbm("KD", [128, 16], F32); gC = sbm("gC", [128, 8], F32)
            DCAR = sbm("DCAR", [128, 128], F32); coef = sbm("coef", [128, 32], F32)
            ones32 = sbm("ones32", [128, 128], F32)
            RF = ctab[:, 0:128]; RB = ctab[:, 128:256]; EI = ctab[:, 256:384]
            E2 = ctab[:, 384:400]; M2 = ctab[:, 400:416]
            cr3 = cr[:, :].rearrange("p (a b) -> p a b", a=NTT); sr3 = sr[:, :].rearrange("p (a b) -> p a b", a=NTT)
            cm3 = cm[:, :].rearrange("p (a b) -> p a b", a=NTT); sm3 = sm[:, :].rearrange("p (a b) -> p a b", a=NTT)

            S.dma('sp', geo[:], geo_d, writes=['geo'])
            S.op('dve', lambda e: e.memset(ones32[:], 1.0), writes=['ones32'])

            with ExitStack() as es:
                sb = lambda n, s, d: es.enter_context(nc.sbuf_tensor("t_" + n, s, d))
                raw = sb("raw", [128, 16], F32); e1 = sb("e1", [128, 16], F32)
                t16 = sb("t16", [128, 16], F32);
                t8 = sb("t8", [128, 8], F32)
                if tsub >= 3:
                    S.dma('sp', raw[:], dec_d[0, :].partition_broadcast(128), writes=['raw'])
                    S.op('act', lambda e: e.activation(out=e1[:], in_=raw[:], func=AF.Exp, scale=-1.0), reads=['raw'], writes=['e1'])
                    S.op('act', lambda e: e.activation(out=e1[:], in_=e1[:], func=AF.Ln, bias=1.0), reads=['e1'], writes=['e1'])
                    S.op('dve', lambda e: e.tensor_scalar(out=LGrow[:], in0=e1[:], scalar1=-1.0, scalar2=None, op0=ALU.mult),
                         reads=['e1'], writes=['LGrow'])
                    S.op('dve', lambda e: e.tensor_copy(out=LG[0:64, :], in_=LGrow[0:64, 0:8]), reads=['LGrow'], writes=['LGa'])
                    S.op('dve', lambda e: e.tensor_copy(out=LG[64:128, :], in_=LGrow[64:128, 8:16]), reads=['LGrow'], writes=['LGb'])
                    LGk = ['LGa', 'LGb']
                if tsub >= 4:
                    S.op('dve', lambda e: e.tensor_tensor(out=t16[:], in0=LGrow[:], in1=E2, op=ALU.mult),
                         reads=['LGrow', 'ctab'], writes=['t16'])
                    S.op('act', lambda e: e.activation(out=KD[:], in_=t16[:], func=AF.Exp, bias=LN8), reads=['t16'], writes=['KD'])
                if tsub >= 5:
                    S.op('act', lambda e: e.activation(out=gC[:], in_=LG[:], func=AF.Exp, scale=128.0), reads=LGk, writes=['gC'])
                    d3 = DCAR[:, :].rearrange("p (c hh) -> p c hh", c=NTT)
                    S.op('dve', lambda e: e.tensor_tensor(out=d3, in0=M2.unsqueeze(2).to_broadcast([128, NTT, 8]),
                                                          in1=LG[:, :].unsqueeze(1).to_broadcast([128, NTT, 8]), op=ALU.mult),
                         reads=['ctab'] + LGk, writes=['DCAR'])
                    S.op('act', lambda e: e.activation(out=DCAR[:], in_=DCAR[:], func=AF.Exp), reads=['DCAR'], writes=['DCAR'])
                if tsub >= 6:
                    for r_ in range(4):
                        S.op('dve', lambda e, r_=r_: e.tensor_scalar(out=t8[:], in0=LG[:], scalar1=geo[:, r_:r_ + 1], scalar2=None,
                                                                     op0=ALU.mult), reads=LGk + ['geo'], writes=['t8'])
                        S.op('act', lambda e, r_=r_: e.activation(out=coef[:, r_ * 8:(r_ + 1) * 8], in_=t8[:], func=AF.Exp, scale=2048.0),
                             reads=['t8'], writes=[('coef', r_)])
                        S.op('dve', lambda e, r_=r_: e.tensor_scalar(out=coef[:, r_ * 8:(r_ + 1) * 8], in0=coef[:, r_ * 8:(r_ + 1) * 8],
                                                                     scalar1=geo[:, 4 + r_:5 + r_], scalar2=None, op0=ALU.mult),
                             reads=[('coef', r_), 'geo'], writes=[('coef', r_)])

                S.barrier()
                if stop == 'tables':
                    S.dma('sp', dbg32[:, 0:512], cr[:], reads=['ropeRc'], writes=['dbg_a'])
                    S.dma('sp', dbg32[:, 512:1024], sr[:], reads=['ropeRs'], writes=['dbg_b'])
                    S.dma('sp', dbg32[:, 1024:1280], cm[:], reads=['ropeMc'], writes=['dbg_c'])
                    S.dma('sp', dbg32[:, 1280:1536], sm[:], reads=['ropeMs'], writes=['dbg_d'])
                    S.dma('sp', dbg32[:, 1536:1552], KD[:], reads=['KD'], writes=['dbg_e'])
                    S.dma('sp', dbg32[:, 1552:1560], LG[:], reads=['LGa', 'LGb'], writes=['dbg_f'])
                    S.dma('sp', dbg32[:, 1560:1688], DCAR[:], reads=['DCAR'], writes=['dbg_g'])
                    S.dma('sp', dbg32[:, 1688:1720], coef[:], reads=[('coef', i) for i in range(4)], writes=['dbg_h'])
                    S.dma('sp', dbg32[:, 1720:1728], gC[:], reads=['gC'], writes=['dbg_i'])
                    out_keys.extend(['dbg_' + c for c in 'abcdefghi'])
                    raise _Stop()

            TAB = ['ropeRs', 'ropeRc', 'ropeMs', 'ropeMc']

            def rope_tok(x4, o4, c3t, s3t, nh, hf, tmps, rk):
                cb = c3t.unsqueeze(1).to_broadcast([128, nh, hf])
                sb_ = s3t.unsqueeze(1).to_broadcast([128, nh, hf])
                n = nh * hf
                tv = [t[:, 0:n].rearrange("p (a b) -> p a b", a=nh) for t in tmps]
                x1 = x4[:, :, 0, :]; x2 = x4[:, :, 1, :]
                rd, wr = rk
                S.op('dve', lambda e: e.tensor_tensor(out=tv[0], in0=x1, in1=cb, op=ALU.mult), reads=rd + TAB, writes=['rt0'])
                S.op('dve', lambda e: e.tensor_tensor(out=tv[1], in0=x2, in1=sb_, op=ALU.mult), reads=rd + TAB, writes=['rt1'])
                S.op('dve', lambda e: e.tensor_tensor(out=o4[:, :, 0, :], in0=tv[0], in1=tv[1], op=ALU.subtract),
                     reads=['rt0', 'rt1'], writes=wr)
                S.op('dve', lambda e: e.tensor_tensor(out=tv[2], in0=x1, in1=sb_, op=ALU.mult), reads=rd + TAB, writes=['rt2'])
                S.op('dve', lambda e: e.tensor_tensor(out=tv[3], in0=x2, in1=cb, op=ALU.mult), reads=rd + TAB, writes=['rt3'])
                S.op('dve', lambda e: e.tensor_tensor(out=o4[:, :, 1, :], in0=tv[2], in1=tv[3], op=ALU.add),
                     reads=['rt2', 'rt3'], writes=wr)

            cqnT = esm.enter_context(nc.sbuf_tensor("m_cqnT", [128, 2, NT], BF16))
            with ExitStack() as es:
                sb = lambda n, s, d: es.enter_context(nc.sbuf_tensor("p1_" + n, s, d))
                gbc = sb("gbc", [128, D], F32); junk = sb("junk", [128, D], BF16)
                xn = [sb("xn%d" % i, [128, D], BF16) for i in range(2)]
                W1 = sb("W1", [128, 8, 416], BF16)
                cqkv = sb("cqkv", [128, NTT, 416], F32)
                nrm = [sb("nrm%d" % i, [128, 512], BF16) for i in range(2)]
                ckvnT_loc = sb("ckvnT_loc", [128, NT], BF16)
                kpeT_loc = sb("kpeT_loc", [32, NT], BF16)
                gq = sb("gq", [128, 256], F32); gkv = sb("gkv", [128, 128], F32)
                kpe_r = sb("kpe_r", [128, NTT, 32], BF16)
                ssq = sb("ssq", [128, NTT], F32); sskv = sb("sskv", [128, NTT], F32)
                v1 = sb("v1", [128, NTT], F32); s1 = sb("s1", [128, NTT], F32)
                rq = sb("rq", [128, NTT], F32); rkv = sb("rkv", [128, NTT], F32)
                tmps = [sb("rt%d" % i, [128, 256], F32) for i in range(4)]
                for i_ in range(2):
                    S.op('dve', lambda e, i_=i_: e.memset(nrm[i_][:], 0.0), writes=[('nrmq', i_), ('nrmk', i_), ('nrmp', i_)])
                S.dma('pool', W1[:], w_inv[:, :, 0:416], writes=['W1'])
                S.dma('sp', gq[:], g_q[0, :].partition_broadcast(128), writes=['gq'])
                S.dma('sp', gkv[:], g_kv[0, :].partition_broadcast(128), writes=['gkv'])
                norm_to_XT(g_mix, gbc, junk, xn)
                for tt in range(NTT):
                    b = tt % 2
                    for k in range(8):
                        S.op('pe', lambda e, k=k, tt=tt, b=b: e.matmul(
                            bank[b][:, 0:416], lhsT=XT[:, k, tt * 128:(tt + 1) * 128], rhs=W1[:, k, :],
                            start=(k == 0), stop=(k == 7)),
                            reads=[('XT', tt), 'W1'], writes=[('bank', b)], inc=(k == 7))
                    S.op('act', lambda e, tt=tt, b=b: e.activation(out=cqkv[:, tt, :], in_=bank[b][:, 0:416], func=AF.Copy),
                         reads=[('bank', b)], writes=[('cqkv', tt)])
                    S.op('dve', lambda e, tt=tt: e.scalar_tensor_tensor(
                        out=junk[:, 0:256], in0=cqkv[:, tt, 0:256], scalar=1.0, in1=cqkv[:, tt, 0:256],
                        op0=ALU.mult, op1=ALU.mult, accum_out=ssq[:, tt:tt + 1]), reads=[('cqkv', tt)], writes=[('ssq', tt)])
                    S.op('dve', lambda e, tt=tt: e.scalar_tensor_tensor(
                        out=junk[:, 256:384], in0=cqkv[:, tt, 256:384], scalar=1.0, in1=cqkv[:, tt, 256:384],
                        op0=ALU.mult, op1=ALU.mult, accum_out=sskv[:, tt:tt + 1]), reads=[('cqkv', tt)], writes=[('sskv', tt)])
                if tsub == 10:
                    S.barrier(); raise _Stop()
                rstd_batch(ssq, NTT, 1.0 / 256, v1, s1, rq, 'ssq')
                rstd_batch(sskv, NTT, 1.0 / 128, v1, s1, rkv, 'sskv')
                x4 = cqkv[:, :, 384:416].rearrange("p t (two j) -> p t two j", two=2)
                o4 = kpe_r[:, :, :].rearrange("p t (two j) -> p t two j", two=2)
                tv = [t[:, 0:256].rearrange("p (a b) -> p a b", a=NTT) for t in tmps]
                allq = [('cqkv', t) for t in range(NTT)]
                S.op('dve', lambda e: e.tensor_tensor(out=tv[0], in0=x4[:, :, 0, :], in1=cm3, op=ALU.mult), reads=allq + TAB, writes=['rt0'])
                S.op('dve', lambda e: e.tensor_tensor(out=tv[1], in0=x4[:, :, 1, :], in1=sm3, op=ALU.mult), reads=allq + TAB, writes=['rt1'])
                S.op('dve', lambda e: e.tensor_tensor(out=o4[:, :, 0, :], in0=tv[0], in1=tv[1], op=ALU.subtract), reads=['rt0', 'rt1'], writes=['kpe_r'])
                S.op('dve', lambda e: e.tensor_tensor(out=tv[2], in0=x4[:, :, 0, :], in1=sm3, op=ALU.mult), reads=allq + TAB, writes=['rt2'])
                S.op('dve', lambda e: e.tensor_tensor(out=tv[3], in0=x4[:, :, 1, :], in1=cm3, op=ALU.mult), reads=allq + TAB, writes=['rt3'])
                S.op('dve', lambda e: e.tensor_tensor(out=o4[:, :, 1, :], in0=tv[2], in1=tv[3], op=ALU.add), reads=['rt2', 'rt3'], writes=['kpe_r'])
                if tsub == 11:
                    S.barrier(); raise _Stop()
                for tt in range(NTT):
                    b = tt % 2
                    S.op('dve', lambda e, tt=tt, b=b: e.scalar_tensor_tensor(
                        out=nrm[b][:, 0:256], in0=cqkv[:, tt, 0:256], scalar=rq[:, tt:tt + 1], in1=gq[:],
                        op0=ALU.mult, op1=ALU.mult), reads=[('cqkv', tt), 'ssq_rstd', 'gq'], writes=[('nrmq', b)])
                    S.op('dve', lambda e, tt=tt, b=b: e.scalar_tensor_tensor(
                        out=nrm[b][:, 256:384], in0=cqkv[:, tt, 256:384], scalar=rkv[:, tt:tt + 1], in1=gkv[:],
                        op0=ALU.mult, op1=ALU.mult), reads=[('cqkv', tt), 'sskv_rstd', 'gkv'], writes=[('nrmk', b)])
                    pb = 6 + b
                    pv = bankbf[pb]
                    S.op('act', lambda e, tt=tt, b=b: e.activation(out=nrm[b][:, 384:416], in_=kpe_r[:, tt, :], func=AF.Copy),
                         reads=['kpe_r'], writes=[('nrmp', b)])
                    for j in range(4):
                        S.op('pe', lambda e, j=j, b=b, pv=pv: e.transpose(
                            out=pv[:, j * 128:(j + 1) * 128], in_=nrm[b][:, j * 128:(j + 1) * 128], identity=ident[:]),
                            reads=[('nrmq', b), ('nrmk', b), ('nrmp', b), 'ident'], writes=[('bank', pb)], inc=(j == 3))
                    S.op('act', lambda e, tt=tt, pv=pv: e.activation(
                        out=cqnT[:, :, tt * 128:(tt + 1) * 128],
                        in_=pv[:, 0:256].rearrange("p (k t) -> p k t", k=2), func=AF.Copy),
                        reads=[('bank', pb)], writes=[('cqnT', tt)])
                    S.op('act', lambda e, tt=tt, pv=pv: e.activation(out=ckvnT_loc[:, tt * 128:(tt + 1) * 128], in_=pv[:, 256:384], func=AF.Copy),
                         reads=[('bank', pb)], writes=['ckvnT_loc'])
                    S.op('act', lambda e, tt=tt, pv=pv: e.activation(out=kpeT_loc[0:32, tt * 128:(tt + 1) * 128], in_=pv[0:32, 384:512], func=AF.Copy),
                         reads=[('bank', pb)], writes=['kpeT_loc'])
                if tsub == 12:
                    S.barrier(); raise _Stop()
                S.dma('sp', xin[0:128, :], ckvnT_loc[:], reads=['ckvnT_loc'], writes=['xin_a'])
                S.dma('sp', xin[128:160, :], kpeT_loc[0:32, :], reads=['kpeT_loc'], writes=['xin_b'])
                if tsub == 13:
                    S.barrier(); raise _Stop()
                S.collective("AllGather", xin.ap().opt(), xout.ap().opt(), GROUPS, reads=['xin_a', 'xin_b'], writes=['xout'], name="ag1")
                if tsub == 14:
                    S.barrier(); raise _Stop()
                if stop == 'p1':
                    S.dma('sp', dbgbf[:, 0:2048], ckvnT_loc[:], reads=['ckvnT_loc'], writes=['dbg_a'])
                    S.dma('sp', dbgbf[0:32, 2048:4096], kpeT_loc[0:32, :], reads=['kpeT_loc'], writes=['dbg_b'])
                    S.dma('sp', dbgbf[:, 4096:8192], cqnT[:, :, :].rearrange("p a t -> p (a t)"), reads=[('cqnT', t) for t in range(NTT)], writes=['dbg_c'])
                    S.dma('sp', dbgbf2[:, :], xout[0:128, :], reads=['xout'], writes=['dbg_d'])
                    out_keys.extend(['dbg_' + c for c in 'abcd'])
                S.barrier()
                if stop == 'p1':
                    raise _Stop()

            esT = ExitStack()
            T_all = esT.enter_context(nc.sbuf_tensor("m_T_all", [128, NTT, 512], BF16))
            DT = esT.enter_context(nc.sbuf_tensor("m_DT", [128, 8, 128], BF16))
            QD = esT.enter_context(nc.sbuf_tensor("m_QD", [128, 8, 128], BF16))
            with ExitStack() as es:
                sb = lambda n, s, d: es.enter_context(nc.sbuf_tensor("dt_" + n, s, d))
                t128 = sb("t128", [128, 128], F32); t128b = sb("t128b", [128, 128], F32)
                LGk = ['LGa', 'LGb']
                for hh in range(8):
                    S.op('dve', lambda e, hh=hh: e.tensor_scalar(out=t128[:], in0=RF, scalar1=LGrow[:, hh:hh + 1], scalar2=None,
                                                                 op0=ALU.mult), reads=['ctab', 'LGrow'], writes=['t128'])
                    S.op('dve', lambda e, hh=hh: e.scalar_tensor_tensor(out=t128b[:], in0=RB, scalar=LGrow[:, 8 + hh:9 + hh],
                                                                        in1=t128[:], op0=ALU.mult, op1=ALU.add),
                         reads=['ctab', 'LGrow', 't128'], writes=['t128b'])
                    S.op('act', lambda e, hh=hh: e.activation(out=DT[:, hh, :], in_=t128b[:], func=AF.Exp, bias=LN8),
                         reads=['t128b'], writes=[('DT', hh)])
                    S.op('act', lambda e, hh=hh: e.activation(out=QD[:, hh, :], in_=EI, func=AF.Exp, scale=LG[:, hh:hh + 1]),
                         reads=['ctab'] + LGk, writes=[('QD', hh)])
                S.barrier()
                if tsub == 20:
                    raise _Stop()

            with ExitStack() as es:
                sb = lambda n, s, d: es.enter_context(nc.sbuf_tensor("ra_" + n, s, d))
                wkr = sb("wkr", [128, 8, 512], BF16); wvr = sb("wvr", [128, 8, 512], BF16)
                rk_c = [sb("rk_c%d" % i, [128, 512], BF16) for i in range(2)]
                rv_c = [sb("rv_c%d" % i, [128, 512], BF16) for i in range(2)]
                kdd = [sb("kdd%d" % i, [128, 8, 128], BF16) for i in range(2)]
                tmps = [sb("rt%d" % i, [128, 256], F32) for i in range(4)]
                R = [sb("R%d" % i, [128, 512], F32) for i in range(2)]
                tR = sb("tR", [128, 512], F32)
                S.dma('pool', wkr[:], w_inv[:, :, 928:1440], writes=['wkr'])
                S.dma('pool', wvr[:], w_inv[:, :, 1440:1952], writes=['wvr'])
                S.op('dve', lambda e: e.memset(R[0][:], 0.0), writes=[('R', 0, 'f'), ('R', 0, 'b')])
                gCb = gC[:, :].unsqueeze(2).to_broadcast([128, 8, 64])
                KDf = KD[:, 0:8].unsqueeze(2).to_broadcast([128, 8, 64])
                KDb = KD[:, 8:16].unsqueeze(2).to_broadcast([128, 8, 64])

                def scan_step(c, i, dirn):
                    lo, hi = (0, 64) if dirn == 'f' else (64, 128)
                    cur, nxt = R[i % 2], R[(i + 1) % 2]
                    S.op('dve', lambda e: e.tensor_tensor(
                        out=tR[lo:hi, :].rearrange("p (a b) -> p a b", a=8), in0=cur[lo:hi, :].rearrange("p (a b) -> p a b", a=8),
                        in1=gCb[lo:hi], op=ALU.mult), reads=[('R', i % 2, dirn), 'gC'], writes=[('tR', dirn)])
                    S.op('dve', lambda e: e.tensor_tensor(out=nxt[lo:hi, :], in0=tR[lo:hi, :], in1=T_all[lo:hi, c, :], op=ALU.add),
                         reads=[('tR', dirn), ('T', c, dirn)], writes=[('R', (i + 1) % 2, dirn)])
                    S.op('act', lambda e: e.activation(out=T_all[lo:hi, c, :], in_=cur[lo:hi, :], func=AF.Copy),
                         reads=[('R', i % 2, dirn)], writes=[('T', c, dirn)])

                def retA_proj(c):
                    b = c % 2
                    bk, bv, bu = b, 2 + b, 4 + b
                    for k in range(8):
                        S.op('pe', lambda e, k=k, c=c, bk=bk: e.matmul(
                            bank[bk][:], lhsT=XT[:, k, c * 128:(c + 1) * 128], rhs=wkr[:, k, :], start=(k == 0), stop=(k == 7)),
                            reads=[('XT', c), 'wkr'], writes=[('bank', bk)], inc=(k == 7))
                    for k in range(8):
                        S.op('pe', lambda e, k=k, c=c, bv=bv: e.matmul(
                            bank[bv][:], lhsT=XT[:, k, c * 128:(c + 1) * 128], rhs=wvr[:, k, :], start=(k == 0), stop=(k == 7)),
                            reads=[('XT', c), 'wvr'], writes=[('bank', bv)], inc=(k == 7))

                def retA_dve(c):
                    b = c % 2
                    bk, bv, bu = b, 2 + b, 4 + b
                    x4 = bank[bk][:, :].rearrange("p (hh two j) -> p hh two j", hh=8, two=2)
                    o4 = rk_c[b][:, :].rearrange("p (hh two j) -> p hh two j", hh=8, two=2)
                    rope_tok(x4, o4, cr3[:, c, :], sr3[:, c, :], 8, 32, tmps, ([('bank', bk)], [('rk_c', b)]))
                    S.op('act', lambda e, b=b, bv=bv: e.activation(out=rv_c[b][:], in_=bank[bv][:], func=AF.Copy),
                         reads=[('bank', bv)], writes=[('rv_c', b)])
                    rk3 = rk_c[b][:, :].rearrange("p (a d) -> p a d", a=8)
                    S.op('dve', lambda e, b=b, rk3=rk3: e.tensor_tensor(out=kdd[b][:, :, 0:64], in0=rk3, in1=KDf, op=ALU.mult),
                         reads=[('rk_c', b), 'KD'], writes=[('kdd', b)])
                    S.op('dve', lambda e, b=b, rk3=rk3: e.tensor_tensor(out=kdd[b][:, :, 64:128], in0=rk3, in1=KDb, op=ALU.mult),
                         reads=[('rk_c', b), 'KD'], writes=[('kdd', b)])

                def retA_state(c):
                    b = c % 2
                    bk, bv, bu = b, 2 + b, 4 + b
                    for hh in range(8):
                        S.op('pe', lambda e, hh=hh, b=b, bu=bu: e.matmul(
                            bank[bu][:, hh * 64:(hh + 1) * 64], lhsT=kdd[b][:, hh, :], rhs=rv_c[b][:, hh * 64:(hh + 1) * 64],
                            start=True, stop=True), reads=[('kdd', b), ('rv_c', b)], writes=[('bank', bu)], inc=(hh == 7))
                    S.op('act', lambda e, c=c, bu=bu: e.activation(out=T_all[:, c, :], in_=bank[bu][:], func=AF.Copy),
                         reads=[('bank', bu)], writes=[('T', c, 'f'), ('T', c, 'b')])
                    scan_step(c, c, 'f')

                retA_proj(0)
                for c in range(NTT):
                    retA_dve(c)
                    if c + 1 < NTT:
                        retA_proj(c + 1)
                    retA_state(c)
                if tsub == 25:
                    S.barrier(); raise _Stop()
                for i in range(NTT):
                    scan_step(NTT - 1 - i, i, 'b')
                    if tsub == 26 and i == 0:
                        S.barrier(); raise _Stop()
                if tsub == 27:
                    S.barrier(); raise _Stop()
                S.dma('sp', xin2[:, :], R[0][:], reads=[('R', 0, 'f'), ('R', 0, 'b')], writes=['xin2'])
                if tsub == 28:
                    S.barrier(); raise _Stop()
                S.collective("AllGather", xin2.ap().opt(), xout2.ap().opt(), GROUPS, reads=['xin2'], writes=['xout2'], name="ag2")
                if tsub == 29:
                    S.barrier(); raise _Stop()
                if stop == 'reta':
                    S.dma('sp', dbgbf[:, 0:8192], T_all[:, :, :].rearrange("p a t -> p (a t)"),
                          reads=[('T', c, d_) for c in range(NTT) for d_ in 'fb'], writes=['dbg_a'])
                    S.dma('sp', dbg32[:, 0:512], R[0][:], reads=[('R', 0, 'f'), ('R', 0, 'b')], writes=['dbg_b'])
                    out_keys.extend(['dbg_a', 'dbg_b'])
                S.barrier()
                if stop == 'reta':
                    raise _Stop()

            with ExitStack() as es:
                sb = lambda n, s, d: es.enter_context(nc.sbuf_tensor("rb_" + n, s, d))
                wqr = sb("wqr", [128, 8, 512], BF16); wkr = sb("wkr", [128, 8, 512], BF16)
                wvr = sb("wvr", [128, 8, 512], BF16); wgr = sb("wgr", [128, 8, 512], BF16)
                carry = sb("carry", [128, 512], F32)
                tC = sb("tC", [128, 512], F32)
                rq_dup = [sb("rq_dup%d" % i, [128, 8, 128], BF16) for i in range(2)]
                rk_c = [sb("rk_c%d" % i, [128, 512], BF16) for i in range(2)]
                rv_c = [sb("rv_c%d" % i, [128, 512], BF16) for i in range(2)]
                qTp = [sb("qTp%d" % i, [128, 8, 128], BF16) for i in range(2)]
                qdT = [sb("qdT%d" % i, [128, 8, 128], BF16) for i in range(2)]
                kT = [sb("kT%d" % i, [128, 4, 128], BF16) for i in range(2)]
                sTm = sb("sTm", [128, 8, 128], BF16)
                Tbf = sb("Tbf", [128, 512], BF16)
                tmps = [sb("rt%d" % i, [128, 256], F32) for i in range(4)]
                osq = sb("osq", [128, 512], F32); ocen = sb("ocen", [128, 512], F32)
                sg = sb("sg", [128, 512], F32); rout = sb("rout", [128, 512], BF16)
                st_s = sb("st_s", [128, 8], F32); st_q = sb("st_q", [128, 8], F32)
                st_m = sb("st_m", [128, 8], F32); st_v = sb("st_v", [128, 8], F32); st_r = sb("st_r", [128, 8], F32)
                mhalf = sb("mhalf", [128, 8], F32)
                S.dma('pool', wqr[:], w_inv[:, :, 416:928], writes=['wqr'])
                S.dma('pool', wkr[:], w_inv[:, :, 928:1440], writes=['wkr'])
                S.dma('pool', wvr[:], w_inv[:, :, 1440:1952], writes=['wvr'])
                S.dma('pool', wgr[:], w_inv[:, :, 1952:2464], writes=['wgr'])
                S.op('dve', lambda e: e.memset(mhalf[:], -0.5), writes=['mhalf'])
                with ExitStack() as es2:
                    Epc = [es2.enter_context(nc.sbuf_tensor("rb_Epc%d" % i, [128, 512], F32)) for i in range(2)]
                    for r_ in range(4):
                        S.dma('sp', Epc[r_ % 2][:], xout2[r_ * 128:(r_ + 1) * 128, :], reads=['xout2'], writes=[('Epc', r_ % 2)])
                        cf = coef[:, r_ * 8:(r_ + 1) * 8].unsqueeze(2).to_broadcast([128, 8, 64])
                        dst = (carry if r_ == 0 else tC)
                        S.op('dve', lambda e, r_=r_, cf=cf, dst=dst: e.tensor_tensor(
                            out=dst[:, :].rearrange("p (a b) -> p a b", a=8), in0=Epc[r_ % 2][:, :].rearrange("p (a b) -> p a b", a=8),
                            in1=cf, op=ALU.mult), reads=[('Epc', r_ % 2), ('coef', r_)], writes=['carry' if r_ == 0 else 'tC'])
                        if r_ > 0:
                            S.op('dve', lambda e: e.tensor_tensor(out=carry[:], in0=carry[:], in1=tC[:], op=ALU.add),
                                 reads=['carry', 'tC'], writes=['carry'])
                    S.barrier()
                d3 = DCAR[:, :].rearrange("p (c hh) -> p c hh", c=NTT)
                sT4 = sTm[:, :, :].rearrange("p (a two) t -> p a two t", two=2)
                DT4 = DT[:, :, :].rearrange("p (a two) t -> p a two t", two=2)

                def retB_proj(c):
                    for (bk, wt, wk_) in ((0, wqr, 'wqr'), (1, wkr, 'wkr'), (2, wvr, 'wvr')):
                        for k in range(8):
                            S.op('pe', lambda e, k=k, bk=bk, wt=wt: e.matmul(
                                bank[bk][:], lhsT=XT[:, k, c * 128:(c + 1) * 128], rhs=wt[:, k, :], start=(k == 0), stop=(k == 7)),
                                reads=[('XT', c), wk_], writes=[('bank', bk)], inc=(k == 7))

                def retB_rest(c):
                    p = c % 2
                    x4 = bank[0][:, :].rearrange("p (hh two j) -> p hh two j", hh=8, two=2)
                    o4 = rq_dup[p][:, :, 0:64].rearrange("p hh (two j) -> p hh two j", two=2)
                    rope_tok(x4, o4, cr3[:, c, :], sr3[:, c, :], 8, 32, tmps, ([('bank', 0)], [('rq_a', p)]))
                    yield
                    S.op('act', lambda e: e.activation(out=rq_dup[p][:, :, 64:128], in_=rq_dup[p][:, :, 0:64], func=AF.Copy),
                         reads=[('rq_a', p)], writes=[('rq_b', p)])
                    yield
                    x4 = bank[1][:, :].rearrange("p (hh two j) -> p hh two j", hh=8, two=2)
                    o4 = rk_c[p][:, :].rearrange("p (hh two j) -> p hh two j", hh=8, two=2)
                    rope_tok(x4, o4, cr3[:, c, :], sr3[:, c, :], 8, 32, tmps, ([('bank', 1)], [('rk_c', p)]))
                    yield
                    S.op('act', lambda e: e.activation(out=rv_c[p][:], in_=bank[2][:], func=AF.Copy),
                         reads=[('bank', 2)], writes=[('rv_c', p)])
                    yield
                    pq = bankbf[3]
                    for hh in range(8):
                        S.op('pe', lambda e, hh=hh: e.transpose(out=pq[:, hh * 128:(hh + 1) * 128], in_=rq_dup[p][:, hh, :], identity=ident[:]),
                             reads=[('rq_a', p), ('rq_b', p), 'ident'], writes=[('bank', 3)], inc=(hh == 7))
                    yield
                    pk = bankbf[4]
                    for pr in range(4):
                        S.op('pe', lambda e, pr=pr: e.transpose(out=pk[:, pr * 128:(pr + 1) * 128], in_=rk_c[p][:, pr * 128:(pr + 1) * 128],
                                                              identity=ident[:]),
                             reads=[('rk_c', p), 'ident'], writes=[('bank', 4)], inc=(pr == 3))
                    yield
                    S.op('act', lambda e: e.activation(out=qTp[p][:, :, :], in_=pq[:, 0:1024].rearrange("p (a t) -> p a t", a=8), func=AF.Copy),
                         reads=[('bank', 3)], writes=[('qTp', p)])
                    yield
                    S.op('dve', lambda e: e.tensor_tensor(out=qdT[p][:, :, :], in0=qTp[p][:, :, :], in1=QD[:, :, :], op=ALU.mult),
                         reads=[('qTp', p)] + [('QD', i) for i in range(8)], writes=[('qdT', p)])
                    yield
                    S.op('act', lambda e: e.activation(out=kT[p][:, :, :], in_=pk[:, 0:512].rearrange("p (a t) -> p a t", a=4), func=AF.Copy),
                         reads=[('bank', 4)], writes=[('kT', p)])

                def retB_back(c):
                    p = c % 2
                    for hh in range(8):
                        lo = (hh % 2) * 64
                        sbk = 5 + (hh % 2)
                        S.op('pe', lambda e, hh=hh, lo=lo, sbk=sbk: e.matmul(
                            bank[sbk][:, (hh // 2) * 128:(hh // 2 + 1) * 128], lhsT=kT[p][lo:lo + 64, hh // 2, :],
                            rhs=qTp[p][lo:lo + 64, hh, :], start=True, stop=True),
                            reads=[('kT', p), ('qTp', p)], writes=[('bank', sbk)], inc=(hh >= 6))
                    yield
                    for g4 in range(2):
                        S.op('dve', lambda e, g4=g4: e.tensor_tensor(
                            out=sT4[:, :, g4, :], in0=bank[5 + g4][:, :].rearrange("p (a t) -> p a t", a=4),
                            in1=DT4[:, :, g4, :], op=ALU.mult),
                            reads=[('bank', 5 + g4)] + [('DT', i) for i in range(8)], writes=[('sTm', g4)])
                    S.op('dve', lambda e: e.tensor_tensor(
                        out=tC[:, :].rearrange("p (a b) -> p a b", a=8), in0=carry[:, :].rearrange("p (a b) -> p a b", a=8),
                        in1=d3[:, c, :].unsqueeze(2).to_broadcast([128, 8, 64]), op=ALU.mult),
                        reads=['carry', 'DCAR'], writes=['tC'])
                    yield
                    S.op('dve', lambda e: e.tensor_tensor(out=Tbf[:], in0=tC[:], in1=T_all[:, c, :], op=ALU.add),
                         reads=['tC', ('T', c, 'f'), ('T', c, 'b')], writes=['Tbf'])
                    yield
                    for hh in range(8):
                        S.op('pe', lambda e, hh=hh: e.matmul(bank[7][:, hh * 64:(hh + 1) * 64], lhsT=sTm[:, hh, :],
                                                             rhs=rv_c[p][:, hh * 64:(hh + 1) * 64], start=True, stop=False),
                             reads=[('sTm', hh % 2), ('rv_c', p)], writes=[('bank', 7)], inc=False)
                        S.op('pe', lambda e, hh=hh: e.matmul(bank[7][:, hh * 64:(hh + 1) * 64], lhsT=qdT[p][:, hh, :],
                                                             rhs=Tbf[:, hh * 64:(hh + 1) * 64], start=False, stop=True),
                             reads=[('qdT', p), 'Tbf'], writes=[('bank', 7)], inc=(hh == 7))
                    yield
                    for k in range(8):
                        S.op('pe', lambda e, k=k: e.matmul(
                            bank[5][:], lhsT=XT[:, k, c * 128:(c + 1) * 128], rhs=wgr[:, k, :], start=(k == 0), stop=(k == 7)),
                            reads=[('XT', c), 'wgr'], writes=[('bank', 5)], inc=(k == 7))
                    S.op('act', lambda e: e.activation(out=sg[:], in_=bank[5][:], func=AF.Silu), reads=[('bank', 5)], writes=['sg'])
                    yield
                    yield
                    o3 = bank[7][:, :].rearrange("p (a b) -> p a b", a=8)
                    S.op('dve', lambda e: e.tensor_reduce(out=st_s[:], in_=o3, axis=AX.X, op=ALU.add), reads=[('bank', 7)], writes=['st_s'])
                    yield
                    S.op('act', lambda e: e.activation(out=osq[:], in_=bank[7][:], func=AF.Square), reads=[('bank', 7)], writes=['osq'])
                    yield
                    S.op('dve', lambda e: e.tensor_reduce(out=st_q[:], in_=osq[:, :].rearrange("p (a b) -> p a b", a=8), axis=AX.X, op=ALU.add),
                         reads=['osq'], writes=['st_q'])
                    yield
                    S.op('dve', lambda e: e.tensor_scalar(out=st_m[:], in0=st_s[:], scalar1=1.0 / 64, scalar2=None, op0=ALU.mult),
                         reads=['st_s'], writes=['st_m'])
                    yield
                    S.op('dve', lambda e: e.tensor_tensor(out=st_v[:], in0=st_m[:], in1=st_m[:], op=ALU.mult), reads=['st_m'], writes=['st_v'])
                    yield
                    S.op('dve', lambda e: e.scalar_tensor_tensor(out=st_v[:], in0=st_q[:], scalar=1.0 / 64, in1=st_v[:],
                                                                op0=ALU.mult, op1=ALU.subtract), reads=['st_q', 'st_v'], writes=['st_v'])
                    yield
                    S.op('dve', lambda e: e.tensor_scalar(out=st_v[:], in0=st_v[:], scalar1=EPS, scalar2=None, op0=ALU.add),
                         reads=['st_v'], writes=['st_v'])
                    yield
                    S.op('pool', lambda e: e.tensor_tensor(out=st_r[:], in0=st_v[:], in1=mhalf[:], op=ALU.pow),
                         reads=['st_v', 'mhalf'], writes=['st_r'])
                    yield
                    S.op('dve', lambda e: e.tensor_tensor(out=ocen[:, :].rearrange("p (a b) -> p a b", a=8), in0=o3,
                                                          in1=st_m[:, :].unsqueeze(2).to_broadcast([128, 8, 64]), op=ALU.subtract),
                         reads=[('bank', 7), 'st_m'], writes=['ocen'])
                    yield
                    S.op('dve', lambda e: e.tensor_tensor(out=ocen[:, :].rearrange("p (a b) -> p a b", a=8),
                                                          in0=ocen[:, :].rearrange("p (a b) -> p a b", a=8),
                                                          in1=st_r[:, :].unsqueeze(2).to_broadcast([128, 8, 64]), op=ALU.mult),
                         reads=['ocen', 'st_r'], writes=['ocen'])
                    yield
                    S.op('dve', lambda e: e.tensor_tensor(out=rout[:], in0=ocen[:], in1=sg[:], op=ALU.mult),
                         reads=['ocen', 'sg'], writes=['rout'])
                    yield
                    pr_ = bankbf[6]
                    for j in range(4):
                        S.op('pe', lambda e, j=j: e.transpose(out=pr_[:, j * 128:(j + 1) * 128], in_=rout[:, j * 128:(j + 1) * 128], identity=ident[:]),
                             reads=['rout', 'ident'], writes=[('bank', 6)], inc=(j == 3))
                    S.op('act', lambda e: e.activation(out=XT[:, 4:8, c * 128:(c + 1) * 128],
                                                       in_=pr_[:, 0:512].rearrange("p (a t) -> p a t", a=4), func=AF.Copy),
                         reads=[('bank', 6)], writes=[('XT', c)])
                    yield

                def interleave(gens):
                    gens = [g for g in gens if g is not None]
                    while gens:
                        for g in list(gens):
                            try:
                                next(g)
                            except StopIteration:
                                gens.remove(g)

                retB_proj(0)
                interleave([retB_rest(0)])
                for c in range(NTT):
                    if c + 1 < NTT:
                        retB_proj(c + 1)
                    interleave([retB_back(c), retB_rest(c + 1) if c + 1 < NTT else None])
                if stop == 'retb':
                    S.dma('sp', dbgbf[:, 0:8192], XT[:, 4:8, :].rearrange("p a t -> p (a t)"), reads=[('XT', c) for c in range(NTT)], writes=['dbg_a'])
                    out_keys.extend(['dbg_a'])
                S.barrier()
                if stop == 'retb':
                    raise _Stop()

            esT.close()

            with ExitStack() as es:
                sb = lambda n, s, d: es.enter_context(nc.sbuf_tensor("at_" + n, s, d))
                qaugT = sb("qaugT", [128, 8, NT], BF16)
                ckvnT_all = sb("ckvnT_all", [128, 4, NT], BF16)
                for r_ in range(4):
                    S.dma('sp', ckvnT_all[:, r_, :], xout[r_ * 160:r_ * 160 + 128, :], reads=['xout'], writes=[('ckvnT_all', r_)])
                wuq = sb("wuq", [128, 2, 768], BF16)
                wukv = sb("wukv", [128, 1024], BF16)
                qa_tok = [sb("qa_tok%d" % i, [128, 8, 128], BF16) for i in range(2)]
                qs = sb("qs", [128, 768], F32)
                tmps = [sb("rt%d" % i, [128, 256], F32) for i in range(4)]
                kaug = [sb("kaug%d" % i, [128, 512], BF16) for i in range(3)]
                vext = [[sb("vext%d_%d" % (par, i), [128, 4, 128], BF16) for i in range(3)] for par in range(2)]
                PT = [sb("PT%d" % i, [128, 512], BF16) for i in range(4)]
                accs = [sb("accs%d" % i, [128, 512], F32) for i in range(4)]
                lnd = sb("lnd", [1, 512], F32); rden = sb("rden", [1, 512], F32)
                bcs = sb("bcs", [128, 512], F32)
                S.dma('pool', wuq[:], w_uq.rearrange("(kc p) n -> p kc n", p=128), writes=['wuq'])
                S.dma('pool', wukv[:], w_ukv, writes=['wukv'])
                for par in range(2):
                    for i in range(3):
                        S.op('dve', lambda e, par=par, i=i: e.memset(vext[par][i][:], 0.0), writes=[('vext', par, i)])
                        col = 64 if par == 0 else 0
                        S.op('dve', lambda e, par=par, i=i, col=col: e.memset(vext[par][i][:, :, col:col + 1], 1.0),
                             reads=[('vext', par, i)], writes=[('vext', par, i)])
                for i_ in range(2):
                    S.op('dve', lambda e, i_=i_: e.memset(qa_tok[i_][:], 0.0),
                         writes=[('qa_n', i_, 0), ('qa_n', i_, 1), ('qa_p', i_, 0), ('qa_p', i_, 1)])
                if tsub == 39:
                    S.barrier(); raise _Stop()
                for tt in range(NTT):
                    b = tt % 2
                    for (bk, c0, c1) in ((0, 0, 480), (1, 480, 768)):
                        for kc in range(2):
                            S.op('pe', lambda e, kc=kc, tt=tt, bk=bk, c0=c0, c1=c1: e.matmul(
                                bank[bk][:, 0:c1 - c0], lhsT=cqnT[:, kc, tt * 128:(tt + 1) * 128], rhs=wuq[:, kc, c0:c1],
                                start=(kc == 0), stop=(kc == 1)), reads=[('cqnT', tt), 'wuq'], writes=[('bank', bk)], inc=(kc == 1))
                    if tsub == 401 and tt == 0:
                        S.barrier(); raise _Stop()
                    S.op('act', lambda e: e.activation(out=qs[:, 0:480], in_=bank[0][:, 0:480], func=AF.Copy),
                         reads=[('bank', 0)], writes=['qs0'])
                    S.op('act', lambda e: e.activation(out=qs[:, 480:768], in_=bank[1][:, 0:288], func=AF.Copy),
                         reads=[('bank', 1)], writes=['qs1'])
                    qs3 = qs[:, :].rearrange("p (a d) -> p a d", a=8)
                    S.op('act', lambda e, b=b, qs3=qs3: e.activation(out=qa_tok[b][:, :, 0:64], in_=qs3[:, :, 0:64], func=AF.Copy),
                         reads=['qs0', 'qs1'], writes=[('qa_n', b, 0), ('qa_n', b, 1)])
                    x4 = qs3[:, :, 64:96].rearrange("p a (two j) -> p a two j", two=2)
                    o4 = qa_tok[b][:, :, 64:96].rearrange("p a (two j) -> p a two j", two=2)
                    rope_tok(x4, o4, cm3[:, tt, :], sm3[:, tt, :], 8, 16, tmps, (['qs0', 'qs1'], [('qa_p', b, 0), ('qa_p', b, 1)]))
                    if tsub == 402 and tt == 0:
                        S.barrier(); raise _Stop()
                    pb = 6 + b
                    pv = bankbf[pb]
                    for hh in range(8):
                        S.op('pe', lambda e, hh=hh, b=b, pv=pv: e.transpose(
                            out=pv[:, hh * 128:(hh + 1) * 128], in_=qa_tok[b][:, hh, :], identity=ident[:]),
                            reads=[('qa_n', b, 0), ('qa_n', b, 1), ('qa_p', b, 0), ('qa_p', b, 1), 'ident'],
                            writes=[('bank', pb)], inc=(hh == 7))
                    S.op('act', lambda e, tt=tt, pv=pv: e.activation(
                        out=qaugT[:, :, tt * 128:(tt + 1) * 128],
                        in_=pv[:, 0:1024].rearrange("p (a t) -> p a t", a=8), func=AF.Copy),
                        reads=[('bank', pb)], writes=[('qaug', tt)])

                if tsub == 40:
                    S.barrier(); raise _Stop()
                xo = xout.ap()

                def prodK(gp):
                    hh, p = gp // 16, gp % 16
                    r_, t0 = p // 4, (p % 4) * 512
                    sl = gp % 3
                    S.op('pe', lambda e: e.matmul(bank[7][0:64, :], lhsT=wukv[:, hh * 128:hh * 128 + 64],
                                                  rhs=ckvnT_all[:, r_, t0:t0 + 512], start=True, stop=True),
                         reads=['wukv', ('ckvnT_all', r_)], writes=[('bank', 7)])
                    S.op('dve', lambda e: e.tensor_copy(out=kaug[sl][0:64, :], in_=bank[7][0:64, :]),
                         reads=[('bank', 7)], writes=[('kaug_n', sl)])
                    S.dma('sp', kaug[sl][64:96, :], xo[r_ * 160 + 128:r_ * 160 + 160, t0:t0 + 512],
                          reads=['xout'], writes=[('kaug_p', sl)])

                def prodV(gp):
                    hh, p = gp // 16, gp % 16
                    r_, t0 = p // 4, (p % 4) * 512
                    sl = gp % 3
                    par = hh % 2
                    for j in range(4):
                        S.op('pe', lambda e, j=j: e.matmul(bank[7][:, j * 64:(j + 1) * 64],
                                                           lhsT=ckvnT_all[:, r_, t0 + j * 128:t0 + (j + 1) * 128],
                                                           rhs=wukv[:, hh * 128 + 64:hh * 128 + 128], start=True, stop=True),
                             reads=['wukv', ('ckvnT_all', r_)], writes=[('bank', 7)], inc=(j == 3))
                    off = 0 if par == 0 else 64
                    S.op('dve', lambda e: e.tensor_copy(out=vext[par][sl][:, :, off:off + 64],
                                                        in_=bank[7][:, 0:256].rearrange("p (a d) -> p a d", a=4)),
                         reads=[('bank', 7)], writes=[('vext', par, sl)])

                steps = [(gp, kt, qg) for gp in range(8 * 16) for kt in range(4) for qg in range(4)]
                NS = len(steps)

                def emit_S(i):
                    gp, kt, qg = steps[i]
                    hh = gp // 16
                    sl = gp % 3
                    sbk = 4 + (i % 3)
                    S.op('pe', lambda e: e.matmul(bank[sbk][:], lhsT=kaug[sl][0:96, kt * 128:(kt + 1) * 128],
                                                  rhs=qaugT[0:96, hh, qg * 512:(qg + 1) * 512], start=True, stop=True),
                         reads=[('kaug_n', sl), ('kaug_p', sl)] + [('qaug', qg * 4 + j) for j in range(4)], writes=[('bank', sbk)])
                    S.op('act', lambda e: e.activation(out=PT[i % 4][:], in_=bank[sbk][:], func=AF.Exp, scale=SCALE),
                         reads=[('bank', sbk)], writes=[('PT', i % 4)])

                def emit_PV(i):
                    gp, kt, qg = steps[i]
                    hh, p = gp // 16, gp % 16
                    sl = gp % 3
                    par = hh % 2
                    first = (p == 0 and kt == 0)
                    last = (p == 15 and kt == 3)
                    S.op('pe', lambda e: e.matmul(bank[qg][:], lhsT=vext[par][sl][:, kt, :], rhs=PT[i % 4][:],
                                                  start=first, stop=last),
                         reads=[('vext', par, sl), ('PT', i % 4)], writes=[('bank', qg)], inc=last)
                    if last:
                        S.op('dve', lambda e: e.tensor_copy(out=accs[qg][:], in_=bank[qg][:]),
                             reads=[('bank', qg)], writes=[('accs', qg)])

                def finish_head(hh):
                    par = hh % 2
                    drow = 64 if par == 0 else 0
                    o0 = 0 if par == 0 else 64
                    for qg in range(4):
                        S.op('act', lambda e, qg=qg: e.activation(out=lnd[0:1, :], in_=accs[qg][drow:drow + 1, :], func=AF.Ln),
                             reads=[('accs', qg)], writes=['lnd'])
                        S.op('act', lambda e: e.activation(out=rden[0:1, :], in_=lnd[0:1, :], func=AF.Exp, scale=-1.0),
                             reads=['lnd'], writes=['rden'])
                        S.op('pe', lambda e: e.matmul(bank[7][:], lhsT=ones32[0:1, :], rhs=rden[0:1, :], start=True, stop=True),
                             reads=['ones32', 'rden'], writes=[('bank', 7)])
                        S.op('dve', lambda e: e.tensor_copy(out=bcs[:], in_=bank[7][:]), reads=[('bank', 7)], writes=['bcs'])
                        S.op('dve', lambda e, qg=qg: e.tensor_tensor(
                            out=XT[o0:o0 + 64, hh // 2, qg * 512:(qg + 1) * 512], in0=accs[qg][o0:o0 + 64, :],
                            in1=bcs[o0:o0 + 64, :], op=ALU.mult),
                            reads=[('accs', qg), 'bcs'], writes=[('XTa', hh // 2, qg)])

                prodK(0)
                prodV(0)
                emit_S(0)
                emit_S(1)
                for i in range(NS):
                    gp, kt, qg = steps[i]
                    if qg == 0 and gp + 1 < 128:
                        if kt == 0:
                            prodK(gp + 1)
                        elif kt == 2:
                            prodV(gp + 1)
                    if i + 2 < NS:
                        emit_S(i + 2)
                    emit_PV(i)
                    if gp % 16 == 15 and kt == 3 and qg == 3:
                        finish_head(gp // 16)
                if stop == 'attn':
                    S.dma('sp', dbgbf[:, 0:8192], XT[:, 0:4, :].rearrange("p a t -> p (a t)"),
                          reads=[('XTa', a, q_) for a in range(4) for q_ in range(4)], writes=['dbg_a'])
                    out_keys.extend(['dbg_a'])
                S.barrier()
                if stop == 'attn':
                    raise _Stop()

            with ExitStack() as es:
                sb = lambda n, s, d: es.enter_context(nc.sbuf_tensor("wo_" + n, s, d))
                wo = sb("wo", [128, 8, D], BF16)
                S.dma('pool', wo[:], w_o.rearrange("(kc p) n -> p kc n", p=128), writes=['wo'])
                st = 0
                for tt in range(NTT):
                    for half in range(2):
                        bd = st % 2
                        for k in range(8):
                            lhs = XT[:, k, tt * 128:(tt + 1) * 128]
                            rd = [('XTa', k, tt // 4)] if k < 4 else [('XT', tt)]
                            S.op('pe', lambda e, k=k, lhs=lhs, half=half, bd=bd: e.matmul(
                                bank[bd][:], lhsT=lhs, rhs=wo[:, k, half * 512:(half + 1) * 512], start=(k == 0), stop=(k == 7)),
                                reads=rd + ['wo'], writes=[('bank', bd)], inc=(k == 7))
                        S.op('dve', lambda e, tt=tt, half=half, bd=bd: e.tensor_tensor(
                            out=h[:, tt, half * 512:(half + 1) * 512], in0=bank[bd][:], in1=h[:, tt, half * 512:(half + 1) * 512],
                            op=ALU.add), reads=[('bank', bd), ('h', tt)], writes=[('h', tt)])
                        st += 1
                S.barrier()

    rope_tables()
    load_x()
    if skip_ffn and _os.environ.get("PRE_ACT"):
        fn = getattr(AF, _os.environ["PRE_ACT"])
        S.op('act', lambda e: e.activation(out=ss[:], in_=ss[:], func=fn), writes=['pre'])
        S.barrier()
    if not skip_ffn:
        ffn(g_ffn1, wg1, wu1, wd1, "f1_")
    if stop == 'ffn1':
        store_h_raw()
        S.wait_all('sp', out_keys)
    else:
        if not skip_mid:
            try:
                middle()
            except _Stop:
                pass
        if stop is not None:
            store_h_raw()
            S.wait_all('sp', out_keys)
        else:
            ffn(g_ffn2, wg2, wu2, wd2, "f2_")
            final_norm_store()
    print("instr", S.ninstr, "waits", S.nwaits, "cnt", S.cnt)
    return nc


_NC_CACHE = {}


def _const_tables():
    ct = np.zeros((128, NCT), np.float32)
    j = np.arange(128, dtype=np.float64)[:, None]
    i = np.arange(128, dtype=np.float64)[None, :]
    ct[:, 0:128] = np.maximum(i - j, 0.0)
    ct[:, 128:256] = np.maximum(j - i, 0.0)
    ct[0:64, 256:384] = i + 1.0
    ct[64:128, 256:384] = 128.0 - i
    ct[:, 384:392] = 127.0 - j
    ct[:, 392:400] = j
    c = np.arange(16, dtype=np.float64)[None, :]
    ct[0:64, 400:416] = 128.0 * c
    ct[64:128, 400:416] = 128.0 * (15.0 - c)
    ct[:, 416:448] = (np.float32(10000.0) ** (-np.arange(32, dtype=np.float32) / np.float32(32)))[None, :]
    ct[:, 448:464] = (np.float32(10000.0) ** (-np.arange(16, dtype=np.float32) / np.float32(16)))[None, :]
    return ct


def _geo(r):
    g = np.zeros((128, 8), np.float32)
    for rp in range(4):
        if rp < r:
            g[0:64, rp] = r - 1 - rp
            g[0:64, 4 + rp] = 1.0
        if rp > r:
            g[64:128, rp] = rp - r - 1
            g[64:128, 4 + rp] = 1.0
    return g


def _core_inputs(inputs):
    f32 = lambda a: np.ascontiguousarray(np.asarray(a, dtype=np.float32))
    x = f32(inputs["x"])
    pos = np.asarray(inputs["positions"]).astype(np.int32)
    common = {
        "g_mix": f32(inputs["mix_norm"]).reshape(1, D), "w_in": f32(inputs["w_in"])[0],
        "g_q": f32(inputs["q_norm"]).reshape(1, 256), "w_uq": f32(inputs["w_uq"])[0],
        "g_kv": f32(inputs["kv_norm"]).reshape(1, 128), "w_ukv": f32(inputs["w_ukv"])[0],
        "dec": np.ascontiguousarray(np.concatenate([f32(inputs["ret_decay_fwd"]).reshape(1, 8),
                                                    f32(inputs["ret_decay_bwd"]).reshape(1, 8)], axis=1)),
        "w_o": f32(inputs["w_o"])[0],
        "ctab": _const_tables(),
        "g_ffn1": f32(inputs["ffn1_norm"]).reshape(1, D),
        "wg1": f32(inputs["ffn1_w_gate"])[0], "wu1": f32(inputs["ffn1_w_up"])[0], "wd1": f32(inputs["ffn1_w_down"])[0],
        "g_ffn2": f32(inputs["ffn2_norm"]).reshape(1, D),
        "wg2": f32(inputs["ffn2_w_gate"])[0], "wu2": f32(inputs["ffn2_w_up"])[0], "wd2": f32(inputs["ffn2_w_down"])[0],
        "g_fin": f32(inputs["final_norm"]).reshape(1, D),
    }
    in_maps = []
    for c in range(NCORES):
        b, r = c // 4, c % 4
        m = dict(common)
        m["x"] = np.ascontiguousarray(x[b, r * NT:(r + 1) * NT, :])
        m["pos_tok"] = np.ascontiguousarray(pos[b, r * NT:(r + 1) * NT].reshape(NTT, 128).T)
        m["geo"] = _geo(r)
        in_maps.append(m)
    return in_maps


def run(inputs, stop=None, skip_mid=False, trace=False, skip_ffn=False, tsub=99):
    key = (stop, skip_mid, skip_ffn, tsub)
    if key not in _NC_CACHE:
        _NC_CACHE[key] = build(stop=stop, skip_mid=skip_mid, skip_ffn=skip_ffn, tsub=tsub)
    nc = _NC_CACHE[key]
    in_maps = _core_inputs(inputs)
    res = run_bass_kernel_spmd(nc, in_maps, core_ids=list(range(NCORES)), **({"trace": True} if trace else {}))
    out = np.empty((2, 8192, D), np.float32)
    for c in range(NCORES):
        b, r = c // 4, c % 4
        out[b, r * NT:(r + 1) * NT, :] = res.results[c]["out"]
    return out, res


def kernel(**inputs):
    out, _ = run(inputs)
    return out
```

```python
import math
from contextlib import ExitStack
import numpy as np
import concourse.bass as bass
import concourse.mybir as mybir
from concourse.bass_utils import run_bass_kernel_spmd

F32 = mybir.dt.float32
BF16 = mybir.dt.bfloat16
I32 = mybir.dt.int32
ALU = mybir.AluOpType
AF = mybir.ActivationFunctionType
AX = mybir.AxisListType

NCORES = 8
NT = 2048
NTT = 16
D = 1024
DFF = 2816
NFC = 22
EPS = 1e-6
FFN_PARTS = [(0, 6), (6, 6), (12, 6), (18, 4)]


class Sched:
    def __init__(self, nc, n_dma=20):
        self.nc = nc
        self.eng = {'pe': nc.tensor, 'act': nc.scalar, 'dve': nc.vector,
                    'pool': nc.gpsimd, 'sp': nc.sync}
        self.semh = {}
        self.cnt = {}
        for k in ('pe', 'act', 'dve', 'pool'):
            self.semh[k] = nc.alloc_semaphore("s_" + k)
            self.cnt[k] = 0
        self.waited = {k: {} for k in self.eng}
        self.res = {}
        self.ring = {}
        self.ring_i = {}
        self.uses = {}
        for q in ('sp', 'pool'):
            keys = []
            for i in range(n_dma):
                key = "d_%s%d" % (q, i)
                self.semh[key] = nc.alloc_semaphore(key)
                self.uses[key] = 0
                keys.append(key)
            self.ring[q] = keys
            self.ring_i[q] = 0
        self.nwaits = 0
        self.ninstr = 0
        self.cc_dummy = nc.alloc_sbuf_tensor("cc_dummy", [128, 8], F32)

    def _deps(self, reads, writes):
        deps = []
        for r in reads:
            e = self.res.get(r)
            if e is not None and e[0] is not None:
                deps.append(e[0])
        for w in writes:
            e = self.res.get(w)
            if e is not None:
                if e[0] is not None:
                    deps.append(e[0])
                deps.extend(e[1].values())
        return deps

    def _record(self, tok, reads, writes, rkey):
        for r in reads:
            e = self.res.setdefault(r, [None, {}])
            e[1][rkey] = tok
        for w in writes:
            self.res[w] = [tok, {}]

    def _wait(self, q, tok, skip_self_pe=True):
        key, val = tok
        if q == 'pe' and key == 'pe' and skip_self_pe:
            return
        if self.waited[q].get(key, 0) >= val:
            return
        self.eng[q].wait_ge(self.semh[key], val)
        self.waited[q][key] = val
        self.nwaits += 1

    def op(self, q, fn, reads=(), writes=(), inc=True):
        for d in self._deps(reads, writes):
            self._wait(q, d)
        ins = fn(self.eng[q])
        self.ninstr += 1
        if inc:
            self.cnt[q] += 1
            ins.then_inc(self.semh[q], 1)
            tok = (q, self.cnt[q])
        else:
            tok = (q, self.cnt[q] + 1)
        self._record(tok, reads, writes, q)
        return tok

    def dma(self, q, out, in_, reads=(), writes=(), **kw):
        ring = self.ring[q]
        key = ring[self.ring_i[q] % len(ring)]
        self.ring_i[q] += 1
        use = self.uses[key]
        if use > 0:
            self._wait(q, (key, 16 * use))
        for d in self._deps(reads, writes):
            self._wait(q, d)
        self.eng[q].dma_start(out=out, in_=in_, **kw).then_inc(self.semh[key], 16)
        self.ninstr += 1
        self.uses[key] = use + 1
        tok = (key, 16 * (use + 1))
        self._record(tok, reads, writes, key)
        return tok

    def collective(self, kind, ins_ap, outs_ap, groups, reads, writes, name):
        key = "cc_" + name
        self.semh[key] = self.nc.alloc_semaphore(key)
        for d in self._deps(reads, writes):
            self._wait('pool', d)
        self.nc.gpsimd.collective_compute(kind, ALU.bypass, replica_groups=groups,
                                          ins=[ins_ap], outs=[outs_ap]).then_inc(self.semh[key])
        self.nc.gpsimd.wait_ge(self.semh[key], 1)
        self.ninstr += 1
        return self.op('pool', lambda e: e.memset(self.cc_dummy[:], 0.0), reads=reads, writes=writes)

    def barrier(self):
        toks = [(k, self.cnt[k]) for k in ('pe', 'act', 'dve', 'pool') if self.cnt[k] > 0]
        toks += [(key, 16 * u) for key, u in self.uses.items() if u > 0]
        for q in self.eng:
            for t in toks:
                self._wait(q, t, skip_self_pe=False)

    def wait_all(self, q, keys):
        for k in keys:
            e = self.res.get(k)
            if e is not None:
                if e[0] is not None:
                    self._wait(q, e[0], skip_self_pe=False)
                for t in e[1].values():
                    self._wait(q, t, skip_self_pe=False)


NCT = 464
SCALE = 96 ** -0.5
MAGIC = 12582912.0
C1 = 6.28125
C2 = 2.0 * math.pi - 6.28125
PI_S = 3.141592
LN8 = math.log(0.125)


class _Stop(Exception):
    pass


def build(stop=None, skip_mid=False, skip_ffn=False, tsub=99):
    import os as _os
    nc = bass.Bass("TRN2", target_bir_lowering=False)

    def din(name, shape, dtype=F32):
        return nc.dram_tensor(name, list(shape), dtype, kind="ExternalInput").ap()

    x_d = din("x", [NT, D])
    g_ffn1 = din("g_ffn1", [1, D]); wg1 = din("wg1", [D, DFF]); wu1 = din("wu1", [D, DFF]); wd1 = din("wd1", [DFF, D])
    g_ffn2 = din("g_ffn2", [1, D]); wg2 = din("wg2", [D, DFF]); wu2 = din("wu2", [D, DFF]); wd2 = din("wd2", [DFF, D])
    g_fin = din("g_fin", [1, D])
    g_mix = din("g_mix", [1, D]); w_in = din("w_in", [D, 2464])
    g_q = din("g_q", [1, 256]); w_uq = din("w_uq", [256, 768])
    g_kv = din("g_kv", [1, 128]); w_ukv = din("w_ukv", [128, 1024])
    dec_d = din("dec", [1, 16]); w_o = din("w_o", [D, D])
    pos_d = din("pos_tok", [128, NTT], I32)
    ctab_d = din("ctab", [128, NCT]); geo_d = din("geo", [128, 8])
    out_d = nc.dram_tensor("out", [NT, D], F32, kind="ExternalOutput").ap()
    if stop is not None:
        dbg32 = nc.dram_tensor("dbg32", [128, 2048], F32, kind="ExternalOutput").ap()
        dbgbf = nc.dram_tensor("dbgbf", [128, 8192], BF16, kind="ExternalOutput").ap()
        dbgbf2 = nc.dram_tensor("dbgbf2", [128, 2048], BF16, kind="ExternalOutput").ap()
    xin = nc.dram_tensor("xin", [160, NT], BF16)
    xout = nc.dram_tensor("xout", [640, NT], BF16)
    xin2 = nc.dram_tensor("xin2", [128, 512], F32)
    xout2 = nc.dram_tensor("xout2", [512, 512], F32)
    GROUPS = [[0, 1, 2, 3], [4, 5, 6, 7]]

    S = Sched(nc)
    ps = nc.alloc_psum_tensor

    h = nc.alloc_sbuf_tensor("h", [128, NTT, D], F32)
    XT = nc.alloc_sbuf_tensor("XT", [128, 8, NT], BF16)
    ident = nc.alloc_sbuf_tensor("ident", [128, 128], BF16)
    ss = nc.alloc_sbuf_tensor("ss", [128, NTT], F32)
    var = nc.alloc_sbuf_tensor("var", [128, NTT], F32)
    sd = nc.alloc_sbuf_tensor("sd", [128, NTT], F32)
    rstd = nc.alloc_sbuf_tensor("rstd", [128, NTT], F32)

    ctab = nc.alloc_sbuf_tensor("ctab_sb", [128, NCT], F32)
    cr = nc.alloc_sbuf_tensor("cr_sb", [128, 512], F32); sr = nc.alloc_sbuf_tensor("sr_sb", [128, 512], F32)
    cm = nc.alloc_sbuf_tensor("cm_sb", [128, 256], F32); sm = nc.alloc_sbuf_tensor("sm_sb", [128, 256], F32)
    INVR = ctab[:, 416:448]; INVM = ctab[:, 448:464]

    bank = [ps("bank%d" % i, [128, 512], F32) for i in range(8)]
    bankbf = [b.bitcast(BF16) for b in bank]

    S.op('dve', lambda e: e.memset(ident[:], 1.0), writes=['ident'])
    S.op('pool', lambda e: e.affine_select(out=ident[:], in_=ident[:], pattern=[[-1, 128]],
                                           compare_op=ALU.is_equal, fill=0.0, base=0, channel_multiplier=1),
         reads=['ident'], writes=['ident'])

    out_keys = []
    ov = out_d.rearrange("(t p) d -> p t d", p=128)

    def load_x():
        xv = x_d.rearrange("(t p) d -> p t d", p=128)
        for tt in range(NTT):
            S.dma('sp', h[:, tt, :], xv[:, tt, :], writes=[('h', tt)])

    def rstd_batch(ss_t, n, inv_n, var_t, sd_t, rstd_t, key):
        S.op('dve', lambda e: e.tensor_scalar(out=var_t[:], in0=ss_t[:], scalar1=inv_n, scalar2=EPS,
                                              op0=ALU.mult, op1=ALU.add),
             reads=[(key, t) for t in range(n)], writes=[key + '_var'])
        S.op('act', lambda e: e.activation(out=sd_t[:], in_=var_t[:], func=AF.Sqrt),
             reads=[key + '_var'], writes=[key + '_sd'])
        S.op('dve', lambda e: e.reciprocal(out=rstd_t[:], in_=sd_t[:]), reads=[key + '_sd'], writes=[key + '_rstd'])

    def norm_stats(gain_d, gbc, junk):
        S.dma('sp', gbc[:], gain_d[0, :].partition_broadcast(128), writes=['gbc'])
        for tt in range(NTT):
            S.op('dve', lambda e, tt=tt: e.scalar_tensor_tensor(
                out=junk[:], in0=h[:, tt, :], scalar=1.0, in1=h[:, tt, :],
                op0=ALU.mult, op1=ALU.mult, accum_out=ss[:, tt:tt + 1]),
                reads=[('h', tt)], writes=[('ss', tt)])
        rstd_batch(ss, NTT, 1.0 / D, var, sd, rstd, 'ss')

    def norm_to_XT(gain_d, gbc, junk, xn):
        norm_stats(gain_d, gbc, junk)
        for tt in range(NTT):
            b = tt % 2
            S.op('dve', lambda e, tt=tt, b=b: e.scalar_tensor_tensor(
                out=xn[b][:], in0=h[:, tt, :], scalar=rstd[:, tt:tt + 1], in1=gbc[:],
                op0=ALU.mult, op1=ALU.mult),
                reads=[('h', tt), 'ss_rstd', 'gbc'], writes=[('xn', b)])
            pb = 6 + b
            pv = bankbf[pb]
            for k in range(8):
                S.op('pe', lambda e, k=k, b=b, pv=pv: e.transpose(
                    out=pv[:, k * 128:(k + 1) * 128], in_=xn[b][:, k * 128:(k + 1) * 128], identity=ident[:]),
                    reads=[('xn', b), 'ident'], writes=[('bank', pb)], inc=(k == 7))
            S.op('act', lambda e, tt=tt, pv=pv: e.activation(
                out=XT[:, :, tt * 128:(tt + 1) * 128],
                in_=pv[:, 0:1024].rearrange("p (k t) -> p k t", k=8), func=AF.Copy),
                reads=[('bank', pb)], writes=[('XT', tt)])

    def rope_tables():
        S.dma('sp', ctab[:], ctab_d, writes=['ctab'])
        with ExitStack() as es:
            sb = lambda n, s, d: es.enter_context(nc.sbuf_tensor("t_" + n, s, d))
            pos_i = sb("pos_i", [128, NTT], I32); pos_f = sb("pos_f", [128, NTT], F32)
            ang = sb("ang", [128, 512], F32)
            ta = sb("ta", [128, 512], F32); tu = sb("tu", [128, 512], F32); tr = sb("tr", [128, 512], F32)
            S.dma('sp', pos_i[:], pos_d, writes=['pos_i'])
            S.op('dve', lambda e: e.tensor_copy(out=pos_f[:], in_=pos_i[:]), reads=['pos_i'], writes=['pos_f'])

            def sincos(n, nf, inv_ap, o_sin, o_cos, key):
                a3 = ang[:, 0:n].rearrange("p (a b) -> p a b", a=NTT)
                S.op('dve', lambda e: e.tensor_tensor(
                    out=a3, in0=pos_f[:, :].unsqueeze(2).to_broadcast([128, NTT, nf]),
                    in1=inv_ap.unsqueeze(1).to_broadcast([128, NTT, nf]), op=ALU.mult),
                    reads=['pos_f', 'ctab', 'tu'], writes=['ang'])
                for dst, off, dk in ((o_sin, 0.0, key + 's'), (o_cos, math.pi / 2, key + 'c')):
                    if off != 0.0:
                        S.op('dve', lambda e, off=off: e.tensor_scalar(out=ta[:, 0:n], in0=ang[:, 0:n], scalar1=off,
                                                                      scalar2=None, op0=ALU.add),
                             reads=['ang'], writes=['ta'])
                        src, sk = ta, 'ta'
                    else:
                        src, sk = ang, 'ang'
                    S.op('dve', lambda e, src=src: e.tensor_scalar(out=tu[:, 0:n], in0=src[:, 0:n],
                                                                  scalar1=1.0 / (2 * math.pi), scalar2=MAGIC,
                                                                  op0=ALU.mult, op1=ALU.add),
                         reads=[sk], writes=['tu'])
                    S.op('dve', lambda e: e.tensor_scalar(out=tu[:, 0:n], in0=tu[:, 0:n], scalar1=MAGIC, scalar2=None,
                                                          op0=ALU.subtract), reads=['tu'], writes=['tu'])
                    S.op('dve', lambda e, src=src: e.scalar_tensor_tensor(out=tr[:, 0:n], in0=tu[:, 0:n], scalar=-C1,
                                                                         in1=src[:, 0:n], op0=ALU.mult, op1=ALU.add),
                         reads=['tu', sk], writes=['tr'])
                    S.op('dve', lambda e: e.scalar_tensor_tensor(out=tr[:, 0:n], in0=tu[:, 0:n], scalar=-C2,
                                                                in1=tr[:, 0:n], op0=ALU.mult, op1=ALU.add),
                         reads=['tu', 'tr'], writes=['tr'])
                    S.op('dve', lambda e: e.tensor_scalar(out=tr[:, 0:n], in0=tr[:, 0:n], scalar1=-PI_S, scalar2=PI_S,
                                                          op0=ALU.max, op1=ALU.min), reads=['tr'], writes=['tr'])
                    S.op('act', lambda e, dst=dst: e.activation(out=dst[:, 0:n], in_=tr[:, 0:n], func=getattr(AF, _os.environ.get('SINFN', 'Sin'))),
                         reads=['tr'], writes=[dk])

            if tsub >= 1:
                sincos(512, 32, INVR, sr, cr, 'ropeR')
            if tsub >= 2:
                sincos(256, 16, INVM, sm, cm, 'ropeM')

            S.barrier()

    def ffn(gain_d, wg_d, wu_d, wd_d, tag):
        with ExitStack() as es:
            sb = lambda n, s, d: es.enter_context(nc.sbuf_tensor(tag + n, s, d))
            gbc = sb("gbc", [128, D], F32)
            junk = sb("junk", [128, D], BF16)
            xn = [sb("xn%d" % i, [128, D], BF16) for i in range(2)]
            wgb = [sb("wgb%d" % i, [128, 8, 256], BF16) for i in range(3)]
            wub = [sb("wub%d" % i, [128, 8, 256], BF16) for i in range(3)]
            wdb = [sb("wdb%d" % i, [128, 6, D], BF16) for i in range(2)]
            actT = sb("actT", [128, 6, NT], BF16)
            sil = [sb("sil%d" % i, [128, 512], F32) for i in range(2)]
            wgv = wg_d.rearrange("(kc p) f -> p kc f", p=128)
            wuv = wu_d.rearrange("(kc p) f -> p kc f", p=128)
            wdv = wd_d.rearrange("(fc p) d -> p fc d", p=128)

            def load_gran(g):
                s = g % 3
                S.dma('pool', wgb[s][:], wgv[:, :, g * 256:(g + 1) * 256], writes=[('wg', s)])
                S.dma('pool', wub[s][:], wuv[:, :, g * 256:(g + 1) * 256], writes=[('wu', s)])

            def load_wd(p):
                a, n = FFN_PARTS[p]
                s = p % 2
                S.dma('pool', wdb[s][:, 0:n, :], wdv[:, a:a + n, :], writes=[('wd', s)])

            load_gran(0); load_gran(1); load_gran(2)
            load_wd(0); load_wd(1)
            norm_to_XT(gain_d, gbc, junk, xn)
            step = 0
            dstep = 0
            for p, (a, n) in enumerate(FFN_PARTS):
                for g in range(a // 2, (a + n) // 2):
                    s = g % 3
                    for fl in range(2):
                        fcp = (g * 2 + fl) - a
                        for tg in range(4):
                            bg = (step % 2) * 2
                            bu = bg + 1
                            xr = [('XT', tg * 4 + i) for i in range(4)]
                            for k in range(8):
                                S.op('pe', lambda e, k=k, s=s, fl=fl, tg=tg, bg=bg: e.matmul(
                                    bank[bg][:], lhsT=wgb[s][:, k, fl * 128:(fl + 1) * 128],
                                    rhs=XT[:, k, tg * 512:(tg + 1) * 512], start=(k == 0), stop=(k == 7)),
                                    reads=[('wg', s)] + xr, writes=[('bank', bg)], inc=(k == 7))
                            for k in range(8):
                                S.op('pe', lambda e, k=k, s=s, fl=fl, tg=tg, bu=bu: e.matmul(
                                    bank[bu][:], lhsT=wub[s][:, k, fl * 128:(fl + 1) * 128],
                                    rhs=XT[:, k, tg * 512:(tg + 1) * 512], start=(k == 0), stop=(k == 7)),
                                    reads=[('wu', s)] + xr, writes=[('bank', bu)], inc=(k == 7))
                            sl = step % 2
                            S.op('act', lambda e, sl=sl, bg=bg: e.activation(out=sil[sl][:], in_=bank[bg][:], func=AF.Silu),
                                 reads=[('bank', bg)], writes=[('sil', sl)])
                            S.op('dve', lambda e, sl=sl, bu=bu, fcp=fcp, tg=tg: e.tensor_tensor(
                                out=actT[:, fcp, tg * 512:(tg + 1) * 512], in0=sil[sl][:], in1=bank[bu][:], op=ALU.mult),
                                reads=[('sil', sl), ('bank', bu)], writes=[('actT', fcp, tg)])
                            step += 1
                    if g + 3 < NFC // 2:
                        load_gran(g + 3)
                s = p % 2
                for tt in range(NTT):
                    for half in range(2):
                        bd = 4 + (dstep % 2)
                        for f in range(n):
                            S.op('pe', lambda e, f=f, tt=tt, half=half, bd=bd, s=s: e.matmul(
                                bank[bd][:], lhsT=actT[:, f, tt * 128:(tt + 1) * 128],
                                rhs=wdb[s][:, f, half * 512:(half + 1) * 512], start=(f == 0), stop=(f == n - 1)),
                                reads=[('actT', f, tt // 4), ('wd', s)], writes=[('bank', bd)], inc=(f == n - 1))
                        S.op('dve', lambda e, tt=tt, half=half, bd=bd: e.scalar_tensor_tensor(
                            out=h[:, tt, half * 512:(half + 1) * 512], in0=bank[bd][:], scalar=0.5,
                            in1=h[:, tt, half * 512:(half + 1) * 512], op0=ALU.mult, op1=ALU.add),
                            reads=[('bank', bd), ('h', tt)], writes=[('h', tt)])
                        dstep += 1
                if p + 2 < len(FFN_PARTS):
                    load_wd(p + 2)
            S.barrier()

    def store_h_raw():
        for tt in range(NTT):
            S.dma('sp', ov[:, tt, :], h[:, tt, :], reads=[('h', tt)], writes=[('out', tt)])
            out_keys.append(('out', tt))

    def final_norm_store():
        with ExitStack() as es:
            sb = lambda n, s, d: es.enter_context(nc.sbuf_tensor("fin_" + n, s, d))
            gbc = sb("gbc", [128, D], F32)
            junk = sb("junk", [128, D], BF16)
            obuf = [sb("obuf%d" % i, [128, D], F32) for i in range(2)]
            norm_stats(g_fin, gbc, junk)
            for tt in range(NTT):
                b = tt % 2
                S.op('dve', lambda e, tt=tt, b=b: e.scalar_tensor_tensor(
                    out=obuf[b][:], in0=h[:, tt, :], scalar=rstd[:, tt:tt + 1], in1=gbc[:],
                    op0=ALU.mult, op1=ALU.mult),
                    reads=[('h', tt), 'ss_rstd', 'gbc'], writes=[('obuf', b)])
                S.dma('sp', ov[:, tt, :], obuf[b][:], reads=[('obuf', b)], writes=[('out', tt)])
                out_keys.append(('out', tt))
            S.wait_all('sp', out_keys)
            S.barrier()

    def middle():
        w_inv = w_in.rearrange("(kc p) n -> p kc n", p=128)
        with ExitStack() as esm:
            sbm = lambda n, s, d: esm.enter_context(nc.sbuf_tensor("m_" + n, s, d))
            geo = sbm("geo", [128, 8], F32)
            LGrow = sbm("LGrow", [128, 16], F32); LG = sbm("LG", [128, 8], F32)
            KD = sbm("KD", [128, 16], F32); gC = sbm("gC", [128, 8], F32)
            DCAR = sbm("DCAR", [128, 128], F32); coef = sbm("coef", [128, 32], F32)
            ones32 = sbm("ones32", [128, 128], F32)
            RF = ctab[:, 0:128]; RB = ctab[:, 128:256]; EI = ctab[:, 256:384]
            E2 = ctab[:, 384:400]; M2 = ctab[:, 400:416]
            cr3 = cr[:, :].rearrange("p (a b) -> p a b", a=NTT); sr3 = sr[:, :].rearrange("p (a b) -> p a b", a=NTT)
            cm3 = cm[:, :].rearrange("p (a b) -> p a b", a=NTT); sm3 = sm[:, :].rearrange("p (a b) -> p a b", a=NTT)

            S.dma('sp', geo[:], geo_d, writes=['geo'])
            S.op('dve', lambda e: e.memset(ones32[:], 1.0), writes=['ones32'])

            with ExitStack() as es:
                sb = lambda n, s, d: es.enter_context(nc.sbuf_tensor("t_" + n, s, d))
                raw = sb("raw", [128, 16], F32); e1 = sb("e1", [128, 16], F32)
                t16 = sb("t16", [128, 16], F32);
                t8 = sb("t8", [128, 8], F32)
                if tsub >= 3:
                    S.dma('sp', raw[:], dec_d[0, :].partition_broadcast(128), writes=['raw'])
                    S.op('act', lambda e: e.activation(out=e1[:], in_=raw[:], func=AF.Exp, scale=-1.0), reads=['raw'], writes=['e1'])
                    S.op('act', lambda e: e.activation(out=e1[:], in_=e1[:], func=AF.Ln, bias=1.0), reads=['e1'], writes=['e1'])
                    S.op('dve', lambda e: e.tensor_scalar(out=LGrow[:], in0=e1[:], scalar1=-1.0, scalar2=None, op0=ALU.mult),
                         reads=['e1'], writes=['LGrow'])
                    S.op('dve', lambda e: e.tensor_copy(out=LG[0:64, :], in_=LGrow[0:64, 0:8]), reads=['LGrow'], writes=['LGa'])
                    S.op('dve', lambda e: e.tensor_copy(out=LG[64:128, :], in_=LGrow[64:128, 8:16]), reads=['LGrow'], writes=['LGb'])
                    LGk = ['LGa', 'LGb']
                if tsub >= 4:
                    S.op('dve', lambda e: e.tensor_tensor(out=t16[:], in0=LGrow[:], in1=E2, op=ALU.mult),
                         reads=['LGrow', 'ctab'], writes=['t16'])
                    S.op('act', lambda e: e.activation(out=KD[:], in_=t16[:], func=AF.Exp, bias=LN8), reads=['t16'], writes=['KD'])
                if tsub >= 5:
                    S.op('act', lambda e: e.activation(out=gC[:], in_=LG[:], func=AF.Exp, scale=128.0), reads=LGk, writes=['gC'])
                    d3 = DCAR[:, :].rearrange("p (c hh) -> p c hh", c=NTT)
                    S.op('dve', lambda e: e.tensor_tensor(out=d3, in0=M2.unsqueeze(2).to_broadcast([128, NTT, 8]),
                                                          in1=LG[:, :].unsqueeze(1).to_broadcast([128, NTT, 8]), op=ALU.mult),
                         reads=['ctab'] + LGk, writes=['DCAR'])
                    S.op('act', lambda e: e.activation(out=DCAR[:], in_=DCAR[:], func=AF.Exp), reads=['DCAR'], writes=['DCAR'])
                if tsub >= 6:
                    for r_ in range(4):
                        S.op('dve', lambda e, r_=r_: e.tensor_scalar(out=t8[:], in0=LG[:], scalar1=geo[:, r_:r_ + 1], scalar2=None,
                                                                     op0=ALU.mult), reads=LGk + ['geo'], writes=['t8'])
                        S.op('act', lambda e, r_=r_: e.activation(out=coef[:, r_ * 8:(r_ + 1) * 8], in_=t8[:], func=AF.Exp, scale=2048.0),
                             reads=['t8'], writes=[('coef', r_)])
                        S.op('dve', lambda e, r_=r_: e.tensor_scalar(out=coef[:, r_ * 8:(r_ + 1) * 8], in0=coef[:, r_ * 8:(r_ + 1) * 8],
                                                                     scalar1=geo[:, 4 + r_:5 + r_], scalar2=None, op0=ALU.mult),
                             reads=[('coef', r_), 'geo'], writes=[('coef', r_)])

                S.barrier()
                if stop == 'tables':
                    S.dma('sp', dbg32[:, 0:512], cr[:], reads=['ropeRc'], writes=['dbg_a'])
                    S.dma('sp', dbg32[:, 512:1024], sr[:], reads=['ropeRs'], writes=['dbg_b'])
                    S.dma('sp', dbg32[:, 1024:1280], cm[:], reads=['ropeMc'], writes=['dbg_c'])
                    S.dma('sp', dbg32[:, 1280:1536], sm[:], reads=['ropeMs'], writes=['dbg_d'])
                    S.dma('sp', dbg32[:, 1536:1552], KD[:], reads=['KD'], writes=['dbg_e'])
                    S.dma('sp', dbg32[:, 1552:1560], LG[:], reads=['LGa', 'LGb'], writes=['dbg_f'])
                    S.dma('sp', dbg32[:, 1560:1688], DCAR[:], reads=['DCAR'], writes=['dbg_g'])
                    S.dma('sp', dbg32[:, 1688:1720], coef[:], reads=[('coef', i) for i in range(4)], writes=['dbg_h'])
                    S.dma('sp', dbg32[:, 1720:1728], gC[:], reads=['gC'], writes=['dbg_i'])
                    out_keys.extend(['dbg_' + c for c in 'abcdefghi'])
                    raise _Stop()

            TAB = ['ropeRs', 'ropeRc', 'ropeMs', 'ropeMc']

            def rope_tok(x4, o4, c3t, s3t, nh, hf, tmps, rk):
                cb = c3t.unsqueeze(1).to_broadcast([128, nh, hf])
                sb_ = s3t.unsqueeze(1).to_broadcast([128, nh, hf])
                n = nh * hf
                tv = [t[:, 0:n].rearrange("p (a b) -> p a b", a=nh) for t in tmps]
                x1 = x4[:, :, 0, :]; x2 = x4[:, :, 1, :]
                rd, wr = rk
                S.op('dve', lambda e: e.tensor_tensor(out=tv[0], in0=x1, in1=cb, op=ALU.mult), reads=rd + TAB, writes=['rt0'])
                S.op('dve', lambda e: e.tensor_tensor(out=tv[1], in0=x2, in1=sb_, op=ALU.mult), reads=rd + TAB, writes=['rt1'])
                S.op('dve', lambda e: e.tensor_tensor(out=o4[:, :, 0, :], in0=tv[0], in1=tv[1], op=ALU.subtract),
                     reads=['rt0', 'rt1'], writes=wr)
                S.op('dve', lambda e: e.tensor_tensor(out=tv[2], in0=x1, in1=sb_, op=ALU.mult), reads=rd + TAB, writes=['rt2'])
                S.op('dve', lambda e: e.tensor_tensor(out=tv[3], in0=x2, in1=cb, op=ALU.mult), reads=rd + TAB, writes=['rt3'])
                S.op('dve', lambda e: e.tensor_tensor(out=o4[:, :, 1, :], in0=tv[2], in1=tv[3], op=ALU.add),
                     reads=['rt2', 'rt3'], writes=wr)

            cqnT = esm.enter_context(nc.sbuf_tensor("m_cqnT", [128, 2, NT], BF16))
            with ExitStack() as es:
                sb = lambda n, s, d: es.enter_context(nc.sbuf_tensor("p1_" + n, s, d))
                gbc = sb("gbc", [128, D], F32); junk = sb("junk", [128, D], BF16)
                xn = [sb("xn%d" % i, [128, D], BF16) for i in range(2)]
                W1 = sb("W1", [128, 8, 416], BF16)
                cqkv = sb("cqkv", [128, NTT, 416], F32)
                nrm = [sb("nrm%d" % i, [128, 512], BF16) for i in range(2)]
                ckvnT_loc = sb("ckvnT_loc", [128, NT], BF16)
                kpeT_loc = sb("kpeT_loc", [32, NT], BF16)
                gq = sb("gq", [128, 256], F32); gkv = sb("gkv", [128, 128], F32)
                kpe_r = sb("kpe_r", [128, NTT, 32], BF16)
                ssq = sb("ssq", [128, NTT], F32); sskv = sb("sskv", [128, NTT], F32)
                v1 = sb("v1", [128, NTT], F32); s1 = sb("s1", [128, NTT], F32)
                rq = sb("rq", [128, NTT], F32); rkv = sb("rkv", [128, NTT], F32)
                tmps = [sb("rt%d" % i, [128, 256], F32) for i in range(4)]
                for i_ in range(2):
                    S.op('dve', lambda e, i_=i_: e.memset(nrm[i_][:], 0.0), writes=[('nrmq', i_), ('nrmk', i_), ('nrmp', i_)])
                S.dma('pool', W1[:], w_inv[:, :, 0:416], writes=['W1'])
                S.dma('sp', gq[:], g_q[0, :].partition_broadcast(128), writes=['gq'])
                S.dma('sp', gkv[:], g_kv[0, :].partition_broadcast(128), writes=['gkv'])
                norm_to_XT(g_mix, gbc, junk, xn)
                for tt in range(NTT):
                    b = tt % 2
                    for k in range(8):
                        S.op('pe', lambda e, k=k, tt=tt, b=b: e.matmul(
                            bank[b][:, 0:416], lhsT=XT[:, k, tt * 128:(tt + 1) * 128], rhs=W1[:, k, :],
                            start=(k == 0), stop=(k == 7)),
                            reads=[('XT', tt), 'W1'], writes=[('bank', b)], inc=(k == 7))
                    S.op('act', lambda e, tt=tt, b=b: e.activation(out=cqkv[:, tt, :], in_=bank[b][:, 0:416], func=AF.Copy),
                         reads=[('bank', b)], writes=[('cqkv', tt)])
                    S.op('dve', lambda e, tt=tt: e.scalar_tensor_tensor(
                        out=junk[:, 0:256], in0=cqkv[:, tt, 0:256], scalar=1.0, in1=cqkv[:, tt, 0:256],
                        op0=ALU.mult, op1=ALU.mult, accum_out=ssq[:, tt:tt + 1]), reads=[('cqkv', tt)], writes=[('ssq', tt)])
                    S.op('dve', lambda e, tt=tt: e.scalar_tensor_tensor(
                        out=junk[:, 256:384], in0=cqkv[:, tt, 256:384], scalar=1.0, in1=cqkv[:, tt, 256:384],
                        op0=ALU.mult, op1=ALU.mult, accum_out=sskv[:, tt:tt + 1]), reads=[('cqkv', tt)], writes=[('sskv', tt)])
                if tsub == 10:
                    S.barrier(); raise _Stop()
                rstd_batch(ssq, NTT, 1.0 / 256, v1, s1, rq, 'ssq')
                rstd_batch(sskv, NTT, 1.0 / 128, v1, s1, rkv, 'sskv')
                x4 = cqkv[:, :, 384:416].rearrange("p t (two j) -> p t two j", two=2)
                o4 = kpe_r[:, :, :].rearrange("p t (two j) -> p t two j", two=2)
                tv = [t[:, 0:256].rearrange("p (a b) -> p a b", a=NTT) for t in tmps]
                allq = [('cqkv', t) for t in range(NTT)]
                S.op('dve', lambda e: e.tensor_tensor(out=tv[0], in0=x4[:, :, 0, :], in1=cm3, op=ALU.mult), reads=allq + TAB, writes=['rt0'])
                S.op('dve', lambda e: e.tensor_tensor(out=tv[1], in0=x4[:, :, 1, :], in1=sm3, op=ALU.mult), reads=allq + TAB, writes=['rt1'])
                S.op('dve', lambda e: e.tensor_tensor(out=o4[:, :, 0, :], in0=tv[0], in1=tv[1], op=ALU.subtract), reads=['rt0', 'rt1'], writes=['kpe_r'])
                S.op('dve', lambda e: e.tensor_tensor(out=tv[2], in0=x4[:, :, 0, :], in1=sm3, op=ALU.mult), reads=allq + TAB, writes=['rt2'])
                S.op('dve', lambda e: e.tensor_tensor(out=tv[3], in0=x4[:, :, 1, :], in1=cm3, op=ALU.mult), reads=allq + TAB, writes=['rt3'])
                S.op('dve', lambda e: e.tensor_tensor(out=o4[:, :, 1, :], in0=tv[2], in1=tv[3], op=ALU.add), reads=['rt2', 'rt3'], writes=['kpe_r'])
                if tsub == 11:
                    S.barrier(); raise _Stop()
                for tt in range(NTT):
                    b = tt % 2
                    S.op('dve', lambda e, tt=tt, b=b: e.scalar_tensor_tensor(
                        out=nrm[b][:, 0:256], in0=cqkv[:, tt, 0:256], scalar=rq[:, tt:tt + 1], in1=gq[:],
                        op0=ALU.mult, op1=ALU.mult), reads=[('cqkv', tt), 'ssq_rstd', 'gq'], writes=[('nrmq', b)])
                    S.op('dve', lambda e, tt=tt, b=b: e.scalar_tensor_tensor(
                        out=nrm[b][:, 256:384], in0=cqkv[:, tt, 256:384], scalar=rkv[:, tt:tt + 1], in1=gkv[:],
                        op0=ALU.mult, op1=ALU.mult), reads=[('cqkv', tt), 'sskv_rstd', 'gkv'], writes=[('nrmk', b)])
                    pb = 6 + b
                    pv = bankbf[pb]
                    S.op('act', lambda e, tt=tt, b=b: e.activation(out=nrm[b][:, 384:416], in_=kpe_r[:, tt, :], func=AF.Copy),
                         reads=['kpe_r'], writes=[('nrmp', b)])
                    for j in range(4):
                        S.op('pe', lambda e, j=j, b=b, pv=pv: e.transpose(
                            out=pv[:, j * 128:(j + 1) * 128], in_=nrm[b][:, j * 128:(j + 1) * 128], identity=ident[:]),
                            reads=[('nrmq', b), ('nrmk', b), ('nrmp', b), 'ident'], writes=[('bank', pb)], inc=(j == 3))
                    S.op('act', lambda e, tt=tt, pv=pv: e.activation(
                        out=cqnT[:, :, tt * 128:(tt + 1) * 128],
                        in_=pv[:, 0:256].rearrange("p (k t) -> p k t", k=2), func=AF.Copy),
                        reads=[('bank', pb)], writes=[('cqnT', tt)])
                    S.op('act', lambda e, tt=tt, pv=pv: e.activation(out=ckvnT_loc[:, tt * 128:(tt + 1) * 128], in_=pv[:, 256:384], func=AF.Copy),
                         reads=[('bank', pb)], writes=['ckvnT_loc'])
                    S.op('act', lambda e, tt=tt, pv=pv: e.activation(out=kpeT_loc[0:32, tt * 128:(tt + 1) * 128], in_=pv[0:32, 384:512], func=AF.Copy),
                         reads=[('bank', pb)], writes=['kpeT_loc'])
                if tsub == 12:
                    S.barrier(); raise _Stop()
                S.dma('sp', xin[0:128, :], ckvnT_loc[:], reads=['ckvnT_loc'], writes=['xin_a'])
                S.dma('sp', xin[128:160, :], kpeT_loc[0:32, :], reads=['kpeT_loc'], writes=['xin_b'])
                if tsub == 13:
                    S.barrier(); raise _Stop()
                if tsub == 14:
                    S.barrier(); raise _Stop()
                if stop == 'p1':
                    S.dma('sp', dbgbf[:, 0:2048], ckvnT_loc[:], reads=['ckvnT_loc'], writes=['dbg_a'])
                    S.dma('sp', dbgbf[0:32, 2048:4096], kpeT_loc[0:32, :], reads=['kpeT_loc'], writes=['dbg_b'])
                    S.dma('sp', dbgbf[:, 4096:8192], cqnT[:, :, :].rearrange("p a t -> p (a t)"), reads=[('cqnT', t) for t in range(NTT)], writes=['dbg_c'])
                    out_keys.extend(['dbg_' + c for c in 'abc'])
                S.barrier()
                if stop == 'p1':
                    raise _Stop()

            esT = ExitStack()
            T_all = esT.enter_context(nc.sbuf_tensor("m_T_all", [128, NTT, 512], BF16))
            DT = esT.enter_context(nc.sbuf_tensor("m_DT", [128, 8, 128], BF16))
            QD = esT.enter_context(nc.sbuf_tensor("m_QD", [128, 8, 128], BF16))
            with ExitStack() as es:
                sb = lambda n, s, d: es.enter_context(nc.sbuf_tensor("dt_" + n, s, d))
                t128 = sb("t128", [128, 128], F32); t128b = sb("t128b", [128, 128], F32)
                LGk = ['LGa', 'LGb']
                for hh in range(8):
                    S.op('dve', lambda e, hh=hh: e.tensor_scalar(out=t128[:], in0=RF, scalar1=LGrow[:, hh:hh + 1], scalar2=None,
                                                                 op0=ALU.mult), reads=['ctab', 'LGrow'], writes=['t128'])
                    S.op('dve', lambda e, hh=hh: e.scalar_tensor_tensor(out=t128b[:], in0=RB, scalar=LGrow[:, 8 + hh:9 + hh],
                                                                        in1=t128[:], op0=ALU.mult, op1=ALU.add),
                         reads=['ctab', 'LGrow', 't128'], writes=['t128b'])
                    S.op('act', lambda e, hh=hh: e.activation(out=DT[:, hh, :], in_=t128b[:], func=AF.Exp, bias=LN8),
                         reads=['t128b'], writes=[('DT', hh)])
                    S.op('act', lambda e, hh=hh: e.activation(out=QD[:, hh, :], in_=EI, func=AF.Exp, scale=LG[:, hh:hh + 1]),
                         reads=['ctab'] + LGk, writes=[('QD', hh)])
                S.barrier()
                if tsub == 20:
                    raise _Stop()

            with ExitStack() as es:
                sb = lambda n, s, d: es.enter_context(nc.sbuf_tensor("ra_" + n, s, d))
                wkr = sb("wkr", [128, 8, 512], BF16); wvr = sb("wvr", [128, 8, 512], BF16)
                rk_c = [sb("rk_c%d" % i, [128, 512], BF16) for i in range(2)]
                rv_c = [sb("rv_c%d" % i, [128, 512], BF16) for i in range(2)]
                kdd = [sb("kdd%d" % i, [128, 8, 128], BF16) for i in range(2)]
                tmps = [sb("rt%d" % i, [128, 256], F32) for i in range(4)]
                R = [sb("R%d" % i, [128, 512], F32) for i in range(2)]
                tR = sb("tR", [128, 512], F32)
                S.dma('pool', wkr[:], w_inv[:, :, 928:1440], writes=['wkr'])
                S.dma('pool', wvr[:], w_inv[:, :, 1440:1952], writes=['wvr'])
                S.collective("AllGather", xin.ap().opt(), xout.ap().opt(), GROUPS, reads=['xin_a', 'xin_b'], writes=['xout'], name="ag1")
                S.op('dve', lambda e: e.memset(R[0][:], 0.0), writes=[('R', 0, 'f'), ('R', 0, 'b')])
                gCb = gC[:, :].unsqueeze(2).to_broadcast([128, 8, 64])
                KDf = KD[:, 0:8].unsqueeze(2).to_broadcast([128, 8, 64])
                KDb = KD[:, 8:16].unsqueeze(2).to_broadcast([128, 8, 64])

                def scan_step(c, i, dirn):
                    lo, hi = (0, 64) if dirn == 'f' else (64, 128)
                    cur, nxt = R[i % 2], R[(i + 1) % 2]
                    S.op('dve', lambda e: e.tensor_tensor(
                        out=tR[lo:hi, :].rearrange("p (a b) -> p a b", a=8), in0=cur[lo:hi, :].rearrange("p (a b) -> p a b", a=8),
                        in1=gCb[lo:hi], op=ALU.mult), reads=[('R', i % 2, dirn), 'gC'], writes=[('tR', dirn)])
                    S.op('dve', lambda e: e.tensor_tensor(out=nxt[lo:hi, :], in0=tR[lo:hi, :], in1=T_all[lo:hi, c, :], op=ALU.add),
                         reads=[('tR', dirn), ('T', c, dirn)], writes=[('R', (i + 1) % 2, dirn)])
                    S.op('act', lambda e: e.activation(out=T_all[lo:hi, c, :], in_=cur[lo:hi, :], func=AF.Copy),
                         reads=[('R', i % 2, dirn)], writes=[('T', c, dirn)])

                def retA_proj(c):
                    b = c % 2
                    bk, bv, bu = b, 2 + b, 4 + b
                    for k in range(8):
                        S.op('pe', lambda e, k=k, c=c, bk=bk: e.matmul(
                            bank[bk][:], lhsT=XT[:, k, c * 128:(c + 1) * 128], rhs=wkr[:, k, :], start=(k == 0), stop=(k == 7)),
                            reads=[('XT', c), 'wkr'], writes=[('bank', bk)], inc=(k == 7))
                    for k in range(8):
                        S.op('pe', lambda e, k=k, c=c, bv=bv: e.matmul(
                            bank[bv][:], lhsT=XT[:, k, c * 128:(c + 1) * 128], rhs=wvr[:, k, :], start=(k == 0), stop=(k == 7)),
                            reads=[('XT', c), 'wvr'], writes=[('bank', bv)], inc=(k == 7))

                def retA_dve(c):
                    b = c % 2
                    bk, bv, bu = b, 2 + b, 4 + b
                    x4 = bank[bk][:, :].rearrange("p (hh two j) -> p hh two j", hh=8, two=2)
                    o4 = rk_c[b][:, :].rearrange("p (hh two j) -> p hh two j", hh=8, two=2)
                    rope_tok(x4, o4, cr3[:, c, :], sr3[:, c, :], 8, 32, tmps, ([('bank', bk)], [('rk_c', b)]))
                    S.op('act', lambda e, b=b, bv=bv: e.activation(out=rv_c[b][:], in_=bank[bv][:], func=AF.Copy),
                         reads=[('bank', bv)], writes=[('rv_c', b)])
                    rk3 = rk_c[b][:, :].rearrange("p (a d) -> p a d", a=8)
                    S.op('dve', lambda e, b=b, rk3=rk3: e.tensor_tensor(out=kdd[b][:, :, 0:64], in0=rk3, in1=KDf, op=ALU.mult),
                         reads=[('rk_c', b), 'KD'], writes=[('kdd', b)])
                    S.op('dve', lambda e, b=b, rk3=rk3: e.tensor_tensor(out=kdd[b][:, :, 64:128], in0=rk3, in1=KDb, op=ALU.mult),
                         reads=[('rk_c', b), 'KD'], writes=[('kdd', b)])

                def retA_state(c):
                    b = c % 2
                    bk, bv, bu = b, 2 + b, 4 + b
                    for hh in range(8):
                        S.op('pe', lambda e, hh=hh, b=b, bu=bu: e.matmul(
                            bank[bu][:, hh * 64:(hh + 1) * 64], lhsT=kdd[b][:, hh, :], rhs=rv_c[b][:, hh * 64:(hh + 1) * 64],
                            start=True, stop=True), reads=[('kdd', b), ('rv_c', b)], writes=[('bank', bu)], inc=(hh == 7))
                    S.op('act', lambda e, c=c, bu=bu: e.activation(out=T_all[:, c, :], in_=bank[bu][:], func=AF.Copy),
                         reads=[('bank', bu)], writes=[('T', c, 'f'), ('T', c, 'b')])
                    scan_step(c, c, 'f')

                retA_proj(0)
                for c in range(NTT):
                    retA_dve(c)
                    if c + 1 < NTT:
                        retA_proj(c + 1)
                    retA_state(c)
                if tsub == 25:
                    S.barrier(); raise _Stop()
                for i in range(NTT):
                    scan_step(NTT - 1 - i, i, 'b')
                    if tsub == 26 and i == 0:
                        S.barrier(); raise _Stop()
                if tsub == 27:
                    S.barrier(); raise _Stop()
                S.dma('sp', xin2[:, :], R[0][:], reads=[('R', 0, 'f'), ('R', 0, 'b')], writes=['xin2'])
                if tsub == 28:
                    S.barrier(); raise _Stop()
                if tsub == 29:
                    S.barrier(); raise _Stop()
                if stop == 'reta':
                    S.dma('sp', dbgbf[:, 0:8192], T_all[:, :, :].rearrange("p a t -> p (a t)"),
                          reads=[('T', c, d_) for c in range(NTT) for d_ in 'fb'], writes=['dbg_a'])
                    S.dma('sp', dbg32[:, 0:512], R[0][:], reads=[('R', 0, 'f'), ('R', 0, 'b')], writes=['dbg_b'])
                    out_keys.extend(['dbg_a', 'dbg_b'])
                S.barrier()
                if stop == 'reta':
                    raise _Stop()

            with ExitStack() as es:
                sb = lambda n, s, d: es.enter_context(nc.sbuf_tensor("rb_" + n, s, d))
                wqr = sb("wqr", [128, 8, 512], BF16); wkr = sb("wkr", [128, 8, 512], BF16)
                wvr = sb("wvr", [128, 8, 512], BF16); wgr = sb("wgr", [128, 8, 512], BF16)
                carry = sb("carry", [128, 512], F32)
                tC = sb("tC", [128, 512], F32)
                rq_dup = [sb("rq_dup%d" % i, [128, 8, 128], BF16) for i in range(2)]
                rk_c = [sb("rk_c%d" % i, [128, 512], BF16) for i in range(2)]
                rv_c = [sb("rv_c%d" % i, [128, 512], BF16) for i in range(2)]
                qTp = [sb("qTp%d" % i, [128, 8, 128], BF16) for i in range(2)]
                qdT = [sb("qdT%d" % i, [128, 8, 128], BF16) for i in range(2)]
                kT = [sb("kT%d" % i, [128, 4, 128], BF16) for i in range(2)]
                sTm = sb("sTm", [128, 8, 128], BF16)
                Tbf = sb("Tbf", [128, 512], BF16)
                tmps = [sb("rt%d" % i, [128, 256], F32) for i in range(4)]
                osq = sb("osq", [128, 512], F32); ocen = sb("ocen", [128, 512], F32)
                sg = sb("sg", [128, 512], F32); rout = sb("rout", [128, 512], BF16)
                st_s = sb("st_s", [128, 8], F32); st_q = sb("st_q", [128, 8], F32)
                st_m = sb("st_m", [128, 8], F32); st_v = sb("st_v", [128, 8], F32); st_r = sb("st_r", [128, 8], F32)
                mhalf = sb("mhalf", [128, 8], F32)
                S.dma('pool', wqr[:], w_inv[:, :, 416:928], writes=['wqr'])
                S.dma('pool', wkr[:], w_inv[:, :, 928:1440], writes=['wkr'])
                S.dma('pool', wvr[:], w_inv[:, :, 1440:1952], writes=['wvr'])
                S.dma('pool', wgr[:], w_inv[:, :, 1952:2464], writes=['wgr'])
                S.collective("AllGather", xin2.ap().opt(), xout2.ap().opt(), GROUPS, reads=['xin2'], writes=['xout2'], name="ag2")
                S.op('dve', lambda e: e.memset(mhalf[:], -0.5), writes=['mhalf'])
                d3 = DCAR[:, :].rearrange("p (c hh) -> p c hh", c=NTT)
                sT4 = sTm[:, :, :].rearrange("p (a two) t -> p a two t", two=2)
                DT4 = DT[:, :, :].rearrange("p (a two) t -> p a two t", two=2)

                def retB_proj(c):
                    for (bk, wt, wk_) in ((0, wqr, 'wqr'), (1, wkr, 'wkr'), (2, wvr, 'wvr')):
                        for k in range(8):
                            S.op('pe', lambda e, k=k, bk=bk, wt=wt: e.matmul(
                                bank[bk][:], lhsT=XT[:, k, c * 128:(c + 1) * 128], rhs=wt[:, k, :], start=(k == 0), stop=(k == 7)),
                                reads=[('XT', c), wk_], writes=[('bank', bk)], inc=(k == 7))

                def retB_rest(c):
                    p = c % 2
                    x4 = bank[0][:, :].rearrange("p (hh two j) -> p hh two j", hh=8, two=2)
                    o4 = rq_dup[p][:, :, 0:64].rearrange("p hh (two j) -> p hh two j", two=2)
                    rope_tok(x4, o4, cr3[:, c, :], sr3[:, c, :], 8, 32, tmps, ([('bank', 0)], [('rq_a', p)]))
                    yield
                    S.op('act', lambda e: e.activation(out=rq_dup[p][:, :, 64:128], in_=rq_dup[p][:, :, 0:64], func=AF.Copy),
                         reads=[('rq_a', p)], writes=[('rq_b', p)])
                    yield
                    x4 = bank[1][:, :].rearrange("p (hh two j) -> p hh two j", hh=8, two=2)
                    o4 = rk_c[p][:, :].rearrange("p (hh two j) -> p hh two j", hh=8, two=2)
                    rope_tok(x4, o4, cr3[:, c, :], sr3[:, c, :], 8, 32, tmps, ([('bank', 1)], [('rk_c', p)]))
                    yield
                    S.op('act', lambda e: e.activation(out=rv_c[p][:], in_=bank[2][:], func=AF.Copy),
                         reads=[('bank', 2)], writes=[('rv_c', p)])
                    yield
                    pq = bankbf[3]
                    for hh in range(8):
                        S.op('pe', lambda e, hh=hh: e.transpose(out=pq[:, hh * 128:(hh + 1) * 128], in_=rq_dup[p][:, hh, :], identity=ident[:]),
                             reads=[('rq_a', p), ('rq_b', p), 'ident'], writes=[('bank', 3)], inc=(hh == 7))
                    yield
                    pk = bankbf[4]
                    for pr in range(4):
                        S.op('pe', lambda e, pr=pr: e.transpose(out=pk[:, pr * 128:(pr + 1) * 128], in_=rk_c[p][:, pr * 128:(pr + 1) * 128],
                                                              identity=ident[:]),
                             reads=[('rk_c', p), 'ident'], writes=[('bank', 4)], inc=(pr == 3))
                    yield
                    S.op('act', lambda e: e.activation(out=qTp[p][:, :, :], in_=pq[:, 0:1024].rearrange("p (a t) -> p a t", a=8), func=AF.Copy),
                         reads=[('bank', 3)], writes=[('qTp', p)])
                    yield
                    S.op('dve', lambda e: e.tensor_tensor(out=qdT[p][:, :, :], in0=qTp[p][:, :, :], in1=QD[:, :, :], op=ALU.mult),
                         reads=[('qTp', p)] + [('QD', i) for i in range(8)], writes=[('qdT', p)])
                    yield
                    S.op('act', lambda e: e.activation(out=kT[p][:, :, :], in_=pk[:, 0:512].rearrange("p (a t) -> p a t", a=4), func=AF.Copy),
                         reads=[('bank', 4)], writes=[('kT', p)])

                def retB_back(c):
                    p = c % 2
                    for hh in range(8):
                        lo = (hh % 2) * 64
                        sbk = 5 + (hh % 2)
                        S.op('pe', lambda e, hh=hh, lo=lo, sbk=sbk: e.matmul(
                            bank[sbk][:, (hh // 2) * 128:(hh // 2 + 1) * 128], lhsT=kT[p][lo:lo + 64, hh // 2, :],
                            rhs=qTp[p][lo:lo + 64, hh, :], start=True, stop=True),
                            reads=[('kT', p), ('qTp', p)], writes=[('bank', sbk)], inc=(hh >= 6))
                    yield
                    for g4 in range(2):
                        S.op('dve', lambda e, g4=g4: e.tensor_tensor(
                            out=sT4[:, :, g4, :], in0=bank[5 + g4][:, :].rearrange("p (a t) -> p a t", a=4),
                            in1=DT4[:, :, g4, :], op=ALU.mult),
                            reads=[('bank', 5 + g4)] + [('DT', i) for i in range(8)], writes=[('sTm', g4)])
                    S.op('dve', lambda e: e.tensor_tensor(
                        out=tC[:, :].rearrange("p (a b) -> p a b", a=8), in0=carry[:, :].rearrange("p (a b) -> p a b", a=8),
                        in1=d3[:, c, :].unsqueeze(2).to_broadcast([128, 8, 64]), op=ALU.mult),
                        reads=['carry', 'DCAR'], writes=['tC'])
                    yield
                    S.op('dve', lambda e: e.tensor_tensor(out=Tbf[:], in0=tC[:], in1=T_all[:, c, :], op=ALU.add),
                         reads=['tC', ('T', c, 'f'), ('T', c, 'b')], writes=['Tbf'])
                    yield
                    for hh in range(8):
                        S.op('pe', lambda e, hh=hh: e.matmul(bank[7][:, hh * 64:(hh + 1) * 64], lhsT=sTm[:, hh, :],
                                                             rhs=rv_c[p][:, hh * 64:(hh + 1) * 64], start=True, stop=False),
                             reads=[('sTm', hh % 2), ('rv_c', p)], writes=[('bank', 7)], inc=False)
                        S.op('pe', lambda e, hh=hh: e.matmul(bank[7][:, hh * 64:(hh + 1) * 64], lhsT=qdT[p][:, hh, :],
                                                             rhs=Tbf[:, hh * 64:(hh + 1) * 64], start=False, stop=True),
                             reads=[('qdT', p), 'Tbf'], writes=[('bank', 7)], inc=(hh == 7))
                    yield
                    for k in range(8):
                        S.op('pe', lambda e, k=k: e.matmul(
                            bank[5][:], lhsT=XT[:, k, c * 128:(c + 1) * 128], rhs=wgr[:, k, :], start=(k == 0), stop=(k == 7)),
                            reads=[('XT', c), 'wgr'], writes=[('bank', 5)], inc=(k == 7))
                    S.op('act', lambda e: e.activation(out=sg[:], in_=bank[5][:], func=AF.Silu), reads=[('bank', 5)], writes=['sg'])
                    yield
                    yield
                    o3 = bank[7][:, :].rearrange("p (a b) -> p a b", a=8)
                    S.op('dve', lambda e: e.tensor_reduce(out=st_s[:], in_=o3, axis=AX.X, op=ALU.add), reads=[('bank', 7)], writes=['st_s'])
                    yield
                    S.op('act', lambda e: e.activation(out=osq[:], in_=bank[7][:], func=AF.Square), reads=[('bank', 7)], writes=['osq'])
                    yield
                    S.op('dve', lambda e: e.tensor_reduce(out=st_q[:], in_=osq[:, :].rearrange("p (a b) -> p a b", a=8), axis=AX.X, op=ALU.add),
                         reads=['osq'], writes=['st_q'])
                    yield
                    S.op('dve', lambda e: e.tensor_scalar(out=st_m[:], in0=st_s[:], scalar1=1.0 / 64, scalar2=None, op0=ALU.mult),
                         reads=['st_s'], writes=['st_m'])
                    yield
                    S.op('dve', lambda e: e.tensor_tensor(out=st_v[:], in0=st_m[:], in1=st_m[:], op=ALU.mult), reads=['st_m'], writes=['st_v'])
                    yield
                    S.op('dve', lambda e: e.scalar_tensor_tensor(out=st_v[:], in0=st_q[:], scalar=1.0 / 64, in1=st_v[:],
                                                                op0=ALU.mult, op1=ALU.subtract), reads=['st_q', 'st_v'], writes=['st_v'])
                    yield
                    S.op('dve', lambda e: e.tensor_scalar(out=st_v[:], in0=st_v[:], scalar1=EPS, scalar2=None, op0=ALU.add),
                         reads=['st_v'], writes=['st_v'])
                    yield
                    S.op('pool', lambda e: e.tensor_tensor(out=st_r[:], in0=st_v[:], in1=mhalf[:], op=ALU.pow),
                         reads=['st_v', 'mhalf'], writes=['st_r'])
                    yield
                    S.op('dve', lambda e: e.tensor_tensor(out=ocen[:, :].rearrange("p (a b) -> p a b", a=8), in0=o3,
                                                          in1=st_m[:, :].unsqueeze(2).to_broadcast([128, 8, 64]), op=ALU.subtract),
                         reads=[('bank', 7), 'st_m'], writes=['ocen'])
                    yield
                    S.op('dve', lambda e: e.tensor_tensor(out=ocen[:, :].rearrange("p (a b) -> p a b", a=8),
                                                          in0=ocen[:, :].rearrange("p (a b) -> p a b", a=8),
                                                          in1=st_r[:, :].unsqueeze(2).to_broadcast([128, 8, 64]), op=ALU.mult),
                         reads=['ocen', 'st_r'], writes=['ocen'])
                    yield
                    S.op('dve', lambda e: e.tensor_tensor(out=rout[:], in0=ocen[:], in1=sg[:], op=ALU.mult),
                         reads=['ocen', 'sg'], writes=['rout'])
                    yield
                    pr_ = bankbf[6]
                    for j in range(4):
                        S.op('pe', lambda e, j=j: e.transpose(out=pr_[:, j * 128:(j + 1) * 128], in_=rout[:, j * 128:(j + 1) * 128], identity=ident[:]),
                             reads=['rout', 'ident'], writes=[('bank', 6)], inc=(j == 3))
                    S.op('act', lambda e: e.activation(out=XT[:, 4:8, c * 128:(c + 1) * 128],
                                                       in_=pr_[:, 0:512].rearrange("p (a t) -> p a t", a=4), func=AF.Copy),
                         reads=[('bank', 6)], writes=[('XT', c)])
                    yield

                def interleave(gens):
                    gens = [g for g in gens if g is not None]
                    while gens:
                        for g in list(gens):
                            try:
                                next(g)
                            except StopIteration:
                                gens.remove(g)

                retB_proj(0)
                interleave([retB_rest(0)])
                with ExitStack() as es2:
                    Epc = [es2.enter_context(nc.sbuf_tensor("rb_Epc%d" % i, [128, 512], F32)) for i in range(2)]
                    for r_ in range(4):
                        S.dma('sp', Epc[r_ % 2][:], xout2[r_ * 128:(r_ + 1) * 128, :], reads=['xout2'], writes=[('Epc', r_ % 2)])
                        cf = coef[:, r_ * 8:(r_ + 1) * 8].unsqueeze(2).to_broadcast([128, 8, 64])
                        dst = (carry if r_ == 0 else tC)
                        S.op('dve', lambda e, r_=r_, cf=cf, dst=dst: e.tensor_tensor(
                            out=dst[:, :].rearrange("p (a b) -> p a b", a=8), in0=Epc[r_ % 2][:, :].rearrange("p (a b) -> p a b", a=8),
                            in1=cf, op=ALU.mult), reads=[('Epc', r_ % 2), ('coef', r_)], writes=['carry' if r_ == 0 else 'tC'])
                        if r_ > 0:
                            S.op('dve', lambda e: e.tensor_tensor(out=carry[:], in0=carry[:], in1=tC[:], op=ALU.add),
                                 reads=['carry', 'tC'], writes=['carry'])
                for c in range(NTT):
                    if c + 1 < NTT:
                        retB_proj(c + 1)
                    interleave([retB_back(c), retB_rest(c + 1) if c + 1 < NTT else None])
                if stop == 'retb':
                    S.dma('sp', dbgbf[:, 0:8192], XT[:, 4:8, :].rearrange("p a t -> p (a t)"), reads=[('XT', c) for c in range(NTT)], writes=['dbg_a'])
                    out_keys.extend(['dbg_a'])
                S.barrier()
                if stop == 'retb':
                    raise _Stop()

            esT.close()

            with ExitStack() as es:
                sb = lambda n, s, d: es.enter_context(nc.sbuf_tensor("at_" + n, s, d))
                qaugT = sb("qaugT", [128, 8, NT], BF16)
                ckvnT_all = sb("ckvnT_all", [128, 4, NT], BF16)
                for r_ in range(4):
                    S.dma('sp', ckvnT_all[:, r_, :], xout[r_ * 160:r_ * 160 + 128, :], reads=['xout'], writes=[('ckvnT_all', r_)])
                wuq = sb("wuq", [128, 2, 768], BF16)
                wukv = sb("wukv", [128, 1024], BF16)
                qa_tok = [sb("qa_tok%d" % i, [128, 8, 128], BF16) for i in range(2)]
                qs = sb("qs", [128, 768], F32)
                tmps = [sb("rt%d" % i, [128, 256], F32) for i in range(4)]
                kaug = [sb("kaug%d" % i, [128, 512], BF16) for i in range(3)]
                vext = [[sb("vext%d_%d" % (par, i), [128, 4, 128], BF16) for i in range(3)] for par in range(2)]
                PT = [sb("PT%d" % i, [128, 512], BF16) for i in range(4)]
                accs = [sb("accs%d" % i, [128, 512], F32) for i in range(4)]
                lnd = sb("lnd", [1, 512], F32); rden = sb("rden", [1, 512], F32)
                bcs = sb("bcs", [128, 512], F32)
                S.dma('pool', wuq[:], w_uq.rearrange("(kc p) n -> p kc n", p=128), writes=['wuq'])
                S.dma('pool', wukv[:], w_ukv, writes=['wukv'])
                for par in range(2):
                    for i in range(3):
                        S.op('dve', lambda e, par=par, i=i: e.memset(vext[par][i][:], 0.0), writes=[('vext', par, i)])
                        col = 64 if par == 0 else 0
                        S.op('dve', lambda e, par=par, i=i, col=col: e.memset(vext[par][i][:, :, col:col + 1], 1.0),
                             reads=[('vext', par, i)], writes=[('vext', par, i)])
                for i_ in range(2):
                    S.op('dve', lambda e, i_=i_: e.memset(qa_tok[i_][:], 0.0),
                         writes=[('qa_n', i_, 0), ('qa_n', i_, 1), ('qa_p', i_, 0), ('qa_p', i_, 1)])
                if tsub == 39:
                    S.barrier(); raise _Stop()
                for tt in range(NTT):
                    b = tt % 2
                    for (bk, c0, c1) in ((0, 0, 480), (1, 480, 768)):
                        for kc in range(2):
                            S.op('pe', lambda e, kc=kc, tt=tt, bk=bk, c0=c0, c1=c1: e.matmul(
                                bank[bk][:, 0:c1 - c0], lhsT=cqnT[:, kc, tt * 128:(tt + 1) * 128], rhs=wuq[:, kc, c0:c1],
                                start=(kc == 0), stop=(kc == 1)), reads=[('cqnT', tt), 'wuq'], writes=[('bank', bk)], inc=(kc == 1))
                    if tsub == 401 and tt == 0:
                        S.barrier(); raise _Stop()
                    S.op('act', lambda e: e.activation(out=qs[:, 0:480], in_=bank[0][:, 0:480], func=AF.Copy),
                         reads=[('bank', 0)], writes=['qs0'])
                    S.op('act', lambda e: e.activation(out=qs[:, 480:768], in_=bank[1][:, 0:288], func=AF.Copy),
                         reads=[('bank', 1)], writes=['qs1'])
                    qs3 = qs[:, :].rearrange("p (a d) -> p a d", a=8)
                    S.op('act', lambda e, b=b, qs3=qs3: e.activation(out=qa_tok[b][:, :, 0:64], in_=qs3[:, :, 0:64], func=AF.Copy),
                         reads=['qs0', 'qs1'], writes=[('qa_n', b, 0), ('qa_n', b, 1)])
                    x4 = qs3[:, :, 64:96].rearrange("p a (two j) -> p a two j", two=2)
                    o4 = qa_tok[b][:, :, 64:96].rearrange("p a (two j) -> p a two j", two=2)
                    rope_tok(x4, o4, cm3[:, tt, :], sm3[:, tt, :], 8, 16, tmps, (['qs0', 'qs1'], [('qa_p', b, 0), ('qa_p', b, 1)]))
                    if tsub == 402 and tt == 0:
                        S.barrier(); raise _Stop()
                    pb = 6 + b
                    pv = bankbf[pb]
                    for hh in range(8):
                        S.op('pe', lambda e, hh=hh, b=b, pv=pv: e.transpose(
                            out=pv[:, hh * 128:(hh + 1) * 128], in_=qa_tok[b][:, hh, :], identity=ident[:]),
                            reads=[('qa_n', b, 0), ('qa_n', b, 1), ('qa_p', b, 0), ('qa_p', b, 1), 'ident'],
                            writes=[('bank', pb)], inc=(hh == 7))
                    S.op('act', lambda e, tt=tt, pv=pv: e.activation(
                        out=qaugT[:, :, tt * 128:(tt + 1) * 128],
                        in_=pv[:, 0:1024].rearrange("p (a t) -> p a t", a=8), func=AF.Copy),
                        reads=[('bank', pb)], writes=[('qaug', tt)])

                if tsub == 40:
                    S.barrier(); raise _Stop()
                xo = xout.ap()

                def prodK(gp):
                    hh, p = gp // 16, gp % 16
                    r_, t0 = p // 4, (p % 4) * 512
                    sl = gp % 3
                    S.op('pe', lambda e: e.matmul(bank[7][0:64, :], lhsT=wukv[:, hh * 128:hh * 128 + 64],
                                                  rhs=ckvnT_all[:, r_, t0:t0 + 512], start=True, stop=True),
                         reads=['wukv', ('ckvnT_all', r_)], writes=[('bank', 7)])
                    S.op('dve', lambda e: e.tensor_copy(out=kaug[sl][0:64, :], in_=bank[7][0:64, :]),
                         reads=[('bank', 7)], writes=[('kaug_n', sl)])
                    S.dma('sp', kaug[sl][64:96, :], xo[r_ * 160 + 128:r_ * 160 + 160, t0:t0 + 512],
                          reads=['xout'], writes=[('kaug_p', sl)])

                def prodV(gp):
                    hh, p = gp // 16, gp % 16
                    r_, t0 = p // 4, (p % 4) * 512
                    sl = gp % 3
                    par = hh % 2
                    for j in range(4):
                        S.op('pe', lambda e, j=j: e.matmul(bank[7][:, j * 64:(j + 1) * 64],
                                                           lhsT=ckvnT_all[:, r_, t0 + j * 128:t0 + (j + 1) * 128],
                                                           rhs=wukv[:, hh * 128 + 64:hh * 128 + 128], start=True, stop=True),
                             reads=['wukv', ('ckvnT_all', r_)], writes=[('bank', 7)], inc=(j == 3))
                    off = 0 if par == 0 else 64
                    S.op('dve', lambda e: e.tensor_copy(out=vext[par][sl][:, :, off:off + 64],
                                                        in_=bank[7][:, 0:256].rearrange("p (a d) -> p a d", a=4)),
                         reads=[('bank', 7)], writes=[('vext', par, sl)])

                steps = [(gp, kt, qg) for gp in range(8 * 16) for kt in range(4) for qg in range(4)]
                NS = len(steps)

                def emit_S(i):
                    gp, kt, qg = steps[i]
                    hh = gp // 16
                    sl = gp % 3
                    sbk = 4 + (i % 3)
                    S.op('pe', lambda e: e.matmul(bank[sbk][:], lhsT=kaug[sl][0:96, kt * 128:(kt + 1) * 128],
                                                  rhs=qaugT[0:96, hh, qg * 512:(qg + 1) * 512], start=True, stop=True),
                         reads=[('kaug_n', sl), ('kaug_p', sl)] + [('qaug', qg * 4 + j) for j in range(4)], writes=[('bank', sbk)])
                    S.op('act', lambda e: e.activation(out=PT[i % 4][:], in_=bank[sbk][:], func=AF.Exp, scale=SCALE),
                         reads=[('bank', sbk)], writes=[('PT', i % 4)])

                def emit_PV(i):
                    gp, kt, qg = steps[i]
                    hh, p = gp // 16, gp % 16
                    sl = gp % 3
                    par = hh % 2
                    first = (p == 0 and kt == 0)
                    last = (p == 15 and kt == 3)
                    S.op('pe', lambda e: e.matmul(bank[qg][:], lhsT=vext[par][sl][:, kt, :], rhs=PT[i % 4][:],
                                                  start=first, stop=last),
                         reads=[('vext', par, sl), ('PT', i % 4)], writes=[('bank', qg)], inc=last)
                    if last:
                        S.op('dve', lambda e: e.tensor_copy(out=accs[qg][:], in_=bank[qg][:]),
                             reads=[('bank', qg)], writes=[('accs', qg)])

                def finish_head(hh):
                    par = hh % 2
                    drow = 64 if par == 0 else 0
                    o0 = 0 if par == 0 else 64
                    for qg in range(4):
                        S.op('act', lambda e, qg=qg: e.activation(out=lnd[0:1, :], in_=accs[qg][drow:drow + 1, :], func=AF.Ln),
                             reads=[('accs', qg)], writes=['lnd'])
                        S.op('act', lambda e: e.activation(out=rden[0:1, :], in_=lnd[0:1, :], func=AF.Exp, scale=-1.0),
                             reads=['lnd'], writes=['rden'])
                        S.op('pe', lambda e: e.matmul(bank[7][:], lhsT=ones32[0:1, :], rhs=rden[0:1, :], start=True, stop=True),
                             reads=['ones32', 'rden'], writes=[('bank', 7)])
                        S.op('dve', lambda e: e.tensor_copy(out=bcs[:], in_=bank[7][:]), reads=[('bank', 7)], writes=['bcs'])
                        S.op('dve', lambda e, qg=qg: e.tensor_tensor(
                            out=XT[o0:o0 + 64, hh // 2, qg * 512:(qg + 1) * 512], in0=accs[qg][o0:o0 + 64, :],
                            in1=bcs[o0:o0 + 64, :], op=ALU.mult),
                            reads=[('accs', qg), 'bcs'], writes=[('XTa', hh // 2, qg)])

                prodK(0)
                prodV(0)
                emit_S(0)
                emit_S(1)
                for i in range(NS):
                    gp, kt, qg = steps[i]
                    if qg == 0 and gp + 1 < 128:
                        if kt == 0:
                            prodK(gp + 1)
                        elif kt == 2:
                            prodV(gp + 1)
                    if i + 2 < NS:
                        emit_S(i + 2)
                    emit_PV(i)
                    if gp % 16 == 15 and kt == 3 and qg == 3:
                        finish_head(gp // 16)
                if stop == 'attn':
                    S.dma('sp', dbgbf[:, 0:8192], XT[:, 0:4, :].rearrange("p a t -> p (a t)"),
                          reads=[('XTa', a, q_) for a in range(4) for q_ in range(4)], writes=['dbg_a'])
                    out_keys.extend(['dbg_a'])
                S.barrier()
                if stop == 'attn':
                    raise _Stop()

            with ExitStack() as es:
                sb = lambda n, s, d: es.enter_context(nc.sbuf_tensor("wo_" + n, s, d))
                wo = sb("wo", [128, 8, D], BF16)
                S.dma('pool', wo[:], w_o.rearrange("(kc p) n -> p kc n", p=128), writes=['wo'])
                st = 0
                for tt in range(NTT):
                    for half in range(2):
                        bd = st % 2
                        for k in range(8):
                            lhs = XT[:, k, tt * 128:(tt + 1) * 128]
                            rd = [('XTa', k, tt // 4)] if k < 4 else [('XT', tt)]
                            S.op('pe', lambda e, k=k, lhs=lhs, half=half, bd=bd: e.matmul(
                                bank[bd][:], lhsT=lhs, rhs=wo[:, k, half * 512:(half + 1) * 512], start=(k == 0), stop=(k == 7)),
                                reads=rd + ['wo'], writes=[('bank', bd)], inc=(k == 7))
                        S.op('dve', lambda e, tt=tt, half=half, bd=bd: e.tensor_tensor(
                            out=h[:, tt, half * 512:(half + 1) * 512], in0=bank[bd][:], in1=h[:, tt, half * 512:(half + 1) * 512],
                            op=ALU.add), reads=[('bank', bd), ('h', tt)], writes=[('h', tt)])
                        st += 1
                S.barrier()

    rope_tables()
    load_x()
    if skip_ffn and _os.environ.get("PRE_ACT"):
        fn = getattr(AF, _os.environ["PRE_ACT"])
        S.op('act', lambda e: e.activation(out=ss[:], in_=ss[:], func=fn), writes=['pre'])
        S.barrier()
    if not skip_ffn:
        ffn(g_ffn1, wg1, wu1, wd1, "f1_")
    if stop == 'ffn1':
        store_h_raw()
        S.wait_all('sp', out_keys)
    else:
        if not skip_mid:
            try:
                middle()
            except _Stop:
                pass
        if stop is not None:
            store_h_raw()
            S.wait_all('sp', out_keys)
        else:
            ffn(g_ffn2, wg2, wu2, wd2, "f2_")
            final_norm_store()
    print("instr", S.ninstr, "waits", S.nwaits, "cnt", S.cnt)
    return nc


_NC_CACHE = {}


def _const_tables():
    ct = np.zeros((128, NCT), np.float32)
    j = np.arange(128, dtype=np.float64)[:, None]
    i = np.arange(128, dtype=np.float64)[None, :]
    ct[:, 0:128] = np.maximum(i - j, 0.0)
    ct[:, 128:256] = np.maximum(j - i, 0.0)
    ct[0:64, 256:384] = i + 1.0
    ct[64:128, 256:384] = 128.0 - i
    ct[:, 384:392] = 127.0 - j
    ct[:, 392:400] = j
    c = np.arange(16, dtype=np.float64)[None, :]
    ct[0:64, 400:416] = 128.0 * c
    ct[64:128, 400:416] = 128.0 * (15.0 - c)
    ct[:, 416:448] = (np.float32(10000.0) ** (-np.arange(32, dtype=np.float32) / np.float32(32)))[None, :]
    ct[:, 448:464] = (np.float32(10000.0) ** (-np.arange(16, dtype=np.float32) / np.float32(16)))[None, :]
    return ct


def _geo(r):
    g = np.zeros((128, 8), np.float32)
    for rp in range(4):
        if rp < r:
            g[0:64, rp] = r - 1 - rp
            g[0:64, 4 + rp] = 1.0
        if rp > r:
            g[64:128, rp] = rp - r - 1
            g[64:128, 4 + rp] = 1.0
    return g


def _core_inputs(inputs):
    f32 = lambda a: np.ascontiguousarray(np.asarray(a, dtype=np.float32))
    x = f32(inputs["x"])
    pos = np.asarray(inputs["positions"]).astype(np.int32)
    common = {
        "g_mix": f32(inputs["mix_norm"]).reshape(1, D), "w_in": f32(inputs["w_in"])[0],
        "g_q": f32(inputs["q_norm"]).reshape(1, 256), "w_uq": f32(inputs["w_uq"])[0],
        "g_kv": f32(inputs["kv_norm"]).reshape(1, 128), "w_ukv": f32(inputs["w_ukv"])[0],
        "dec": np.ascontiguousarray(np.concatenate([f32(inputs["ret_decay_fwd"]).reshape(1, 8),
                                                    f32(inputs["ret_decay_bwd"]).reshape(1, 8)], axis=1)),
        "w_o": f32(inputs["w_o"])[0],
        "ctab": _const_tables(),
        "g_ffn1": f32(inputs["ffn1_norm"]).reshape(1, D),
        "wg1": f32(inputs["ffn1_w_gate"])[0], "wu1": f32(inputs["ffn1_w_up"])[0], "wd1": f32(inputs["ffn1_w_down"])[0],
        "g_ffn2": f32(inputs["ffn2_norm"]).reshape(1, D),
        "wg2": f32(inputs["ffn2_w_gate"])[0], "wu2": f32(inputs["ffn2_w_up"])[0], "wd2": f32(inputs["ffn2_w_down"])[0],
        "g_fin": f32(inputs["final_norm"]).reshape(1, D),
    }
    in_maps = []
    for c in range(NCORES):
        b, r = c // 4, c % 4
        m = dict(common)
        m["x"] = np.ascontiguousarray(x[b, r * NT:(r + 1) * NT, :])
        m["pos_tok"] = np.ascontiguousarray(pos[b, r * NT:(r + 1) * NT].reshape(NTT, 128).T)
        m["geo"] = _geo(r)
        in_maps.append(m)
    return in_maps


def run(inputs, stop=None, skip_mid=False, trace=False, skip_ffn=False, tsub=99):
    key = (stop, skip_mid, skip_ffn, tsub)
    if key not in _NC_CACHE:
        _NC_CACHE[key] = build(stop=stop, skip_mid=skip_mid, skip_ffn=skip_ffn, tsub=tsub)
    nc = _NC_CACHE[key]
    in_maps = _core_inputs(inputs)
    res = run_bass_kernel_spmd(nc, in_maps, core_ids=list(range(NCORES)), **({"trace": True} if trace else {}))
    out = np.empty((2, 8192, D), np.float32)
    for c in range(NCORES):
        b, r = c // 4, c % 4
        out[b, r * NT:(r + 1) * NT, :] = res.results[c]["out"]
    return out, res


def kernel(**inputs):
    out, _ = run(inputs)
    return out
```
